# Optimizing a Trainium2 kernel written in Bass

```python
import math
import jax, jax.numpy as jnp
from jax import lax
import numpy as np

D_MODEL = 2048
BATCH = 1
SEQ = 8192
DEPTH = 1
DEC_BATCH = 2
DEC_SEQ = 4096
PAST_LEN = 128

GRID_W = 64
WIN_H = 8
WIN_W = 16
QBLK_W = 16
KBLK_W = 32
N_QBLK = GRID_W // QBLK_W
HEAD_DIM = 64
NA_HEADS = 16
NA_WIDTH = NA_HEADS * HEAD_DIM
CONV_GROUPS = 8
CONV_WIDTH = CONV_GROUPS * HEAD_DIM
CONV_K = 3
MEM_HEADS = 4
MEM_HEAD_DIM = 128
MEM_WIDTH = MEM_HEADS * MEM_HEAD_DIM
N_MEM = 256
MIX_WIDTH = NA_WIDTH + CONV_WIDTH + MEM_WIDTH
IN_WIDTH = 3 * NA_WIDTH + 3 * CONV_WIDTH + MEM_WIDTH
D_FF = 5632
EPS = 1e-6
NEG_INF = -1e30

kernel_name = "hymba_natten_shortconv_memxattn_encoder"


def rmsnorm(x, g):
    x32 = x.astype(jnp.float32)
    y = x32 * lax.rsqrt(jnp.mean(x32 * x32, axis=-1, keepdims=True) + EPS) * g.astype(jnp.float32)
    return y.astype(x.dtype)


def dwconv3(u, w):
    up = jnp.pad(u, ((0, 0), (1, 1), (0, 0)))
    return up[:, :-2] * w[0] + up[:, 1:-1] * w[1] + up[:, 2:] * w[2]


def neighbourhood_attention(q, k, v, rel_bias):
    b, t, h, dh = q.shape
    rows = t // GRID_W
    wh = min(WIN_H, rows)
    r = jnp.arange(rows)
    row_idx = jnp.clip(r - wh // 2, 0, rows - wh)[:, None] + jnp.arange(wh)[None, :]
    qc = (jnp.arange(N_QBLK) * QBLK_W)[:, None] + jnp.arange(QBLK_W)[None, :]
    q_cs = jnp.clip(qc - WIN_W // 2, 0, GRID_W - WIN_W)
    k_cs = jnp.clip(jnp.arange(N_QBLK) * QBLK_W - WIN_W // 2, 0, GRID_W - KBLK_W)
    col_idx = k_cs[:, None] + jnp.arange(KBLK_W)[None, :]
    n_keys = wh * KBLK_W
    ri = row_idx[:, None, :, None]
    ci = col_idx[None, :, None, :]
    kg = k.reshape(b, rows, GRID_W, h, dh)[:, ri, ci].reshape(b, rows, N_QBLK, n_keys, h, dh)
    vg = v.reshape(b, rows, GRID_W, h, dh)[:, ri, ci].reshape(b, rows, N_QBLK, n_keys, h, dh)
    qg = q.reshape(b, rows, N_QBLK, QBLK_W, h, dh)
    kcol = col_idx[:, None, :]
    valid = (kcol >= q_cs[:, :, None]) & (kcol < q_cs[:, :, None] + WIN_W)
    dr = row_idx - r[:, None] + (WIN_H - 1)
    dc = jnp.clip(kcol - qc[:, :, None], -(WIN_W - 1), WIN_W - 1) + (WIN_W - 1)
    bias = rel_bias.astype(jnp.float32)[:, dr[:, None, None, :, None], dc[None, :, :, None, :]]
    bias = jnp.where(valid[None, None, :, :, None, :], bias, NEG_INF)
    bias = bias.reshape(h, rows, N_QBLK, QBLK_W, n_keys)
    s = jnp.einsum('brjqhd,brjkhd->bhrjqk', qg, kg, preferred_element_type=jnp.float32)
    s = s * (1.0 / math.sqrt(dh)) + bias[None]
    p = jax.nn.softmax(s, axis=-1).astype(v.dtype)
    o = jnp.einsum('bhrjqk,brjkhd->brjqhd', p, vg)
    return o.reshape(b, t, h * dh)


def memory_attention(q, mk, mv):
    b, t, h, dh = q.shape
    s = jnp.einsum('bthd,bmhd->bhtm', q, mk, preferred_element_type=jnp.float32) * (1.0 / math.sqrt(dh))
    p = jax.nn.softmax(s, axis=-1).astype(mv.dtype)
    return jnp.einsum('bhtm,bmhd->bthd', p, mv).reshape(b, t, h * dh)


def encoder_layer(x, mem, g_mix, w_in, na_q_gain, na_k_gain, na_rel_bias, conv_w,
                  mem_norm_g, w_mem_kv, mem_q_gain, mem_k_gain, w_out,
                  g_ffn, w_ffn_in, ffn_conv_w, w_ffn_out):
    b, t, _ = x.shape
    n = rmsnorm(x, g_mix)
    z = n @ w_in
    cuts = [NA_WIDTH, 2 * NA_WIDTH, 3 * NA_WIDTH,
            3 * NA_WIDTH + CONV_WIDTH, 3 * NA_WIDTH + 2 * CONV_WIDTH, 3 * NA_WIDTH + 3 * CONV_WIDTH]
    q_na, k_na, v_na, h_c, b_c, c_c, q_m = jnp.split(z, cuts, axis=-1)
    q_na = rmsnorm(q_na.reshape(b, t, NA_HEADS, HEAD_DIM), na_q_gain)
    k_na = rmsnorm(k_na.reshape(b, t, NA_HEADS, HEAD_DIM), na_k_gain)
    v_na = v_na.reshape(b, t, NA_HEADS, HEAD_DIM)
    y_na = neighbourhood_attention(q_na, k_na, v_na, na_rel_bias)
    y_c = b_c * dwconv3(c_c * h_c, conv_w)
    mkv = rmsnorm(mem, mem_norm_g) @ w_mem_kv
    m = mem.shape[1]
    mk, mv = jnp.split(mkv, 2, axis=-1)
    mk = rmsnorm(mk.reshape(b, m, MEM_HEADS, MEM_HEAD_DIM), mem_k_gain)
    mv = mv.reshape(b, m, MEM_HEADS, MEM_HEAD_DIM)
    q_m = rmsnorm(q_m.reshape(b, t, MEM_HEADS, MEM_HEAD_DIM), mem_q_gain)
    y_m = memory_attention(q_m, mk, mv)
    x = x + jnp.concatenate([y_na, y_c, y_m], axis=-1) @ w_out
    u = dwconv3(rmsnorm(x, g_ffn) @ w_ffn_in, ffn_conv_w)
    a, g = jnp.split(u, 2, axis=-1)
    return x + (jax.nn.silu(a) * g) @ w_ffn_out


def setup_inputs(seed: int = 0) -> dict:
    key = jax.random.key(seed)
    ks = jax.random.split(key, 20)
    f32 = jnp.float32

    def nrm(k, shape, scale):
        return jax.random.normal(k, shape, f32) * scale

    def gain(k, shape):
        return 1.0 + 0.02 * jax.random.normal(k, shape, f32)

    return {
        "x_prompt": nrm(ks[0], (BATCH, SEQ, D_MODEL), 1.0),
        "x_sample": nrm(ks[1], (DEC_BATCH, DEC_SEQ, D_MODEL), 1.0),
        "mem_prompt": nrm(ks[2], (BATCH, N_MEM, D_MODEL), 1.0),
        "mem_sample": nrm(ks[3], (DEC_BATCH, N_MEM, D_MODEL), 1.0),
        "g_mix": gain(ks[4], (DEPTH, D_MODEL)),
        "w_in": nrm(ks[5], (DEPTH, D_MODEL, IN_WIDTH), D_MODEL ** -0.5),
        "na_q_gain": gain(ks[6], (DEPTH, HEAD_DIM)),
        "na_k_gain": gain(ks[7], (DEPTH, HEAD_DIM)),
        "na_rel_bias": nrm(ks[8], (DEPTH, NA_HEADS, 2 * WIN_H - 1, 2 * WIN_W - 1), 0.1),
        "conv_w": nrm(ks[9], (DEPTH, CONV_K, CONV_WIDTH), CONV_K ** -0.5),
        "mem_norm_g": gain(ks[10], (DEPTH, D_MODEL)),
        "w_mem_kv": nrm(ks[11], (DEPTH, D_MODEL, 2 * MEM_WIDTH), D_MODEL ** -0.5),
        "mem_q_gain": gain(ks[12], (DEPTH, MEM_HEAD_DIM)),
        "mem_k_gain": gain(ks[13], (DEPTH, MEM_HEAD_DIM)),
        "w_out": nrm(ks[14], (DEPTH, MIX_WIDTH, D_MODEL), MIX_WIDTH ** -0.5),
        "g_ffn": gain(ks[15], (DEPTH, D_MODEL)),
        "w_ffn_in": nrm(ks[16], (DEPTH, D_MODEL, 2 * D_FF), D_MODEL ** -0.5),
        "ffn_conv_w": nrm(ks[17], (DEPTH, CONV_K, 2 * D_FF), CONV_K ** -0.5),
        "w_ffn_out": nrm(ks[18], (DEPTH, D_FF, D_MODEL), D_FF ** -0.5),
    }


def reference(x_prompt, x_sample, mem_prompt, mem_sample, g_mix, w_in, na_q_gain, na_k_gain,
              na_rel_bias, conv_w, mem_norm_g, w_mem_kv, mem_q_gain, mem_k_gain, w_out,
              g_ffn, w_ffn_in, ffn_conv_w, w_ffn_out):
    y_prompt = x_prompt
    y_sample = x_sample
    for l in range(DEPTH):
        p = (g_mix[l], w_in[l], na_q_gain[l], na_k_gain[l], na_rel_bias[l], conv_w[l],
             mem_norm_g[l], w_mem_kv[l], mem_q_gain[l], mem_k_gain[l], w_out[l],
             g_ffn[l], w_ffn_in[l], ffn_conv_w[l], w_ffn_out[l])
        y_prompt = encoder_layer(y_prompt, mem_prompt, *p)
        y_sample = encoder_layer(y_sample, mem_sample, *p)
    return (y_prompt, y_sample)
```

```python
import contextlib
import math
import os

import numpy as np
import concourse.bass as bass
import concourse.mybir as mybir
from concourse.bass_utils import run_bass_kernel_spmd

F32 = mybir.dt.float32
BF16 = mybir.dt.bfloat16
AF = mybir.ActivationFunctionType
ALU = mybir.AluOpType

NEG = -1.0e30
EPS = 1e-6
D = 2048
KC = 16
NCORE = 8
TOK = 2048
NST = 2
STT = 1024
XW = 2688
WIDE = 1664
MID = 1152
MIX = 1026
NPAIR = 13
D_FF = 5632
NCV = 358
C_GMIX, C_GFFN, C_GMEM, C_GQ, C_GK, C_GMQ, C_GMK, C_CONV, C_FCONV, C_KMASK, C_FLAG = 0, 16, 32, 48, 49, 50, 51, 52, 64, 328, 354

DEBUG = bool(int(os.environ.get("MK_DEBUG", "0")))
SAME_SYNC = {"pe": False, "act": True, "dve": True, "pool": True, "sp": False}
SAME_ALL = bool(int(os.environ.get("MK_SAME_ALL", "1")))


class Sched:
    ENG = ("pe", "act", "dve", "pool", "sp")

    def __init__(self, nc):
        self.nc = nc
        self.ops = []
        self.last_w = {}
        self.readers = {}
        self.slot_cnt = {}
        self.last_on = {}
        self.pending_bar = {}

    def op(self, eng, fn, reads=(), writes=(), dma_slot=None, extra_deps=(), small=True):
        idx = len(self.ops)
        deps = set(extra_deps)
        if eng in self.pending_bar:
            deps.update(self.pending_bar.pop(eng))
        for k in reads:
            w = self.last_w.get(k)
            if w is not None:
                deps.add(w)
        for k in writes:
            w = self.last_w.get(k)
            if w is not None:
                deps.add(w)
            deps.update(self.readers.get(k, {}).values())
        for k in reads:
            rk = self.readers.setdefault(k, {})
            rk[eng if dma_slot is None else (eng, dma_slot)] = idx
        for k in writes:
            self.last_w[k] = idx
            self.readers[k] = {}
        deps.discard(idx)
        o = dict(eng=eng, fn=fn, deps=deps, idx=idx, dma=dma_slot, sig=False, small=small)
        if dma_slot is not None:
            self.slot_cnt[dma_slot] = self.slot_cnt.get(dma_slot, 0) + 16
            o["dma_val"] = self.slot_cnt[dma_slot]
        self.ops.append(o)
        self.last_on[eng] = idx
        return idx

    def barrier(self, dummies):
        marks = []
        for e in ("pe", "act", "dve", "pool"):
            if e in self.last_on:
                marks.append(self.last_on[e])
        dmas = []
        seen = set()
        for o in reversed(self.ops):
            if o["dma"] is not None and o["dma"] not in seen:
                seen.add(o["dma"])
                dmas.append(o["idx"])
        deps = marks + dmas
        d_act, d_dve, d_pool = dummies
        self.op("act", lambda e: e.activation(out=d_act, in_=d_act, func=AF.Copy), reads=["dums"], extra_deps=deps)
        self.op("dve", lambda e: e.memset(d_dve, 0.0), extra_deps=deps)
        self.op("pool", lambda e: e.memset(d_pool, 0.0), extra_deps=deps)
        self.pending_bar = {"pe": list(deps), "sp": list(deps)}

    def emit(self, final_wait_eng="sp"):
        nc = self.nc
        ops = self.ops
        for o in ops:
            for d in o["deps"]:
                do = ops[d]
                if do["dma"] is None:
                    if do["eng"] != o["eng"] or (SAME_SYNC[o["eng"]] and (SAME_ALL or do["small"])) or o["dma"] is not None:
                        do["sig"] = True
        cnt = {e: 0 for e in self.ENG}
        for o in ops:
            if o["dma"] is None and o["sig"]:
                cnt[o["eng"]] += 1
                o["sigval"] = cnt[o["eng"]]
        slots = sorted(self.slot_cnt)
        waited = {e: {} for e in self.ENG}
        nwait = 0
        for o in ops:
            need = {}
            for d in o["deps"]:
                do = ops[d]
                if do["dma"] is not None:
                    key = ("d", do["dma"])
                    val = do["dma_val"]
                else:
                    if do["eng"] == o["eng"] and o["dma"] is None and not (SAME_SYNC[o["eng"]] and (SAME_ALL or do["small"])):
                        continue
                    key = ("e", do["eng"])
                    val = do["sigval"]
                need[key] = max(need.get(key, 0), val)
            w = []
            for key, val in need.items():
                if waited[o["eng"]].get(key, 0) >= val:
                    continue
                waited[o["eng"]][key] = val
                w.append((key, val))
            o["waits"] = w
            nwait += len(w)
        self.stats = dict(nops=len(ops), nwait=nwait, nsig=dict(cnt), nslots=len(slots))
        with contextlib.ExitStack() as st:
            esem = {e: st.enter_context(nc.semaphore("s_" + e)) for e in self.ENG}
            ssem = {s: st.enter_context(nc.semaphore("d_" + str(s))) for s in slots}
            block = st.enter_context(nc.Block())

            def run_engine(ename):
                def body(eng):
                    for o in ops:
                        if o["eng"] != ename:
                            continue
                        for key, val in o["waits"]:
                            sem = ssem[key[1]] if key[0] == "d" else esem[key[1]]
                            eng.wait_ge(sem, val)
                        ins = o["fn"](eng)
                        if o["dma"] is not None:
                            ins.then_inc(ssem[o["dma"]], 16)
                        elif o["sig"]:
                            ins.then_inc(esem[ename], 1)
                    if ename == final_wait_eng:
                        for s in slots:
                            eng.wait_ge(ssem[s], self.slot_cnt[s])
                return body

            block.tensor(run_engine("pe"))
            block.scalar(run_engine("act"))
            block.vector(run_engine("dve"))
            block.gpsimd(run_engine("pool"))
            block.sync(run_engine("sp"))


def split_rows(lo, hi, m):
    out = []
    while lo < hi:
        n = min(m, hi - lo)
        out.append((lo, lo + n))
        lo += n
    return out


def build_program():
    nc = bass.Bass("TRN2", target_bir_lowering=False)
    xT_t = nc.dram_tensor("xT", [KC, 128, XW], F32, kind="ExternalInput")
    memT_t = nc.dram_tensor("memT", [KC, 128, 256], F32, kind="ExternalInput")
    w_in_t = nc.dram_tensor("w_in", [D, 5120], F32, kind="ExternalInput")
    w_out_t = nc.dram_tensor("w_out", [D, D], F32, kind="ExternalInput")
    w_fi_t = nc.dram_tensor("w_ffn_in", [D, 2 * D_FF], F32, kind="ExternalInput")
    w_fo_t = nc.dram_tensor("w_ffn_out", [D_FF, D], F32, kind="ExternalInput")
    w_mkv_t = nc.dram_tensor("w_mem_kv", [D, 1024], F32, kind="ExternalInput")
    cvec_t = nc.dram_tensor("cvec", [128, NCV], F32, kind="ExternalInput")
    cmask_t = nc.dram_tensor("colmask", [128, 64], F32, kind="ExternalInput")
    g2_t = nc.dram_tensor("g2", [16 * 15 * 64, 128], F32, kind="ExternalInput")
    yT_t = nc.dram_tensor("yT", [KC, 128, TOK], F32, kind="ExternalOutput")
    if DEBUG:
        dbg_mix_t = nc.dram_tensor("dbg_mix", [NST, KC, 128, MID], F32, kind="ExternalOutput")
        dbg_x1_t = nc.dram_tensor("dbg_x1", [NST, KC, 128, MIX], F32, kind="ExternalOutput")

    ARENA_BYTES = 211968
    arena = nc.alloc_sbuf_tensor("arena", [128, ARENA_BYTES // 4], F32)

    def view(off, shape, dt):
        n = int(np.prod(shape))
        sz = 4 if dt == F32 else 2
        assert off % 4 == 0 and (n * sz) % 4 == 0
        assert off + n * sz <= ARENA_BYTES, (off, n * sz)
        ap = arena[:, off // 4: off // 4 + (n * sz) // 4]
        if dt != F32:
            ap = ap.bitcast(dt)
        if len(shape) == 2:
            ap = ap.rearrange("p (a b) -> p a b", a=shape[0])
        elif len(shape) == 3:
            ap = ap.rearrange("p (a b c) -> p a b c", a=shape[0], b=shape[1])
        return ap

    class Alloc:
        def __init__(self, base, limit):
            self.o = base
            self.limit = limit

        def get(self, shape, dt):
            n = int(np.prod(shape)) * (4 if dt == F32 else 2)
            n = (n + 31) // 32 * 32
            v = view(self.o, shape, dt)
            self.o += n
            assert self.o <= self.limit, (self.o, self.limit)
            return v

    ca = Alloc(0, 8192)
    cvec = ca.get([NCV], F32)
    colmask = ca.get([64], F32)
    ones32 = ca.get([128], F32)
    ones16 = ca.get([128], BF16)
    onesbd = ca.get([128], BF16)
    onesz = [ca.get([128], BF16) for _ in range(2)]
    mkT = ca.get([4, 256], BF16)
    mvtok = ca.get([2, 512], BF16)
    gq8 = ca.get([1], F32)
    gmq = ca.get([1], F32)
    dums = ca.get([3, 1], F32)
    W0 = 8192
    wslot = [view(W0 + i * 16384, [16, 512], BF16) for i in range(2)]
    wslot_fo = [view(W0 + i * 16384, [8, 512], BF16) for i in range(2)]
    R1 = W0 + 2 * 16384
    nT = view(R1, [KC, WIDE], BF16)
    x1T = view(R1, [KC, MIX], F32)
    R2 = R1 + 65664
    yTb = view(R2, [KC, MID], BF16)
    n2T = view(R2, [KC, MIX], BF16)
    R3 = R2 + 36864
    R3_END = ARENA_BYTES
    a3 = Alloc(R3, R3_END)
    xin = [a3.get([KC, 416], F32) for _ in range(2)]
    b3 = Alloc(R3, R3_END)
    qT4 = b3.get([4, MID], BF16)
    kT4 = b3.get([4, WIDE], BF16)
    Vtok = b3.get([NPAIR, 768], BF16)
    BT = [b3.get([21, 64], F32) for _ in range(2)]
    BT += [view(R1 + 53248 + q * 5376, [21, 64], F32) for q in range(2)]
    qz = [[view(R2 + 8 * MID * 2 + (2 * db + e) * MID * 2, [MID], BF16) for e in range(2)] for db in range(2)]
    PT = [b3.get([576], BF16) for _ in range(3)]
    PT.append(view(R1 + 53248 + 2 * 5376, [576], BF16))
    Sb = [view(R2 + 12 * MID * 2 + q * 2304, [576], F32) for q in range(4)]
    SCR = R3_END - 11264
    assert b3.o <= SCR, b3.o
    sa_ = Alloc(SCR, R3_END)
    PT.append(sa_.get([576], BF16))
    sqx3 = sa_.get([448], F32)
    sa_.get([160], F32)
    rstd = [sa_.get([448], F32) for _ in range(2)]
    so_ = sa_.o
    sq32x = [sa_.get([448], F32) for _ in range(2)] + [sqx3]
    sb_ = Alloc(so_, R3_END)
    sq16b = [sb_.get([448], BF16) for _ in range(2)]
    recD = [sb_.get([256], F32) for _ in range(2)]
    c3 = Alloc(R3, SCR)
    ch32 = c3.get([4, MID], F32)
    cc32 = c3.get([4, MID], F32)
    qmT = [c3.get([384], BF16) for _ in range(2)]
    PTm = [c3.get([2, 384], BF16) for _ in range(2)]
    recM = [c3.get([384], F32) for _ in range(2)]
    sq16 = [c3.get([448], BF16) for _ in range(2)]
    assert c3.o <= SCR, c3.o
    d3 = Alloc(R3, R3_END)
    hT = [d3.get([8, STT], BF16) for _ in range(2)]
    sa32 = d3.get([4, STT], F32)
    ctmp = [d3.get([344], F32) for _ in range(4)]
    sq32 = [d3.get([344], F32) for _ in range(4)]
    rstdD = d3.get([MIX], F32)
    sdD = d3.get([344], F32)

    PS = [nc.alloc_psum_tensor(f"ps{i}", [128, 512], F32) for i in range(8)]

    s = Sched(nc)
    dummies = (dums[0:1, 0, :], dums[0:1, 1, :], dums[0:1, 2, :])

    def barrier():
        s.barrier(dummies)

    wq = []
    wstate = {"issued": 0}

    def wsrc(t, ncols_total, col0, nk, ncols, row0=0):
        return bass.AP(t, row0 * ncols_total + col0, [[ncols_total, 128], [128 * ncols_total, nk], [1, ncols]])

    def w_issue_upto(n):
        while wstate["issued"] < min(n, len(wq)):
            i = wstate["issued"]
            src, kind = wq[i]
            sl = i % 2
            dst = wslot[sl][:] if kind == "k16" else wslot_fo[sl][:, 0:kind, :]
            s.op("pool", lambda e, dst=dst, src=src: e.dma_start(out=dst, in_=src),
                 writes=[f"w{sl}"], dma_slot=f"w{sl}")
            wstate["issued"] += 1

    wuse = {"i": 0}

    def w_next():
        i = wuse["i"]
        wuse["i"] += 1
        w_issue_upto(i + 2)
        sl = i % 2
        return sl

    wq.append((wsrc(w_mkv_t, 1024, 0, 16, 512), "k16"))
    wq.append((wsrc(w_mkv_t, 1024, 512, 16, 512), "k16"))
    FF_GROUPS = [(0, 1), (2, 3), (4, 5), (6, 7), (8, 9), (10,)]
    for st_ in range(NST):
        for b in (0, 2, 4, 1, 3, 5, 6, 8, 7, 9):
            wq.append((wsrc(w_in_t, 5120, b * 512, 16, 512), "k16"))
        for b in range(4):
            wq.append((wsrc(w_out_t, D, b * 512, 16, 512), "k16"))
        for grp in FF_GROUPS:
            for b in grp:
                wq.append((wsrc(w_fi_t, 2 * D_FF, b * 512, 16, 512), "k16"))
                wq.append((wsrc(w_fi_t, 2 * D_FF, D_FF + b * 512, 16, 512), "k16"))
            for mb in range(4):
                wq.append((wsrc(w_fo_t, D, mb * 512, 4 * len(grp), 512, row0=grp[0] * 512), 4 * len(grp)))

    psrot = {"i": 0}

    def ps_next(banks):
        b = banks[psrot["i"] % len(banks)]
        psrot["i"] += 1
        return b

    def MM(out, lhsT, rhs, start, stop, reads, writes):
        s.op("pe", lambda e: e.matmul(out, lhsT, rhs, start=start, stop=stop, skip_group_check=True),
             reads=reads, writes=writes)

    def ACT(out, in_, func, reads, writes, bias=None, scale=None):
        kw = {}
        if bias is not None:
            kw["bias"] = bias
        if scale is not None:
            kw["scale"] = scale
        s.op("act", lambda e: e.activation(out=out, in_=in_, func=func, **kw), reads=reads, writes=writes, small=out.free_size() < 128)

    def DVE_TT(out, in0, in1, op, reads, writes, eng="dve"):
        s.op(eng, lambda e: e.tensor_tensor(out=out, in0=in0, in1=in1, op=op), reads=reads, writes=writes, small=out.free_size() < 128)

    def DVE_STT(out, in0, scalar, in1, op0, op1, reads, writes, eng="dve"):
        s.op(eng, lambda e: e.scalar_tensor_tensor(out=out, in0=in0, scalar=scalar, in1=in1, op0=op0, op1=op1),
             reads=reads, writes=writes, small=out.free_size() < 128)

    def DVE_TS(out, in0, scalar1, op0, reads, writes, eng="dve"):
        s.op(eng, lambda e: e.tensor_scalar(out=out, in0=in0, scalar1=scalar1, scalar2=None, op0=op0),
             reads=reads, writes=writes, small=out.free_size() < 128)

    def DVE_RECIP(out, in_, reads, writes):
        ACT(out, in_, AF.Ln, reads, writes)
        ACT(out, out, AF.Exp, writes, writes, scale=-1.0)

    def RSTD(out, in_psum, inv_d, reads, writes):
        ACT(out, in_psum, AF.Ln, reads, writes, bias=EPS, scale=inv_d)
        ACT(out, out, AF.Exp, writes, writes, scale=-0.5)

    def MEMSET(eng, ap, val, writes):
        s.op(eng, lambda e: e.memset(ap, val), writes=writes, small=ap.free_size() < 128)

    def DMA(eng, out, in_, reads, writes, slot):
        s.op(eng, lambda e: e.dma_start(out=out, in_=in_), reads=reads, writes=writes, dma_slot=slot)

    def cv(col, n=1):
        return cvec[:, col:col + n]

    DMA("sp", cvec, cvec_t.ap(), [], ["cvec"], "c0")
    DMA("sp", colmask, cmask_t.ap(), [], ["colmask"], "c1")
    MEMSET("dve", dums[:, :, :], 0.0, ["dums"])
    MEMSET("dve", ones32, 1.0, ["ones32"])
    MEMSET("dve", ones16, 1.0, ["ones16"])
    MEMSET("dve", onesbd, 0.0, ["onesbd"])
    MEMSET("dve", onesbd[0:64, 0:64], 1.0, ["onesbd"])
    MEMSET("dve", onesbd[64:128, 64:128], 1.0, ["onesbd"])
    for e_ in range(2):
        MEMSET("dve", onesz[e_], 0.0, [f"onesz{e_}"])
        MEMSET("dve", onesz[e_][:, 64 * e_:64 * e_ + 64], 1.0, [f"onesz{e_}"])
    DVE_TS(gq8, cv(C_GQ), 0.125, ALU.mult, ["cvec"], ["gq8"])
    DVE_TS(gmq, cv(C_GMQ), 1.0 / math.sqrt(128.0), ALU.mult, ["cvec"], ["gmq"])
    w_issue_upto(2)

    def rms_norm_tiles(n_list, gcol, dst_fn, tagdst, xin_loader):
        for ti, (off, w) in enumerate(n_list):
            buf = ti % 2
            xin_loader(buf, off, w)
            pb = ps_next([0, 1])
            for c in range(KC):
                sq = sq32x[c % 3]
                ACT(sq[:, 0:w], xin[buf][:, c, 0:w], AF.Square, [f"xin{buf}"], [f"sqx{c % 3}"])
                MM(PS[pb][:, 0:w], ones32, sq[:, 0:w], c == 0, c == KC - 1, ["ones32", f"sqx{c % 3}"], [f"ps{pb}"])
            RSTD(rstd[buf][:, 0:w], PS[pb][:, 0:w], 1.0 / D, [f"ps{pb}"], [f"rstd{buf}"])
            for c in range(KC):
                DVE_STT(dst_fn(c, off, w), xin[buf][:, c, 0:w], cv(gcol + c), rstd[buf][:, 0:w], ALU.mult, ALU.mult,
                        [f"xin{buf}", f"rstd{buf}", "cvec"], [tagdst])

    def load_mem(buf, off, w):
        DMA("sp", xin[buf][:, :, 0:w], bass.AP(memT_t, off, [[256, 128], [128 * 256, KC], [1, w]]),
            [], [f"xin{buf}"], f"xin{buf}")

    memn = nT
    rms_norm_tiles([(0, 256)], C_GMEM, lambda c, off, w: memn[:, c, off:off + w], "nT", load_mem)
    sl = w_next()
    for hm in range(4):
        pz = ps_next([2, 3])
        for c in range(KC):
            MM(PS[pz][:, 0:256], wslot[sl][:, c, hm * 128:(hm + 1) * 128], memn[:, c, 0:256], c == 0, c == KC - 1,
               [f"w{sl}", "nT"], [f"ps{pz}"])
        ACT(sq16b[0][:, 0:256], PS[pz][:, 0:256], AF.Square, [f"ps{pz}"], ["sq16b0"])
        MM(PS[4][:, 0:256], ones16, sq16b[0][:, 0:256], True, True, ["ones16", "sq16b0"], ["ps4"])
        RSTD(rstd[0][:, 0:256], PS[4][:, 0:256], 1.0 / 128, ["ps4"], ["rstd0"])
        DVE_STT(mkT[:, hm, :], PS[pz][:, 0:256], cv(C_GMK), rstd[0][:, 0:256], ALU.mult, ALU.mult,
                [f"ps{pz}", "rstd0", "cvec"], ["mkT"])
    sl = w_next()
    for mt in range(2):
        pz = ps_next([2, 3])
        for c in range(KC):
            MM(PS[pz][:, :], memn[:, c, mt * 128:(mt + 1) * 128], wslot[sl][:, c, :], c == 0, c == KC - 1,
               [f"w{sl}", "nT"], [f"ps{pz}"])
        ACT(mvtok[:, mt, :], PS[pz][:, :], AF.Copy, [f"ps{pz}"], ["mvtok"])

    for st in range(NST):
        T0 = st * STT
        R0 = st * 16
        barrier()
        def load_x(buf, off, w, T0=T0):
            DMA("sp", xin[buf][:, :, 0:w], bass.AP(xT_t, T0 + off, [[XW, 128], [128 * XW, KC], [1, w]]),
                [], [f"xin{buf}"], f"xin{buf}")

        rms_norm_tiles([(0, 416), (416, 416), (832, 416), (1248, 416)], C_GMIX,
                       lambda c, off, w: nT[:, c, off:off + w], "nT", load_x)
        barrier()

        WT = [(0, 448), (448, 448), (896, 384), (1280, 384)]
        MT = [(0, 384), (384, 384), (768, 384)]

        def fm_block(sl, ntiles, col_off, consume, banks=(0, 1, 2, 3)):
            for m in range(4):
                for (off, w) in ntiles:
                    pz = ps_next(list(banks))
                    for c in range(KC):
                        MM(PS[pz][:, 0:w], wslot[sl][:, c, m * 128:(m + 1) * 128], nT[:, c, col_off + off: col_off + off + w],
                           c == 0, c == KC - 1, [f"w{sl}", "nT"], [f"ps{pz}"])
                    consume(m, off, w, pz)

        qkc = {"i": 0}

        def qk_consume(dst, gain_ap, lhs_ones, inv_d, tagdst):
            pend = []

            def stage_b(m, off, w, pz, i):
                pst = 4 + i
                MM(PS[pst][:, 0:w], lhs_ones, sq16b[i][:, 0:w], True, True, ["ones16", "onesbd", f"sq16b{i}"], [f"ps{pst}"])
                RSTD(rstd[i][:, 0:w], PS[pst][:, 0:w], inv_d, [f"ps{pst}"], [f"rstd{i}"])
                DVE_STT(dst(m, off, w), PS[pz][:, 0:w], gain_ap, rstd[i][:, 0:w], ALU.mult, ALU.mult,
                        [f"ps{pz}", f"rstd{i}", "cvec", "gq8", "gmq"], [tagdst])

            def f(m, off, w, pz):
                i = qkc["i"] % 2
                qkc["i"] += 1
                ACT(sq16b[i][:, 0:w], PS[pz][:, 0:w], AF.Square, [f"ps{pz}"], [f"sq16b{i}"])
                if pend:
                    stage_b(*pend.pop(0))
                pend.append((m, off, w, pz, i))

            def flush():
                while pend:
                    stage_b(*pend.pop(0))
            f.flush = flush
            return f

        def build_jobs():
            jobs = []
            for i in range(NPAIR):
                lo = max(0, 7 - 2 * i)
                hi = min(9, 25 - 2 * i)
                jobs.append(dict(pair=i, segs=[(a, b, 2 * i - 7 + a, a) for (a, b) in split_rows(lo, hi, 8)],
                                 bias=("kmask", i)))
                if st == 0 and i in (4, 5, 6):
                    base = 9 + 4 * (i - 4)
                    jobs.append(dict(pair=i, segs=[(0, 4, 1, base)], bias=("flag", 0)))
                if st == 1 and i in (6, 7):
                    base = 9 + 4 * (i - 6)
                    jobs.append(dict(pair=i, segs=[(0, 4, 13, base)], bias=("flag", 1)))
            return jobs

        jobs = build_jobs()
        first_touch, last_touch = {}, {}
        for ji, jb in enumerate(jobs):
            for (a, b, mrow0, slot0) in jb["segs"]:
                for r in range(mrow0, mrow0 + (b - a)):
                    g = r // 4
                    first_touch.setdefault(g, ji)
                    last_touch[g] = ji

        def bt_load(buf, h):
            t = BT[buf]
            key = f"BT{buf}"
            subs = [f"BT{buf}_{q}" for q in range(8)]
            MEMSET("pool", t[:, :, :], NEG, [key] + subs)

            def g2src(j0, nj):
                return bass.AP(g2_t, ((h * 15 + j0) * 64) * 128 + 63, [[127, 64], [64 * 128, nj], [1, 64]])
            lst = [(0, 0, 4, 8), (1, 1, 4, 8)]
            if st == 0:
                lst += [(0, 9 + 8, 0, 4), (0, 9 + 4, 2, 2), (1, 9 + 4, 1, 3), (1, 9 + 0, 3, 1)]
            else:
                lst += [(1, 9 + 1, 12, 3), (0, 9 + 4 + 2, 12, 2), (1, 9 + 4 + 3, 12, 1)]
            for q, (half, s0, j0, n) in enumerate(lst):
                DMA("sp", t[64 * half:64 * half + 64, s0:s0 + n, :], g2src(j0, n), [], [subs[q]], key)
            cm = colmask.unsqueeze(1).to_broadcast([128, 21, 64])
            DVE_TT(t[:, :, :], t[:, :, :], cm, ALU.add, [key, "colmask"] + subs, [key], eng="pool")

        def na_hp(hg, hp):
            hls = [hp * 2, hp * 2 + 1]
            k = hg * 4 + hp
            if k + 1 < 8:
                for eh in range(2):
                    bt_load(2 * ((k + 1) % 2) + eh, (k + 1) * 2 + eh)

            def qz_build(kk, hpp):
                for e in range(2):
                    PHe = slice(64 * e, 64 * e + 64)
                    s.op("pool", lambda en, e=e, PHe=PHe: en.tensor_copy(out=qz[kk % 2][e][PHe, :], in_=qT4[PHe, hpp, :]),
                         reads=["qT4"], writes=[f"qz{kk % 2}_{e}"], small=False)
            if hp == 0:
                qz_build(k, hp)
            if hp + 1 < 4:
                qz_build(k + 1, hp + 1)
            chunk = k
            units = [(ji, eh) for ji in range(len(jobs)) for eh in range(2)]

            def s_stage(ui):
                ji, eh = units[ui]
                jb = jobs[ji]
                i = jb["pair"]
                PH = slice(64 * eh, 64 * eh + 64)
                slot = ui % 4
                pslot = ui % 5
                pbase = 2 * (ui % 2)
                bt = BT[2 * (k % 2) + eh]
                btk = f"BT{2 * (k % 2) + eh}"
                col = 0
                for kk, (a, b, mrow0, bslot0) in enumerate(jb["segs"]):
                    n = (b - a) * 64
                    pb = pbase + kk
                    MM(PS[pb][:, 0:n], kT4[:, hp, i * 128:(i + 1) * 128], qz[k % 2][eh][:, mrow0 * 64: mrow0 * 64 + n],
                       True, True, ["kT4", f"qz{k % 2}_{eh}"], [f"ps{pb}"])
                    DVE_TT(Sb[slot][:, col:col + n], PS[pb][:, 0:n],
                           bt[:, bslot0:bslot0 + (b - a), :].rearrange("p a b -> p (a b)"), ALU.add,
                           [f"ps{pb}", btk], [f"Sb{slot}"])
                    col += n
                if jb["bias"][0] == "kmask":
                    bias_ap = cv(C_KMASK + st * NPAIR + i)
                else:
                    bias_ap = cv(C_FLAG + jb["bias"][1])
                ACT(PT[pslot][:, 0:col], Sb[slot][:, 0:col], AF.Exp, [f"Sb{slot}", "cvec"], [f"PT{pslot}"], bias=bias_ap)

            fresh = set()

            def p_stage(ui):
                ji, eh = units[ui]
                jb = jobs[ji]
                i = jb["pair"]
                hl = hls[eh]
                PH = slice(64 * eh, 64 * eh + 64)
                slot = ui % 5
                if eh == 0:
                    for g, fj in first_touch.items():
                        if fj == ji:
                            fresh.add(g)
                col = 0
                for (a, b, mrow0, bslot0) in jb["segs"]:
                    r = mrow0
                    while r < mrow0 + (b - a):
                        g = r // 4
                        r_end = min(mrow0 + (b - a), 4 * g + 4)
                        n = (r_end - r) * 64
                        pcol = col + (r - mrow0) * 64
                        acc = PS[4 + g % 4]
                        oc = (r - 4 * g) * 64
                        vc = hp * 192 + 64 * eh
                        st_flag = g in fresh
                        fresh.discard(g)
                        MM(acc[:, oc:oc + n], Vtok[:, i, vc:vc + 128], PT[slot][:, pcol:pcol + n],
                           st_flag, False, ["Vtok", f"PT{slot}"], [f"ps{4 + g % 4}"])
                        MM(acc[:, 256 + oc:256 + oc + n], onesz[eh], PT[slot][:, pcol:pcol + n],
                           False, False, [f"onesz{eh}", f"PT{slot}"], [f"ps{4 + g % 4}"])
                        r = r_end
                    col += (b - a) * 64
                if eh == 1:
                    for g, lj in last_touch.items():
                        if lj == ji:
                            nr = min(4, 18 - 4 * g)
                            n = nr * 64
                            acc = PS[4 + g % 4]
                            rb = g % 2
                            DVE_RECIP(recD[rb][:, 0:n], acc[:, 256:256 + n], [f"ps{4 + g % 4}"], [f"recD{rb}"])
                            DVE_TT(yTb[:, chunk, 256 * g:256 * g + n], acc[:, 0:n], recD[rb][:, 0:n], ALU.mult,
                                   [f"ps{4 + g % 4}", f"recD{rb}"], ["yTb"])

            LOOK = 4
            for ui in range(len(units) + LOOK):
                if ui < len(units):
                    s_stage(ui)
                if ui - LOOK >= 0:
                    p_stage(ui - LOOK)

        for eh in range(2):
            bt_load(eh, eh)
        for db_ in range(2):
            for e_ in range(2):
                MEMSET("pool", qz[db_][e_][64 * (1 - e_):64 * (1 - e_) + 64, :], 0.0, [f"qz{db_}_{e_}"])
        for hp_ in range(4):
            MEMSET("pool", Vtok[:, :, hp_ * 192 + 64:hp_ * 192 + 128], 0.0, ["Vtok"])
        for hg in range(2):
            sl = w_next()
            qcons = qk_consume(lambda m, off, w: qT4[:, m, off:off + w], gq8, onesbd, 1.0 / 64, "qT4")
            fm_block(sl, MT, 256, qcons)
            qcons.flush()
            sl = w_next()
            kcons = qk_consume(lambda m, off, w: kT4[:, m, off:off + w], cv(C_GK), onesbd, 1.0 / 64, "kT4")
            fm_block(sl, WT, 0, kcons)
            kcons.flush()
            sl = w_next()
            for i in range(NPAIR):
                pz = ps_next([0, 1, 2, 3])
                for c in range(KC):
                    MM(PS[pz][:, :], nT[:, c, i * 128:(i + 1) * 128], wslot[sl][:, c, :], c == 0, c == KC - 1,
                       [f"w{sl}", "nT"], [f"ps{pz}"])
                vsrc = PS[pz][:, :].rearrange("p (h e d) -> p h e d", h=4, e=2)
                vdst = Vtok[:, i, :].rearrange("p (h x) -> p h x", h=4)
                ACT(vdst[:, :, 0:64], vsrc[:, :, 0, :], AF.Copy, [f"ps{pz}"], ["Vtok"])
                ACT(vdst[:, :, 128:192], vsrc[:, :, 1, :], AF.Copy, [f"ps{pz}"], ["Vtok"])
            for hp in range(4):
                na_hp(hg, hp)
        barrier()
        sl = w_next()
        fm_block(sl, MT, 256, lambda m, off, w, pz: ACT(ch32[:, m, off:off + w], PS[pz][:, 0:w], AF.Copy, [f"ps{pz}"], ["ch32"]))
        sl = w_next()
        fm_block(sl, MT, 256, lambda m, off, w, pz: DVE_TT(ch32[:, m, off:off + w], PS[pz][:, 0:w], ch32[:, m, off:off + w],
                                                            ALU.mult, [f"ps{pz}", "ch32"], ["ch32"]))
        for m in range(4):
            wc = C_CONV + 3 * m
            ACT(cc32[:, m, 1:MID - 1], ch32[:, m, 1:MID - 1], AF.Copy, ["ch32", "cvec"], ["cc32"], scale=cv(wc + 1))
            DVE_STT(cc32[:, m, 1:MID - 1], ch32[:, m, 0:MID - 2], cv(wc + 0), cc32[:, m, 1:MID - 1], ALU.mult, ALU.add,
                    ["ch32", "cc32", "cvec"], ["cc32"])
            DVE_STT(cc32[:, m, 1:MID - 1], ch32[:, m, 2:MID], cv(wc + 2), cc32[:, m, 1:MID - 1], ALU.mult, ALU.add,
                    ["ch32", "cc32", "cvec"], ["cc32"])
        sl = w_next()

        def b_consume(m, off, w, pz):
            lo = max(off, 1)
            hi = min(off + w, MID - 1)
            DVE_TT(yTb[:, 8 + m, lo:hi], PS[pz][:, lo - off:hi - off], cc32[:, m, lo:hi], ALU.mult,
                   [f"ps{pz}", "cc32"], ["yTb"])
        fm_block(sl, MT, 256, b_consume)
        sl = w_next()
        mq = []

        def qm_s1(u):
            m, off, w, pz = mq[u]
            i = u % 2
            ACT(sq16[i][:, 0:w], PS[pz][:, 0:w], AF.Square, [f"ps{pz}"], [f"sq16{i}"])
            MM(PS[3][:, 0:w], ones16, sq16[i][:, 0:w], True, True, ["ones16", f"sq16{i}"], ["ps3"])
            RSTD(rstd[i][:, 0:w], PS[3][:, 0:w], 1.0 / 128, ["ps3"], [f"rstd{i}"])
            DVE_STT(qmT[i][:, 0:w], PS[pz][:, 0:w], gmq, rstd[i][:, 0:w], ALU.mult, ALU.mult,
                    [f"ps{pz}", f"rstd{i}", "gmq"], [f"qmT{i}"])

        def qm_s2(u):
            m, off, w, pz = mq[u]
            i = u % 2
            for mc in range(2):
                pS = 4 + mc
                MM(PS[pS][:, 0:w], mkT[:, m, mc * 128:(mc + 1) * 128], qmT[i][:, 0:w], True, True,
                   ["mkT", f"qmT{i}"], [f"ps{pS}"])
                ACT(PTm[i][:, mc, 0:w], PS[pS][:, 0:w], AF.Exp, [f"ps{pS}"], [f"PTm{i}_{mc}"])

        def qm_s3(u):
            m, off, w, pz = mq[u]
            i = u % 2
            pO, pD = 6, 7
            for mc in range(2):
                MM(PS[pO][:, 0:w], mvtok[:, mc, m * 128:(m + 1) * 128], PTm[i][:, mc, 0:w], mc == 0, mc == 1,
                   ["mvtok", f"PTm{i}_{mc}"], [f"ps{pO}"])
            for mc in range(2):
                MM(PS[pD][:, 0:w], ones16, PTm[i][:, mc, 0:w], mc == 0, mc == 1, ["ones16", f"PTm{i}_{mc}"], [f"ps{pD}"])
            DVE_RECIP(recM[i][:, 0:w], PS[pD][:, 0:w], [f"ps{pD}"], [f"recM{i}"])
            DVE_TT(yTb[:, 12 + m, off:off + w], PS[pO][:, 0:w], recM[i][:, 0:w], ALU.mult,
                   [f"ps{pO}", f"recM{i}"], ["yTb"])

        def qm_consume(m, off, w, pz):
            u = len(mq)
            mq.append((m, off, w, pz))
            qm_s1(u)
            if u >= 1:
                qm_s2(u - 1)
            if u >= 2:
                qm_s3(u - 2)
        fm_block(sl, MT, 256, qm_consume, banks=(0, 1, 2))
        nq = len(mq)
        qm_s2(nq - 1)
        qm_s3(nq - 2)
        qm_s3(nq - 1)
        barrier()
        if DEBUG:
            for c in range(KC):
                t32 = xin[0]
                ACT(t32[:, 0, 0:384], yTb[:, c, 0:384], AF.Copy, ["yTb"], ["xin0"])
                ACT(t32[:, 1, 0:384], yTb[:, c, 384:768], AF.Copy, ["yTb"], ["xin0"])
                ACT(t32[:, 2, 0:384], yTb[:, c, 768:1152], AF.Copy, ["yTb"], ["xin0"])
                DMA("sp", dbg_mix_t.ap()[st, c].rearrange("p (a b) -> p a b", a=3), t32[:, 0:3, 0:384], ["xin0"], [], "dbg")
            barrier()

        for c in range(KC):
            DMA("sp", x1T[:, c, :], bass.AP(xT_t, c * 128 * XW + T0 + 319, [[XW, 128], [1, MIX]]), [], [f"x1T{c}"], f"x1ld{c}")
        XT = [(0, 342), (342, 342), (684, 342)]
        for b in range(4):
            sl = w_next()
            for m in range(4):
                mc = b * 4 + m
                for (off, w) in XT:
                    pz = ps_next([0, 1, 2, 3])
                    for c in range(KC):
                        MM(PS[pz][:, 0:w], wslot[sl][:, c, m * 128:(m + 1) * 128], yTb[:, c, 63 + off:63 + off + w],
                           c == 0, c == KC - 1, [f"w{sl}", "yTb"], [f"ps{pz}"])
                    DVE_TT(x1T[:, mc, off:off + w], PS[pz][:, 0:w], x1T[:, mc, off:off + w], ALU.add,
                           [f"ps{pz}", f"x1T{mc}"], [f"x1T{mc}"])
        barrier()
        if DEBUG:
            for c in range(KC):
                DMA("sp", dbg_x1_t.ap()[st, c], x1T[:, c, :], [f"x1T{c}"], [], "dbg")
            barrier()

        for ti, (off, w) in enumerate(XT):
            pb = ps_next([0, 1, 2])
            for c in range(KC):
                sq = sq32[c % 4]
                ACT(sq[:, 0:w], x1T[:, c, off:off + w], AF.Square, [f"x1T{c}"], [f"sq32{c % 4}"])
                MM(PS[pb][:, 0:w], ones32, sq[:, 0:w], c == 0, c == KC - 1, ["ones32", f"sq32{c % 4}"], [f"ps{pb}"])
            RSTD(rstdD[:, off:off + w], PS[pb][:, 0:w], 1.0 / D, [f"ps{pb}"], ["rstdD"])
        if st == 0:
            DVE_TT(rstdD[:, 0:1], rstdD[:, 0:1], cv(C_FLAG + 2), ALU.mult, ["rstdD", "cvec"], ["rstdD"])
        else:
            DVE_TT(rstdD[:, MIX - 1:MIX], rstdD[:, MIX - 1:MIX], cv(C_FLAG + 3), ALU.mult, ["rstdD", "cvec"], ["rstdD"])
        NP = [(0, 344), (344, 686), (686, MIX)]

        def n2k(c, lo, hi):
            return [f"n2T_{c}_{p}" for p, (a_, b_) in enumerate(NP) if a_ < hi and lo < b_]
        for p, (a_, b_) in enumerate(NP):
            for c in range(KC):
                DVE_STT(n2T[:, c, a_:b_], x1T[:, c, a_:b_], cv(C_GFFN + c), rstdD[:, a_:b_], ALU.mult, ALU.mult,
                        [f"x1T{c}", "rstdD", "cvec"], [f"n2T_{c}_{p}"])
        OT = [(0, 342), (342, 684), (684, 1024)]
        tcount = {"i": 0}
        for gi, grp in enumerate(FF_GROUPS):
            hb = gi % 2
            nblk = len(grp)
            for bi, b in enumerate(grp):
                sl = w_next()
                for jj in range(4):
                    j = b * 4 + jj
                    wc = C_FCONV + 3 * j
                    banks = [0, 1, 2] if (tcount["i"] % 2 == 0) else [3, 4, 5]
                    tcount["i"] += 1
                    for ti, (o0, o1) in enumerate(OT):
                        w = o1 - o0 + 2
                        pz = banks[ti]
                        for c in range(KC):
                            MM(PS[pz][:, 0:w], wslot[sl][:, c, jj * 128:(jj + 1) * 128], n2T[:, c, o0:o0 + w],
                               c == 0, c == KC - 1, [f"w{sl}"] + n2k(c, o0, o0 + w), [f"ps{pz}"])
                        n = o1 - o0
                        t = ctmp[ti % 2]
                        tk = f"ctmp{ti % 2}"
                        ACT(t[:, 0:n], PS[pz][:, 1:1 + n], AF.Copy, [f"ps{pz}", "cvec"], [tk], scale=cv(wc + 1))
                        DVE_STT(t[:, 0:n], PS[pz][:, 0:n], cv(wc + 0), t[:, 0:n], ALU.mult, ALU.add, [f"ps{pz}", tk, "cvec"], [tk])
                        DVE_STT(t[:, 0:n], PS[pz][:, 2:2 + n], cv(wc + 2), t[:, 0:n], ALU.mult, ALU.add, [f"ps{pz}", tk, "cvec"], [tk])
                        ACT(sa32[:, jj, o0:o1], t[:, 0:n], AF.Silu, [tk], ["sa32"])
                sl = w_next()
                for jj in range(4):
                    j = D_FF // 128 + b * 4 + jj
                    wc = C_FCONV + 3 * j
                    banks = [0, 1, 2] if (tcount["i"] % 2 == 0) else [3, 4, 5]
                    tcount["i"] += 1
                    for ti, (o0, o1) in enumerate(OT):
                        w = o1 - o0 + 2
                        pz = banks[ti]
                        for c in range(KC):
                            MM(PS[pz][:, 0:w], wslot[sl][:, c, jj * 128:(jj + 1) * 128], n2T[:, c, o0:o0 + w],
                               c == 0, c == KC - 1, [f"w{sl}"] + n2k(c, o0, o0 + w), [f"ps{pz}"])
                        n = o1 - o0
                        t = ctmp[2 + ti % 2]
                        tk = f"ctmp{2 + ti % 2}"
                        ACT(t[:, 0:n], PS[pz][:, 1:1 + n], AF.Copy, [f"ps{pz}", "cvec"], [tk], scale=cv(wc + 1))
                        DVE_STT(t[:, 0:n], PS[pz][:, 0:n], cv(wc + 0), t[:, 0:n], ALU.mult, ALU.add, [f"ps{pz}", tk, "cvec"], [tk])
                        DVE_STT(t[:, 0:n], PS[pz][:, 2:2 + n], cv(wc + 2), t[:, 0:n], ALU.mult, ALU.add, [f"ps{pz}", tk, "cvec"], [tk])
                        DVE_TT(hT[hb][:, bi * 4 + jj, o0:o1], t[:, 0:n], sa32[:, jj, o0:o1], ALU.mult,
                               [tk, "sa32"], [f"hT{hb}_{bi * 4 + jj}"], eng="pool")
            nkf = 4 * nblk
            last_grp = (gi == len(FF_GROUPS) - 1)
            for mb in range(4):
                sl = w_next()
                for m in range(4):
                    mc = mb * 4 + m
                    for (o0, o1) in ((0, 512), (512, 1024)):
                        n = o1 - o0
                        pz = ps_next([6, 7])
                        for kf in range(nkf):
                            MM(PS[pz][:, 0:n], wslot_fo[sl][:, kf, m * 128:(m + 1) * 128], hT[hb][:, kf, o0:o1],
                               kf == 0, kf == nkf - 1, [f"w{sl}", f"hT{hb}_{kf}"], [f"ps{pz}"])
                        DVE_TT(x1T[:, mc, 1 + o0:1 + o1], PS[pz][:, 0:n], x1T[:, mc, 1 + o0:1 + o1], ALU.add,
                               [f"ps{pz}", f"x1T{mc}"], [f"x1T{mc}"])
                    if last_grp:
                        DMA("sp", bass.AP(yT_t, mc * 128 * TOK + T0, [[TOK, 128], [1, STT]]), x1T[:, mc, 1:1 + STT],
                            [f"x1T{mc}"], [], f"yout{mc % 4}")

    assert wuse["i"] == len(wq), (wuse["i"], len(wq))
    s.emit()
    return nc, s.stats


_SEQS = [("p", 0, 0), ("p", 0, 32), ("p", 0, 64), ("p", 0, 96), ("s", 0, 0), ("s", 0, 32), ("s", 1, 0), ("s", 1, 32)]


def _prep_core(core, x_prompt, x_sample, mem_prompt, mem_sample, cv_common, nrows):
    kind, b, r0 = _SEQS[core]
    x = x_prompt[b] if kind == "p" else x_sample[b]
    mem = mem_prompt[b] if kind == "p" else mem_sample[b]
    rows = nrows[kind]
    xe = np.zeros((XW, D), np.float32)
    g0 = (r0 - 5) * 64
    lo = max(g0, 0)
    hi = min(g0 + XW, rows * 64)
    xe[lo - g0:hi - g0] = x[lo:hi]
    xT = np.ascontiguousarray(xe.T).reshape(KC, 128, XW)
    memT = np.ascontiguousarray(mem.T).reshape(KC, 128, 256)
    cvv = cv_common.copy()
    for st in range(NST):
        for i in range(NPAIR):
            for half in range(2):
                gr = r0 + 16 * st - 5 + 2 * i + half
                ok = 0 <= gr < rows
                cvv[64 * half:64 * half + 64, C_KMASK + st * NPAIR + i] = 0.0 if ok else NEG
    top = (r0 == 0)
    bot = (r0 + 32 == rows)
    cvv[:, C_FLAG + 0] = 0.0 if top else NEG
    cvv[:, C_FLAG + 1] = 0.0 if bot else NEG
    cvv[:, C_FLAG + 2] = 0.0 if top else 1.0
    cvv[:, C_FLAG + 3] = 0.0 if bot else 1.0
    return xT, memT, cvv


_CACHE = {}


def kernel(x_prompt, x_sample, mem_prompt, mem_sample, g_mix, w_in, na_q_gain, na_k_gain,
           na_rel_bias, conv_w, mem_norm_g, w_mem_kv, mem_q_gain, mem_k_gain, w_out,
           g_ffn, w_ffn_in, ffn_conv_w, w_ffn_out):
    f = lambda a: np.ascontiguousarray(np.asarray(a, dtype=np.float32))
    x_prompt, x_sample, mem_prompt, mem_sample = f(x_prompt), f(x_sample), f(mem_prompt), f(mem_sample)
    cvc = np.zeros((128, NCV), np.float32)
    cvc[:, C_GMIX:C_GMIX + 16] = f(g_mix)[0].reshape(16, 128).T
    cvc[:, C_GFFN:C_GFFN + 16] = f(g_ffn)[0].reshape(16, 128).T
    cvc[:, C_GMEM:C_GMEM + 16] = f(mem_norm_g)[0].reshape(16, 128).T
    cvc[:, C_GQ] = np.tile(f(na_q_gain)[0], 2)
    cvc[:, C_GK] = np.tile(f(na_k_gain)[0], 2)
    cvc[:, C_GMQ] = f(mem_q_gain)[0]
    cvc[:, C_GMK] = f(mem_k_gain)[0]
    cw = f(conv_w)[0]
    cvc[:, C_CONV:C_CONV + 12] = cw.reshape(3, 4, 128).transpose(2, 1, 0).reshape(128, 12)
    fw = f(ffn_conv_w)[0]
    cvc[:, C_FCONV:C_FCONV + 264] = fw.reshape(3, 88, 128).transpose(2, 1, 0).reshape(128, 264)
    qc = np.arange(64)
    q_cs = np.clip(qc - 8, 0, 48)
    kc = np.arange(64)[:, None]
    valid = (kc >= q_cs[None, :]) & (kc < q_cs[None, :] + 16)
    cm = np.where(valid, 0.0, NEG).astype(np.float32)
    colmask = np.concatenate([cm, cm], axis=0)
    rb = f(na_rel_bias)[0]
    G = np.zeros((16, 15, 128), np.float32)
    G[:, :, 48:79] = rb[:, ::-1, ::-1]
    g2 = np.ascontiguousarray(np.broadcast_to(G[:, :, None, :], (16, 15, 64, 128))).reshape(16 * 15 * 64, 128)

    nrows = {"p": x_prompt.shape[1] // 64, "s": x_sample.shape[1] // 64}
    if "nc" not in _CACHE:
        _CACHE["nc"] = build_program()
    nc, stats = _CACHE["nc"]
    shared = {"w_in": f(w_in)[0], "w_out": f(w_out)[0], "w_ffn_in": f(w_ffn_in)[0], "w_ffn_out": f(w_ffn_out)[0],
              "w_mem_kv": f(w_mem_kv)[0], "colmask": colmask, "g2": g2}
    in_maps = []
    for core in range(NCORE):
        xT, memT, cvv = _prep_core(core, x_prompt, x_sample, mem_prompt, mem_sample, cvc, nrows)
        d = dict(shared)
        d.update({"xT": xT, "memT": memT, "cvec": cvv})
        in_maps.append(d)
    res = run_bass_kernel_spmd(nc, in_maps, core_ids=list(range(NCORE)))
    _CACHE["res"] = res
    y_prompt = np.zeros_like(x_prompt)
    y_sample = np.zeros_like(x_sample)
    for core in range(NCORE):
        kind, b, r0 = _SEQS[core]
        yT = np.asarray(res.results[core]["yT"]).reshape(D, TOK)
        dst = y_prompt if kind == "p" else y_sample
        dst[b, r0 * 64:r0 * 64 + TOK, :] = yT.T
    return (y_prompt, y_sample)
```

```python
import contextlib
import math
import os

import numpy as np
import concourse.bass as bass
import concourse.mybir as mybir
from concourse.bass_utils import run_bass_kernel_spmd

F32 = mybir.dt.float32
BF16 = mybir.dt.bfloat16
AF = mybir.ActivationFunctionType
ALU = mybir.AluOpType

NEG = -1.0e30
EPS = 1e-6
D = 2048
KC = 16
NCORE = 8
TOK = 2048
NST = 2
STT = 1024
XW = 2688
WIDE = 1664
MID = 1152
MIX = 1026
NPAIR = 13
D_FF = 5632
NCV = 358
C_GMIX, C_GFFN, C_GMEM, C_GQ, C_GK, C_GMQ, C_GMK, C_CONV, C_FCONV, C_KMASK, C_FLAG = 0, 16, 32, 48, 49, 50, 51, 52, 64, 328, 354

DEBUG = bool(int(os.environ.get("MK_DEBUG", "0")))
SAME_SYNC = {"pe": False, "act": True, "dve": True, "pool": True, "sp": False}
SAME_ALL = bool(int(os.environ.get("MK_SAME_ALL", "1")))


class Sched:
    ENG = ("pe", "act", "dve", "pool", "sp")

    def __init__(self, nc):
        self.nc = nc
        self.ops = []
        self.last_w = {}
        self.readers = {}
        self.slot_cnt = {}
        self.last_on = {}
        self.pending_bar = {}

    def op(self, eng, fn, reads=(), writes=(), dma_slot=None, extra_deps=(), small=True):
        idx = len(self.ops)
        deps = set(extra_deps)
        if eng in self.pending_bar:
            deps.update(self.pending_bar.pop(eng))
        for k in reads:
            w = self.last_w.get(k)
            if w is not None:
                deps.add(w)
        for k in writes:
            w = self.last_w.get(k)
            if w is not None:
                deps.add(w)
            deps.update(self.readers.get(k, {}).values())
        for k in reads:
            rk = self.readers.setdefault(k, {})
            rk[eng if dma_slot is None else (eng, dma_slot)] = idx
        for k in writes:
            self.last_w[k] = idx
            self.readers[k] = {}
        deps.discard(idx)
        o = dict(eng=eng, fn=fn, deps=deps, idx=idx, dma=dma_slot, sig=False, small=small)
        if dma_slot is not None:
            self.slot_cnt[dma_slot] = self.slot_cnt.get(dma_slot, 0) + 16
            o["dma_val"] = self.slot_cnt[dma_slot]
        self.ops.append(o)
        self.last_on[eng] = idx
        return idx

    def barrier(self, dummies):
        marks = []
        for e in ("pe", "act", "dve", "pool"):
            if e in self.last_on:
                marks.append(self.last_on[e])
        dmas = []
        seen = set()
        for o in reversed(self.ops):
            if o["dma"] is not None and o["dma"] not in seen:
                seen.add(o["dma"])
                dmas.append(o["idx"])
        deps = marks + dmas
        d_act, d_dve, d_pool = dummies
        self.op("act", lambda e: e.activation(out=d_act, in_=d_act, func=AF.Copy), reads=["dums"], extra_deps=deps)
        self.op("dve", lambda e: e.memset(d_dve, 0.0), extra_deps=deps)
        self.op("pool", lambda e: e.memset(d_pool, 0.0), extra_deps=deps)
        self.pending_bar = {"pe": list(deps), "sp": list(deps)}

    def emit(self, final_wait_eng="sp"):
        nc = self.nc
        ops = self.ops
        for o in ops:
            for d in o["deps"]:
                do = ops[d]
                if do["dma"] is None:
                    if do["eng"] != o["eng"] or (SAME_SYNC[o["eng"]] and (SAME_ALL or do["small"])) or o["dma"] is not None:
                        do["sig"] = True
        cnt = {e: 0 for e in self.ENG}
        for o in ops:
            if o["dma"] is None and o["sig"]:
                cnt[o["eng"]] += 1
                o["sigval"] = cnt[o["eng"]]
        slots = sorted(self.slot_cnt)
        waited = {e: {} for e in self.ENG}
        nwait = 0
        for o in ops:
            need = {}
            for d in o["deps"]:
                do = ops[d]
                if do["dma"] is not None:
                    key = ("d", do["dma"])
                    val = do["dma_val"]
                else:
                    if do["eng"] == o["eng"] and o["dma"] is None and not (SAME_SYNC[o["eng"]] and (SAME_ALL or do["small"])):
                        continue
                    key = ("e", do["eng"])
                    val = do["sigval"]
                need[key] = max(need.get(key, 0), val)
            w = []
            for key, val in need.items():
                if waited[o["eng"]].get(key, 0) >= val:
                    continue
                waited[o["eng"]][key] = val
                w.append((key, val))
            o["waits"] = w
            nwait += len(w)
        self.stats = dict(nops=len(ops), nwait=nwait, nsig=dict(cnt), nslots=len(slots))
        with contextlib.ExitStack() as st:
            esem = {e: st.enter_context(nc.semaphore("s_" + e)) for e in self.ENG}
            ssem = {s: st.enter_context(nc.semaphore("d_" + str(s))) for s in slots}
            block = st.enter_context(nc.Block())

            def run_engine(ename):
                def body(eng):
                    for o in ops:
                        if o["eng"] != ename:
                            continue
                        for key, val in o["waits"]:
                            sem = ssem[key[1]] if key[0] == "d" else esem[key[1]]
                            eng.wait_ge(sem, val)
                        ins = o["fn"](eng)
                        if o["dma"] is not None:
                            ins.then_inc(ssem[o["dma"]], 16)
                        elif o["sig"]:
                            ins.then_inc(esem[ename], 1)
                    if ename == final_wait_eng:
                        for s in slots:
                            eng.wait_ge(ssem[s], self.slot_cnt[s])
                return body

            block.tensor(run_engine("pe"))
            block.scalar(run_engine("act"))
            block.vector(run_engine("dve"))
            block.gpsimd(run_engine("pool"))
            block.sync(run_engine("sp"))


def split_rows(lo, hi, m):
    out = []
    while lo < hi:
        n = min(m, hi - lo)
        out.append((lo, lo + n))
        lo += n
    return out


def build_program():
    nc = bass.Bass("TRN2", target_bir_lowering=False)
    xT_t = nc.dram_tensor("xT", [KC, 128, XW], F32, kind="ExternalInput")
    memT_t = nc.dram_tensor("memT", [KC, 128, 256], F32, kind="ExternalInput")
    w_in_t = nc.dram_tensor("w_in", [D, 5120], F32, kind="ExternalInput")
    w_out_t = nc.dram_tensor("w_out", [D, D], F32, kind="ExternalInput")
    w_fi_t = nc.dram_tensor("w_ffn_in", [D, 2 * D_FF], F32, kind="ExternalInput")
    w_fo_t = nc.dram_tensor("w_ffn_out", [D_FF, D], F32, kind="ExternalInput")
    w_mkv_t = nc.dram_tensor("w_mem_kv", [D, 1024], F32, kind="ExternalInput")
    cvec_t = nc.dram_tensor("cvec", [128, NCV], F32, kind="ExternalInput")
    cmask_t = nc.dram_tensor("colmask", [128, 64], F32, kind="ExternalInput")
    g2_t = nc.dram_tensor("g2", [16 * 15 * 64, 128], F32, kind="ExternalInput")
    yT_t = nc.dram_tensor("yT", [KC, 128, TOK], F32, kind="ExternalOutput")
    if DEBUG:
        dbg_mix_t = nc.dram_tensor("dbg_mix", [NST, KC, 128, MID], F32, kind="ExternalOutput")
        dbg_x1_t = nc.dram_tensor("dbg_x1", [NST, KC, 128, MIX], F32, kind="ExternalOutput")

    ARENA_BYTES = 211968
    arena = nc.alloc_sbuf_tensor("arena", [128, ARENA_BYTES // 4], F32)

    def view(off, shape, dt):
        n = int(np.prod(shape))
        sz = 4 if dt == F32 else 2
        assert off % 4 == 0 and (n * sz) % 4 == 0
        assert off + n * sz <= ARENA_BYTES, (off, n * sz)
        ap = arena[:, off // 4: off // 4 + (n * sz) // 4]
        if dt != F32:
            ap = ap.bitcast(dt)
        if len(shape) == 2:
            ap = ap.rearrange("p (a b) -> p a b", a=shape[0])
        elif len(shape) == 3:
            ap = ap.rearrange("p (a b c) -> p a b c", a=shape[0], b=shape[1])
        return ap

    class Alloc:
        def __init__(self, base, limit):
            self.o = base
            self.limit = limit

        def get(self, shape, dt):
            n = int(np.prod(shape)) * (4 if dt == F32 else 2)
            n = (n + 31) // 32 * 32
            v = view(self.o, shape, dt)
            self.o += n
            assert self.o <= self.limit, (self.o, self.limit)
            return v

    ca = Alloc(0, 8192)
    cvec = ca.get([NCV], F32)
    colmask = ca.get([64], F32)
    ones32 = ca.get([128], F32)
    ones16 = ca.get([128], BF16)
    onesbd = ca.get([128], BF16)
    onesz = [ca.get([128], BF16) for _ in range(2)]
    mkT = ca.get([4, 256], BF16)
    mvtok = ca.get([2, 512], BF16)
    gq8 = ca.get([1], F32)
    gmq = ca.get([1], F32)
    dums = ca.get([3, 1], F32)
    W0 = 8192
    wslot = [view(W0 + i * 16384, [16, 512], BF16) for i in range(2)]
    wslot_fo = [view(W0 + i * 16384, [8, 512], BF16) for i in range(2)]
    R1 = W0 + 2 * 16384
    nT = view(R1, [KC, WIDE], BF16)
    x1T = view(R1, [KC, MIX], F32)
    R2 = R1 + 65664
    yTb = view(R2, [KC, MID], BF16)
    n2T = view(R2, [KC, MIX], BF16)
    R3 = R2 + 36864
    R3_END = ARENA_BYTES
    a3 = Alloc(R3, R3_END)
    xin = [a3.get([KC, 416], F32) for _ in range(2)]
    b3 = Alloc(R3, R3_END)
    qT4 = b3.get([4, MID], BF16)
    kT4 = b3.get([4, WIDE], BF16)
    Vtok = b3.get([NPAIR, 768], BF16)
    BT = [b3.get([21, 64], F32) for _ in range(2)]
    BT += [view(R1 + 53248 + q * 5376, [21, 64], F32) for q in range(2)]
    qz = [[view(R2 + 8 * MID * 2 + (2 * db + e) * MID * 2, [MID], BF16) for e in range(2)] for db in range(2)]
    PT = [b3.get([576], BF16) for _ in range(3)]
    PT.append(view(R1 + 53248 + 2 * 5376, [576], BF16))
    Sb = [view(R2 + 12 * MID * 2 + q * 2304, [576], F32) for q in range(4)]
    SCR = R3_END - 11264
    assert b3.o <= SCR, b3.o
    sa_ = Alloc(SCR, R3_END)
    PT.append(sa_.get([576], BF16))
    sqx3 = sa_.get([448], F32)
    sa_.get([160], F32)
    rstd = [sa_.get([448], F32) for _ in range(2)]
    so_ = sa_.o
    sq32x = [sa_.get([448], F32) for _ in range(2)] + [sqx3]
    sb_ = Alloc(so_, R3_END)
    sq16b = [sb_.get([448], BF16) for _ in range(2)]
    recD = [sb_.get([256], F32) for _ in range(2)]
    c3 = Alloc(R3, SCR)
    ch32 = c3.get([4, MID], F32)
    cc32 = c3.get([4, MID], F32)
    qmT = [c3.get([384], BF16) for _ in range(2)]
    PTm = [c3.get([2, 384], BF16) for _ in range(2)]
    recM = [c3.get([384], F32) for _ in range(2)]
    sq16 = [c3.get([448], BF16) for _ in range(2)]
    assert c3.o <= SCR, c3.o
    d3 = Alloc(R3, R3_END)
    hT = [d3.get([8, STT], BF16) for _ in range(2)]
    sa32 = d3.get([4, STT], F32)
    ctmp = [d3.get([344], F32) for _ in range(6)]
    sq32 = [d3.get([344], F32) for _ in range(4)]
    rstdD = d3.get([MIX], F32)
    sdD = d3.get([344], F32)

    PS = [nc.alloc_psum_tensor(f"ps{i}", [128, 512], F32) for i in range(8)]

    s = Sched(nc)
    dummies = (dums[0:1, 0, :], dums[0:1, 1, :], dums[0:1, 2, :])

    def barrier():
        s.barrier(dummies)

    wq = []
    wstate = {"issued": 0}

    def wsrc(t, ncols_total, col0, nk, ncols, row0=0):
        return bass.AP(t, row0 * ncols_total + col0, [[ncols_total, 128], [128 * ncols_total, nk], [1, ncols]])

    def w_issue_upto(n):
        while wstate["issued"] < min(n, len(wq)):
            i = wstate["issued"]
            src, kind = wq[i]
            sl = i % 2
            dst = wslot[sl][:] if kind == "k16" else wslot_fo[sl][:, 0:kind, :]
            s.op("pool", lambda e, dst=dst, src=src: e.dma_start(out=dst, in_=src),
                 writes=[f"w{sl}"], dma_slot=f"w{sl}")
            wstate["issued"] += 1

    wuse = {"i": 0}

    def w_next():
        i = wuse["i"]
        wuse["i"] += 1
        w_issue_upto(i + 2)
        sl = i % 2
        return sl

    wq.append((wsrc(w_mkv_t, 1024, 0, 16, 512), "k16"))
    wq.append((wsrc(w_mkv_t, 1024, 512, 16, 512), "k16"))
    FF_GROUPS = [(0, 1), (2, 3), (4, 5), (6, 7), (8, 9), (10,)]
    for st_ in range(NST):
        for b in (0, 2, 4, 1, 3, 5, 6, 8, 7, 9):
            wq.append((wsrc(w_in_t, 5120, b * 512, 16, 512), "k16"))
        for b in range(4):
            wq.append((wsrc(w_out_t, D, b * 512, 16, 512), "k16"))
        for grp in FF_GROUPS:
            for b in grp:
                wq.append((wsrc(w_fi_t, 2 * D_FF, b * 512, 16, 512), "k16"))
                wq.append((wsrc(w_fi_t, 2 * D_FF, D_FF + b * 512, 16, 512), "k16"))
            for mb in range(4):
                wq.append((wsrc(w_fo_t, D, mb * 512, 4 * len(grp), 512, row0=grp[0] * 512), 4 * len(grp)))

    psrot = {"i": 0}

    def ps_next(banks):
        b = banks[psrot["i"] % len(banks)]
        psrot["i"] += 1
        return b

    def MM(out, lhsT, rhs, start, stop, reads, writes):
        s.op("pe", lambda e: e.matmul(out, lhsT, rhs, start=start, stop=stop, skip_group_check=True),
             reads=reads, writes=writes)

    def ACT(out, in_, func, reads, writes, bias=None, scale=None):
        kw = {}
        if bias is not None:
            kw["bias"] = bias
        if scale is not None:
            kw["scale"] = scale
        s.op("act", lambda e: e.activation(out=out, in_=in_, func=func, **kw), reads=reads, writes=writes, small=out.free_size() < 128)

    def DVE_TT(out, in0, in1, op, reads, writes, eng="dve"):
        s.op(eng, lambda e: e.tensor_tensor(out=out, in0=in0, in1=in1, op=op), reads=reads, writes=writes, small=out.free_size() < 128)

    def DVE_STT(out, in0, scalar, in1, op0, op1, reads, writes, eng="dve"):
        s.op(eng, lambda e: e.scalar_tensor_tensor(out=out, in0=in0, scalar=scalar, in1=in1, op0=op0, op1=op1),
             reads=reads, writes=writes, small=out.free_size() < 128)

    def DVE_TS(out, in0, scalar1, op0, reads, writes, eng="dve"):
        s.op(eng, lambda e: e.tensor_scalar(out=out, in0=in0, scalar1=scalar1, scalar2=None, op0=op0),
             reads=reads, writes=writes, small=out.free_size() < 128)

    def DVE_RECIP(out, in_, reads, writes):
        ACT(out, in_, AF.Ln, reads, writes)
        ACT(out, out, AF.Exp, writes, writes, scale=-1.0)

    def RSTD(out, in_psum, inv_d, reads, writes):
        ACT(out, in_psum, AF.Ln, reads, writes, bias=EPS, scale=inv_d)
        ACT(out, out, AF.Exp, writes, writes, scale=-0.5)

    def MEMSET(eng, ap, val, writes):
        s.op(eng, lambda e: e.memset(ap, val), writes=writes, small=ap.free_size() < 128)

    def DMA(eng, out, in_, reads, writes, slot):
        s.op(eng, lambda e: e.dma_start(out=out, in_=in_), reads=reads, writes=writes, dma_slot=slot)

    def cv(col, n=1):
        return cvec[:, col:col + n]

    DMA("sp", cvec, cvec_t.ap(), [], ["cvec"], "c0")
    DMA("sp", colmask, cmask_t.ap(), [], ["colmask"], "c1")
    MEMSET("dve", dums[:, :, :], 0.0, ["dums"])
    MEMSET("dve", ones32, 1.0, ["ones32"])
    MEMSET("dve", ones16, 1.0, ["ones16"])
    MEMSET("dve", onesbd, 0.0, ["onesbd"])
    MEMSET("dve", onesbd[0:64, 0:64], 1.0, ["onesbd"])
    MEMSET("dve", onesbd[64:128, 64:128], 1.0, ["onesbd"])
    for e_ in range(2):
        MEMSET("dve", onesz[e_], 0.0, [f"onesz{e_}"])
        MEMSET("dve", onesz[e_][:, 64 * e_:64 * e_ + 64], 1.0, [f"onesz{e_}"])
    DVE_TS(gq8, cv(C_GQ), 0.125, ALU.mult, ["cvec"], ["gq8"])
    DVE_TS(gmq, cv(C_GMQ), 1.0 / math.sqrt(128.0), ALU.mult, ["cvec"], ["gmq"])
    w_issue_upto(2)

    def rms_norm_tiles(n_list, gcol, dst_fn, tagdst, xin_loader):
        for ti, (off, w) in enumerate(n_list):
            buf = ti % 2
            xin_loader(buf, off, w)
            pb = ps_next([0, 1])
            for c in range(KC):
                sq = sq32x[c % 3]
                ACT(sq[:, 0:w], xin[buf][:, c, 0:w], AF.Square, [f"xin{buf}"], [f"sqx{c % 3}"])
                MM(PS[pb][:, 0:w], ones32, sq[:, 0:w], c == 0, c == KC - 1, ["ones32", f"sqx{c % 3}"], [f"ps{pb}"])
            RSTD(rstd[buf][:, 0:w], PS[pb][:, 0:w], 1.0 / D, [f"ps{pb}"], [f"rstd{buf}"])
            for c in range(KC):
                DVE_STT(dst_fn(c, off, w), xin[buf][:, c, 0:w], cv(gcol + c), rstd[buf][:, 0:w], ALU.mult, ALU.mult,
                        [f"xin{buf}", f"rstd{buf}", "cvec"], [tagdst])

    def load_mem(buf, off, w):
        DMA("sp", xin[buf][:, :, 0:w], bass.AP(memT_t, off, [[256, 128], [128 * 256, KC], [1, w]]),
            [], [f"xin{buf}"], f"xin{buf}")

    memn = nT
    rms_norm_tiles([(0, 256)], C_GMEM, lambda c, off, w: memn[:, c, off:off + w], "nT", load_mem)
    sl = w_next()
    for hm in range(4):
        pz = ps_next([2, 3])
        for c in range(KC):
            MM(PS[pz][:, 0:256], wslot[sl][:, c, hm * 128:(hm + 1) * 128], memn[:, c, 0:256], c == 0, c == KC - 1,
               [f"w{sl}", "nT"], [f"ps{pz}"])
        ACT(sq16b[0][:, 0:256], PS[pz][:, 0:256], AF.Square, [f"ps{pz}"], ["sq16b0"])
        MM(PS[4][:, 0:256], ones16, sq16b[0][:, 0:256], True, True, ["ones16", "sq16b0"], ["ps4"])
        RSTD(rstd[0][:, 0:256], PS[4][:, 0:256], 1.0 / 128, ["ps4"], ["rstd0"])
        DVE_STT(mkT[:, hm, :], PS[pz][:, 0:256], cv(C_GMK), rstd[0][:, 0:256], ALU.mult, ALU.mult,
                [f"ps{pz}", "rstd0", "cvec"], ["mkT"])
    sl = w_next()
    for mt in range(2):
        pz = ps_next([2, 3])
        for c in range(KC):
            MM(PS[pz][:, :], memn[:, c, mt * 128:(mt + 1) * 128], wslot[sl][:, c, :], c == 0, c == KC - 1,
               [f"w{sl}", "nT"], [f"ps{pz}"])
        ACT(mvtok[:, mt, :], PS[pz][:, :], AF.Copy, [f"ps{pz}"], ["mvtok"])

    for st in range(NST):
        T0 = st * STT
        R0 = st * 16
        barrier()
        def load_x(buf, off, w, T0=T0):
            DMA("sp", xin[buf][:, :, 0:w], bass.AP(xT_t, T0 + off, [[XW, 128], [128 * XW, KC], [1, w]]),
                [], [f"xin{buf}"], f"xin{buf}")

        rms_norm_tiles([(0, 416), (416, 416), (832, 416), (1248, 416)], C_GMIX,
                       lambda c, off, w: nT[:, c, off:off + w], "nT", load_x)
        barrier()

        WT = [(0, 448), (448, 448), (896, 384), (1280, 384)]
        MT = [(0, 384), (384, 384), (768, 384)]

        def fm_block(sl, ntiles, col_off, consume, banks=(0, 1, 2, 3)):
            for m in range(4):
                for (off, w) in ntiles:
                    pz = ps_next(list(banks))
                    for c in range(KC):
                        MM(PS[pz][:, 0:w], wslot[sl][:, c, m * 128:(m + 1) * 128], nT[:, c, col_off + off: col_off + off + w],
                           c == 0, c == KC - 1, [f"w{sl}", "nT"], [f"ps{pz}"])
                    consume(m, off, w, pz)

        qkc = {"i": 0}

        def qk_consume(dst, gain_ap, lhs_ones, inv_d, tagdst):
            pend = []

            def stage_b(m, off, w, pz, i):
                pst = 4 + i
                MM(PS[pst][:, 0:w], lhs_ones, sq16b[i][:, 0:w], True, True, ["ones16", "onesbd", f"sq16b{i}"], [f"ps{pst}"])
                RSTD(rstd[i][:, 0:w], PS[pst][:, 0:w], inv_d, [f"ps{pst}"], [f"rstd{i}"])
                DVE_STT(dst(m, off, w), PS[pz][:, 0:w], gain_ap, rstd[i][:, 0:w], ALU.mult, ALU.mult,
                        [f"ps{pz}", f"rstd{i}", "cvec", "gq8", "gmq"], [tagdst])

            def f(m, off, w, pz):
                i = qkc["i"] % 2
                qkc["i"] += 1
                ACT(sq16b[i][:, 0:w], PS[pz][:, 0:w], AF.Square, [f"ps{pz}"], [f"sq16b{i}"])
                if pend:
                    stage_b(*pend.pop(0))
                pend.append((m, off, w, pz, i))

            def flush():
                while pend:
                    stage_b(*pend.pop(0))
            f.flush = flush
            return f

        def build_jobs():
            jobs = []
            for i in range(NPAIR):
                lo = max(0, 7 - 2 * i)
                hi = min(9, 25 - 2 * i)
                jobs.append(dict(pair=i, segs=[(a, b, 2 * i - 7 + a, a) for (a, b) in split_rows(lo, hi, 8)],
                                 bias=("kmask", i)))
                if st == 0 and i in (4, 5, 6):
                    base = 9 + 4 * (i - 4)
                    jobs.append(dict(pair=i, segs=[(0, 4, 1, base)], bias=("flag", 0)))
                if st == 1 and i in (6, 7):
                    base = 9 + 4 * (i - 6)
                    jobs.append(dict(pair=i, segs=[(0, 4, 13, base)], bias=("flag", 1)))
            return jobs

        jobs = build_jobs()
        first_touch, last_touch = {}, {}
        for ji, jb in enumerate(jobs):
            for (a, b, mrow0, slot0) in jb["segs"]:
                for r in range(mrow0, mrow0 + (b - a)):
                    g = r // 4
                    first_touch.setdefault(g, ji)
                    last_touch[g] = ji

        def bt_load(buf, h):
            t = BT[buf]
            key = f"BT{buf}"
            subs = [f"BT{buf}_{q}" for q in range(8)]
            MEMSET("pool", t[:, :, :], NEG, [key] + subs)

            def g2src(j0, nj):
                return bass.AP(g2_t, ((h * 15 + j0) * 64) * 128 + 63, [[127, 64], [64 * 128, nj], [1, 64]])
            lst = [(0, 0, 4, 8), (1, 1, 4, 8)]
            if st == 0:
                lst += [(0, 9 + 8, 0, 4), (0, 9 + 4, 2, 2), (1, 9 + 4, 1, 3), (1, 9 + 0, 3, 1)]
            else:
                lst += [(1, 9 + 1, 12, 3), (0, 9 + 4 + 2, 12, 2), (1, 9 + 4 + 3, 12, 1)]
            for q, (half, s0, j0, n) in enumerate(lst):
                DMA("sp", t[64 * half:64 * half + 64, s0:s0 + n, :], g2src(j0, n), [], [subs[q]], key)
            cm = colmask.unsqueeze(1).to_broadcast([128, 21, 64])
            DVE_TT(t[:, :, :], t[:, :, :], cm, ALU.add, [key, "colmask"] + subs, [key], eng="pool")

        def na_hp(hg, hp):
            hls = [hp * 2, hp * 2 + 1]
            k = hg * 4 + hp
            if k + 1 < 8:
                for eh in range(2):
                    bt_load(2 * ((k + 1) % 2) + eh, (k + 1) * 2 + eh)

            def qz_build(kk, hpp):
                for e in range(2):
                    PHe = slice(64 * e, 64 * e + 64)
                    s.op("pool", lambda en, e=e, PHe=PHe: en.tensor_copy(out=qz[kk % 2][e][PHe, :], in_=qT4[PHe, hpp, :]),
                         reads=["qT4"], writes=[f"qz{kk % 2}_{e}"], small=False)
            if hp == 0:
                qz_build(k, hp)
            if hp + 1 < 4:
                qz_build(k + 1, hp + 1)
            chunk = k
            units = [(ji, eh) for ji in range(len(jobs)) for eh in range(2)]

            def s_stage(ui):
                ji, eh = units[ui]
                jb = jobs[ji]
                i = jb["pair"]
                PH = slice(64 * eh, 64 * eh + 64)
                slot = ui % 4
                pslot = ui % 5
                pbase = 2 * (ui % 2)
                bt = BT[2 * (k % 2) + eh]
                btk = f"BT{2 * (k % 2) + eh}"
                col = 0
                for kk, (a, b, mrow0, bslot0) in enumerate(jb["segs"]):
                    n = (b - a) * 64
                    pb = pbase + kk
                    MM(PS[pb][:, 0:n], kT4[:, hp, i * 128:(i + 1) * 128], qz[k % 2][eh][:, mrow0 * 64: mrow0 * 64 + n],
                       True, True, ["kT4", f"qz{k % 2}_{eh}"], [f"ps{pb}"])
                    DVE_TT(Sb[slot][:, col:col + n], PS[pb][:, 0:n],
                           bt[:, bslot0:bslot0 + (b - a), :].rearrange("p a b -> p (a b)"), ALU.add,
                           [f"ps{pb}", btk], [f"Sb{slot}"])
                    col += n
                if jb["bias"][0] == "kmask":
                    bias_ap = cv(C_KMASK + st * NPAIR + i)
                else:
                    bias_ap = cv(C_FLAG + jb["bias"][1])
                ACT(PT[pslot][:, 0:col], Sb[slot][:, 0:col], AF.Exp, [f"Sb{slot}", "cvec"], [f"PT{pslot}"], bias=bias_ap)

            fresh = set()

            def p_stage(ui):
                ji, eh = units[ui]
                jb = jobs[ji]
                i = jb["pair"]
                hl = hls[eh]
                PH = slice(64 * eh, 64 * eh + 64)
                slot = ui % 5
                if eh == 0:
                    for g, fj in first_touch.items():
                        if fj == ji:
                            fresh.add(g)
                col = 0
                for (a, b, mrow0, bslot0) in jb["segs"]:
                    r = mrow0
                    while r < mrow0 + (b - a):
                        g = r // 4
                        r_end = min(mrow0 + (b - a), 4 * g + 4)
                        n = (r_end - r) * 64
                        pcol = col + (r - mrow0) * 64
                        acc = PS[4 + g % 4]
                        oc = (r - 4 * g) * 64
                        vc = hp * 192 + 64 * eh
                        st_flag = g in fresh
                        fresh.discard(g)
                        MM(acc[:, oc:oc + n], Vtok[:, i, vc:vc + 128], PT[slot][:, pcol:pcol + n],
                           st_flag, False, ["Vtok", f"PT{slot}"], [f"ps{4 + g % 4}"])
                        MM(acc[:, 256 + oc:256 + oc + n], onesz[eh], PT[slot][:, pcol:pcol + n],
                           False, False, [f"onesz{eh}", f"PT{slot}"], [f"ps{4 + g % 4}"])
                        r = r_end
                    col += (b - a) * 64
                if eh == 1:
                    for g, lj in last_touch.items():
                        if lj == ji:
                            nr = min(4, 18 - 4 * g)
                            n = nr * 64
                            acc = PS[4 + g % 4]
                            rb = g % 2
                            DVE_RECIP(recD[rb][:, 0:n], acc[:, 256:256 + n], [f"ps{4 + g % 4}"], [f"recD{rb}"])
                            DVE_TT(yTb[:, chunk, 256 * g:256 * g + n], acc[:, 0:n], recD[rb][:, 0:n], ALU.mult,
                                   [f"ps{4 + g % 4}", f"recD{rb}"], ["yTb"])

            LOOK = 4
            for ui in range(len(units) + LOOK):
                if ui < len(units):
                    s_stage(ui)
                if ui - LOOK >= 0:
                    p_stage(ui - LOOK)

        for eh in range(2):
            bt_load(eh, eh)
        for db_ in range(2):
            for e_ in range(2):
                MEMSET("pool", qz[db_][e_][64 * (1 - e_):64 * (1 - e_) + 64, :], 0.0, [f"qz{db_}_{e_}"])
        for hp_ in range(4):
            MEMSET("pool", Vtok[:, :, hp_ * 192 + 64:hp_ * 192 + 128], 0.0, ["Vtok"])
        for hg in range(2):
            sl = w_next()
            qcons = qk_consume(lambda m, off, w: qT4[:, m, off:off + w], gq8, onesbd, 1.0 / 64, "qT4")
            fm_block(sl, MT, 256, qcons)
            qcons.flush()
            sl = w_next()
            kcons = qk_consume(lambda m, off, w: kT4[:, m, off:off + w], cv(C_GK), onesbd, 1.0 / 64, "kT4")
            fm_block(sl, WT, 0, kcons)
            kcons.flush()
            sl = w_next()
            for i in range(NPAIR):
                pz = ps_next([0, 1, 2, 3])
                for c in range(KC):
                    MM(PS[pz][:, :], nT[:, c, i * 128:(i + 1) * 128], wslot[sl][:, c, :], c == 0, c == KC - 1,
                       [f"w{sl}", "nT"], [f"ps{pz}"])
                vsrc = PS[pz][:, :].rearrange("p (h e d) -> p h e d", h=4, e=2)
                vdst = Vtok[:, i, :].rearrange("p (h x) -> p h x", h=4)
                ACT(vdst[:, :, 0:64], vsrc[:, :, 0, :], AF.Copy, [f"ps{pz}"], ["Vtok"])
                ACT(vdst[:, :, 128:192], vsrc[:, :, 1, :], AF.Copy, [f"ps{pz}"], ["Vtok"])
            for hp in range(4):
                na_hp(hg, hp)
        barrier()
        sl = w_next()
        fm_block(sl, MT, 256, lambda m, off, w, pz: ACT(ch32[:, m, off:off + w], PS[pz][:, 0:w], AF.Copy, [f"ps{pz}"], ["ch32"]))
        sl = w_next()
        fm_block(sl, MT, 256, lambda m, off, w, pz: DVE_TT(ch32[:, m, off:off + w], PS[pz][:, 0:w], ch32[:, m, off:off + w],
                                                            ALU.mult, [f"ps{pz}", "ch32"], ["ch32"]))
        for m in range(4):
            wc = C_CONV + 3 * m
            ACT(cc32[:, m, 1:MID - 1], ch32[:, m, 1:MID - 1], AF.Copy, ["ch32", "cvec"], ["cc32"], scale=cv(wc + 1))
            DVE_STT(cc32[:, m, 1:MID - 1], ch32[:, m, 0:MID - 2], cv(wc + 0), cc32[:, m, 1:MID - 1], ALU.mult, ALU.add,
                    ["ch32", "cc32", "cvec"], ["cc32"])
            DVE_STT(cc32[:, m, 1:MID - 1], ch32[:, m, 2:MID], cv(wc + 2), cc32[:, m, 1:MID - 1], ALU.mult, ALU.add,
                    ["ch32", "cc32", "cvec"], ["cc32"])
        sl = w_next()

        def b_consume(m, off, w, pz):
            lo = max(off, 1)
            hi = min(off + w, MID - 1)
            DVE_TT(yTb[:, 8 + m, lo:hi], PS[pz][:, lo - off:hi - off], cc32[:, m, lo:hi], ALU.mult,
                   [f"ps{pz}", "cc32"], ["yTb"])
        fm_block(sl, MT, 256, b_consume)
        sl = w_next()
        mq = []

        def qm_s1(u):
            m, off, w, pz = mq[u]
            i = u % 2
            ACT(sq16[i][:, 0:w], PS[pz][:, 0:w], AF.Square, [f"ps{pz}"], [f"sq16{i}"])
            MM(PS[3][:, 0:w], ones16, sq16[i][:, 0:w], True, True, ["ones16", f"sq16{i}"], ["ps3"])
            RSTD(rstd[i][:, 0:w], PS[3][:, 0:w], 1.0 / 128, ["ps3"], [f"rstd{i}"])
            DVE_STT(qmT[i][:, 0:w], PS[pz][:, 0:w], gmq, rstd[i][:, 0:w], ALU.mult, ALU.mult,
                    [f"ps{pz}", f"rstd{i}", "gmq"], [f"qmT{i}"])

        def qm_s2(u):
            m, off, w, pz = mq[u]
            i = u % 2
            for mc in range(2):
                pS = 4 + mc
                MM(PS[pS][:, 0:w], mkT[:, m, mc * 128:(mc + 1) * 128], qmT[i][:, 0:w], True, True,
                   ["mkT", f"qmT{i}"], [f"ps{pS}"])
                ACT(PTm[i][:, mc, 0:w], PS[pS][:, 0:w], AF.Exp, [f"ps{pS}"], [f"PTm{i}_{mc}"])

        def qm_s3(u):
            m, off, w, pz = mq[u]
            i = u % 2
            pO, pD = 6, 7
            for mc in range(2):
                MM(PS[pO][:, 0:w], mvtok[:, mc, m * 128:(m + 1) * 128], PTm[i][:, mc, 0:w], mc == 0, mc == 1,
                   ["mvtok", f"PTm{i}_{mc}"], [f"ps{pO}"])
            for mc in range(2):
                MM(PS[pD][:, 0:w], ones16, PTm[i][:, mc, 0:w], mc == 0, mc == 1, ["ones16", f"PTm{i}_{mc}"], [f"ps{pD}"])
            DVE_RECIP(recM[i][:, 0:w], PS[pD][:, 0:w], [f"ps{pD}"], [f"recM{i}"])
            DVE_TT(yTb[:, 12 + m, off:off + w], PS[pO][:, 0:w], recM[i][:, 0:w], ALU.mult,
                   [f"ps{pO}", f"recM{i}"], ["yTb"])

        def qm_consume(m, off, w, pz):
            u = len(mq)
            mq.append((m, off, w, pz))
            qm_s1(u)
            if u >= 1:
                qm_s2(u - 1)
            if u >= 2:
                qm_s3(u - 2)
        fm_block(sl, MT, 256, qm_consume, banks=(0, 1, 2))
        nq = len(mq)
        qm_s2(nq - 1)
        qm_s3(nq - 2)
        qm_s3(nq - 1)
        barrier()
        if DEBUG:
            for c in range(KC):
                t32 = xin[0]
                ACT(t32[:, 0, 0:384], yTb[:, c, 0:384], AF.Copy, ["yTb"], ["xin0"])
                ACT(t32[:, 1, 0:384], yTb[:, c, 384:768], AF.Copy, ["yTb"], ["xin0"])
                ACT(t32[:, 2, 0:384], yTb[:, c, 768:1152], AF.Copy, ["yTb"], ["xin0"])
                DMA("sp", dbg_mix_t.ap()[st, c].rearrange("p (a b) -> p a b", a=3), t32[:, 0:3, 0:384], ["xin0"], [], "dbg")
            barrier()

        for c in range(KC):
            DMA("sp", x1T[:, c, :], bass.AP(xT_t, c * 128 * XW + T0 + 319, [[XW, 128], [1, MIX]]), [], [f"x1T{c}"], f"x1ld{c}")
        XT = [(0, 342), (342, 342), (684, 342)]
        for b in range(4):
            sl = w_next()
            for m in range(4):
                mc = b * 4 + m
                for (off, w) in XT:
                    pz = ps_next([0, 1, 2, 3])
                    for c in range(KC):
                        MM(PS[pz][:, 0:w], wslot[sl][:, c, m * 128:(m + 1) * 128], yTb[:, c, 63 + off:63 + off + w],
                           c == 0, c == KC - 1, [f"w{sl}", "yTb"], [f"ps{pz}"])
                    DVE_TT(x1T[:, mc, off:off + w], PS[pz][:, 0:w], x1T[:, mc, off:off + w], ALU.add,
                           [f"ps{pz}", f"x1T{mc}"], [f"x1T{mc}"])
        barrier()
        if DEBUG:
            for c in range(KC):
                DMA("sp", dbg_x1_t.ap()[st, c], x1T[:, c, :], [f"x1T{c}"], [], "dbg")
            barrier()

        for ti, (off, w) in enumerate(XT):
            pb = ps_next([0, 1, 2])
            for c in range(KC):
                sq = sq32[c % 4]
                ACT(sq[:, 0:w], x1T[:, c, off:off + w], AF.Square, [f"x1T{c}"], [f"sq32{c % 4}"])
                MM(PS[pb][:, 0:w], ones32, sq[:, 0:w], c == 0, c == KC - 1, ["ones32", f"sq32{c % 4}"], [f"ps{pb}"])
            RSTD(rstdD[:, off:off + w], PS[pb][:, 0:w], 1.0 / D, [f"ps{pb}"], ["rstdD"])
        if st == 0:
            DVE_TT(rstdD[:, 0:1], rstdD[:, 0:1], cv(C_FLAG + 2), ALU.mult, ["rstdD", "cvec"], ["rstdD"])
        else:
            DVE_TT(rstdD[:, MIX - 1:MIX], rstdD[:, MIX - 1:MIX], cv(C_FLAG + 3), ALU.mult, ["rstdD", "cvec"], ["rstdD"])
        NP = [(0, 344), (344, 686), (686, MIX)]

        def n2k(c, lo, hi):
            return [f"n2T_{c}_{p}" for p, (a_, b_) in enumerate(NP) if a_ < hi and lo < b_]
        for p, (a_, b_) in enumerate(NP):
            for c in range(KC):
                DVE_STT(n2T[:, c, a_:b_], x1T[:, c, a_:b_], cv(C_GFFN + c), rstdD[:, a_:b_], ALU.mult, ALU.mult,
                        [f"x1T{c}", "rstdD", "cvec"], [f"n2T_{c}_{p}"])
        OT = [(0, 342), (342, 684), (684, 1024)]
        tcount = {"i": 0}
        for gi, grp in enumerate(FF_GROUPS):
            hb = gi % 2
            nblk = len(grp)
            for bi, b in enumerate(grp):
                sl = w_next()
                for jj in range(4):
                    j = b * 4 + jj
                    wc = C_FCONV + 3 * j
                    banks = [0, 1, 2] if (tcount["i"] % 2 == 0) else [3, 4, 5]
                    tcount["i"] += 1
                    for ti, (o0, o1) in enumerate(OT):
                        w = o1 - o0 + 2
                        pz = banks[ti]
                        for c in range(KC):
                            MM(PS[pz][:, 0:w], wslot[sl][:, c, jj * 128:(jj + 1) * 128], n2T[:, c, o0:o0 + w],
                               c == 0, c == KC - 1, [f"w{sl}"] + n2k(c, o0, o0 + w), [f"ps{pz}"])
                        n = o1 - o0
                        t = ctmp[ti]
                        tk = f"ctmp{ti}"
                        ACT(t[:, 0:n], PS[pz][:, 1:1 + n], AF.Copy, [f"ps{pz}", "cvec"], [tk], scale=cv(wc + 1))
                        DVE_STT(t[:, 0:n], PS[pz][:, 0:n], cv(wc + 0), t[:, 0:n], ALU.mult, ALU.add, [f"ps{pz}", tk, "cvec"], [tk])
                        DVE_STT(t[:, 0:n], PS[pz][:, 2:2 + n], cv(wc + 2), t[:, 0:n], ALU.mult, ALU.add, [f"ps{pz}", tk, "cvec"], [tk])
                        ACT(sa32[:, jj, o0:o1], t[:, 0:n], AF.Silu, [tk], [f"sa32_{jj}_{ti}"])
                sl = w_next()
                for jj in range(4):
                    j = D_FF // 128 + b * 4 + jj
                    wc = C_FCONV + 3 * j
                    banks = [0, 1, 2] if (tcount["i"] % 2 == 0) else [3, 4, 5]
                    tcount["i"] += 1
                    for ti, (o0, o1) in enumerate(OT):
                        w = o1 - o0 + 2
                        pz = banks[ti]
                        for c in range(KC):
                            MM(PS[pz][:, 0:w], wslot[sl][:, c, jj * 128:(jj + 1) * 128], n2T[:, c, o0:o0 + w],
                               c == 0, c == KC - 1, [f"w{sl}"] + n2k(c, o0, o0 + w), [f"ps{pz}"])
                        n = o1 - o0
                        t = ctmp[3 + ti]
                        tk = f"ctmp{3 + ti}"
                        ACT(t[:, 0:n], PS[pz][:, 1:1 + n], AF.Copy, [f"ps{pz}", "cvec"], [tk], scale=cv(wc + 1))
                        DVE_STT(t[:, 0:n], PS[pz][:, 0:n], cv(wc + 0), t[:, 0:n], ALU.mult, ALU.add, [f"ps{pz}", tk, "cvec"], [tk])
                        DVE_STT(t[:, 0:n], PS[pz][:, 2:2 + n], cv(wc + 2), t[:, 0:n], ALU.mult, ALU.add, [f"ps{pz}", tk, "cvec"], [tk])
                        DVE_TT(hT[hb][:, bi * 4 + jj, o0:o1], t[:, 0:n], sa32[:, jj, o0:o1], ALU.mult,
                               [tk, f"sa32_{jj}_{ti}"], [f"hT{hb}_{bi * 4 + jj}"], eng="pool")
            nkf = 4 * nblk
            last_grp = (gi == len(FF_GROUPS) - 1)
            for mb in range(4):
                sl = w_next()
                for m in range(4):
                    mc = mb * 4 + m
                    for (o0, o1) in ((0, 512), (512, 1024)):
                        n = o1 - o0
                        pz = ps_next([6, 7])
                        for kf in range(nkf):
                            MM(PS[pz][:, 0:n], wslot_fo[sl][:, kf, m * 128:(m + 1) * 128], hT[hb][:, kf, o0:o1],
                               kf == 0, kf == nkf - 1, [f"w{sl}", f"hT{hb}_{kf}"], [f"ps{pz}"])
                        DVE_TT(x1T[:, mc, 1 + o0:1 + o1], PS[pz][:, 0:n], x1T[:, mc, 1 + o0:1 + o1], ALU.add,
                               [f"ps{pz}", f"x1T{mc}"], [f"x1T{mc}"])
                    if last_grp:
                        DMA("sp", bass.AP(yT_t, mc * 128 * TOK + T0, [[TOK, 128], [1, STT]]), x1T[:, mc, 1:1 + STT],
                            [f"x1T{mc}"], [], f"yout{mc % 4}")

    assert wuse["i"] == len(wq), (wuse["i"], len(wq))
    s.emit()
    return nc, s.stats


_SEQS = [("p", 0, 0), ("p", 0, 32), ("p", 0, 64), ("p", 0, 96), ("s", 0, 0), ("s", 0, 32), ("s", 1, 0), ("s", 1, 32)]


def _prep_core(core, x_prompt, x_sample, mem_prompt, mem_sample, cv_common, nrows):
    kind, b, r0 = _SEQS[core]
    x = x_prompt[b] if kind == "p" else x_sample[b]
    mem = mem_prompt[b] if kind == "p" else mem_sample[b]
    rows = nrows[kind]
    xe = np.zeros((XW, D), np.float32)
    g0 = (r0 - 5) * 64
    lo = max(g0, 0)
    hi = min(g0 + XW, rows * 64)
    xe[lo - g0:hi - g0] = x[lo:hi]
    xT = np.ascontiguousarray(xe.T).reshape(KC, 128, XW)
    memT = np.ascontiguousarray(mem.T).reshape(KC, 128, 256)
    cvv = cv_common.copy()
    for st in range(NST):
        for i in range(NPAIR):
            for half in range(2):
                gr = r0 + 16 * st - 5 + 2 * i + half
                ok = 0 <= gr < rows
                cvv[64 * half:64 * half + 64, C_KMASK + st * NPAIR + i] = 0.0 if ok else NEG
    top = (r0 == 0)
    bot = (r0 + 32 == rows)
    cvv[:, C_FLAG + 0] = 0.0 if top else NEG
    cvv[:, C_FLAG + 1] = 0.0 if bot else NEG
    cvv[:, C_FLAG + 2] = 0.0 if top else 1.0
    cvv[:, C_FLAG + 3] = 0.0 if bot else 1.0
    return xT, memT, cvv


_CACHE = {}


def kernel(x_prompt, x_sample, mem_prompt, mem_sample, g_mix, w_in, na_q_gain, na_k_gain,
           na_rel_bias, conv_w, mem_norm_g, w_mem_kv, mem_q_gain, mem_k_gain, w_out,
           g_ffn, w_ffn_in, ffn_conv_w, w_ffn_out):
    f = lambda a: np.ascontiguousarray(np.asarray(a, dtype=np.float32))
    x_prompt, x_sample, mem_prompt, mem_sample = f(x_prompt), f(x_sample), f(mem_prompt), f(mem_sample)
    cvc = np.zeros((128, NCV), np.float32)
    cvc[:, C_GMIX:C_GMIX + 16] = f(g_mix)[0].reshape(16, 128).T
    cvc[:, C_GFFN:C_GFFN + 16] = f(g_ffn)[0].reshape(16, 128).T
    cvc[:, C_GMEM:C_GMEM + 16] = f(mem_norm_g)[0].reshape(16, 128).T
    cvc[:, C_GQ] = np.tile(f(na_q_gain)[0], 2)
    cvc[:, C_GK] = np.tile(f(na_k_gain)[0], 2)
    cvc[:, C_GMQ] = f(mem_q_gain)[0]
    cvc[:, C_GMK] = f(mem_k_gain)[0]
    cw = f(conv_w)[0]
    cvc[:, C_CONV:C_CONV + 12] = cw.reshape(3, 4, 128).transpose(2, 1, 0).reshape(128, 12)
    fw = f(ffn_conv_w)[0]
    cvc[:, C_FCONV:C_FCONV + 264] = fw.reshape(3, 88, 128).transpose(2, 1, 0).reshape(128, 264)
    qc = np.arange(64)
    q_cs = np.clip(qc - 8, 0, 48)
    kc = np.arange(64)[:, None]
    valid = (kc >= q_cs[None, :]) & (kc < q_cs[None, :] + 16)
    cm = np.where(valid, 0.0, NEG).astype(np.float32)
    colmask = np.concatenate([cm, cm], axis=0)
    rb = f(na_rel_bias)[0]
    G = np.zeros((16, 15, 128), np.float32)
    G[:, :, 48:79] = rb[:, ::-1, ::-1]
    g2 = np.ascontiguousarray(np.broadcast_to(G[:, :, None, :], (16, 15, 64, 128))).reshape(16 * 15 * 64, 128)

    nrows = {"p": x_prompt.shape[1] // 64, "s": x_sample.shape[1] // 64}
    if "nc" not in _CACHE:
        _CACHE["nc"] = build_program()
    nc, stats = _CACHE["nc"]
    shared = {"w_in": f(w_in)[0], "w_out": f(w_out)[0], "w_ffn_in": f(w_ffn_in)[0], "w_ffn_out": f(w_ffn_out)[0],
              "w_mem_kv": f(w_mem_kv)[0], "colmask": colmask, "g2": g2}
    in_maps = []
    for core in range(NCORE):
        xT, memT, cvv = _prep_core(core, x_prompt, x_sample, mem_prompt, mem_sample, cvc, nrows)
        d = dict(shared)
        d.update({"xT": xT, "memT": memT, "cvec": cvv})
        in_maps.append(d)
    res = run_bass_kernel_spmd(nc, in_maps, core_ids=list(range(NCORE)))
    _CACHE["res"] = res
    y_prompt = np.zeros_like(x_prompt)
    y_sample = np.zeros_like(x_sample)
    for core in range(NCORE):
        kind, b, r0 = _SEQS[core]
        yT = np.asarray(res.results[core]["yT"]).reshape(D, TOK)
        dst = y_prompt if kind == "p" else y_sample
        dst[b, r0 * 64:r0 * 64 + TOK, :] = yT.T
    return (y_prompt, y_sample)
```

```python
import contextlib
import math
import os

import numpy as np
import concourse.bass as bass
import concourse.mybir as mybir
from concourse.bass_utils import run_bass_kernel_spmd

F32 = mybir.dt.float32
BF16 = mybir.dt.bfloat16
AF = mybir.ActivationFunctionType
ALU = mybir.AluOpType

NEG = -1.0e30
EPS = 1e-6
D = 2048
KC = 16
NCORE = 8
TOK = 2048
NST = 2
STT = 1024
XW = 2688
WIDE = 1664
MID = 1152
MIX = 1026
NPAIR = 13
D_FF = 5632
NCV = 358
C_GMIX, C_GFFN, C_GMEM, C_GQ, C_GK, C_GMQ, C_GMK, C_CONV, C_FCONV, C_KMASK, C_FLAG = 0, 16, 32, 48, 49, 50, 51, 52, 64, 328, 354

DEBUG = bool(int(os.environ.get("MK_DEBUG", "0")))
SAME_SYNC = {"pe": False, "act": True, "dve": True, "pool": True, "sp": False}
SAME_ALL = bool(int(os.environ.get("MK_SAME_ALL", "1")))


class Sched:
    ENG = ("pe", "act", "dve", "pool", "sp")

    def __init__(self, nc):
        self.nc = nc
        self.ops = []
        self.last_w = {}
        self.readers = {}
        self.slot_cnt = {}
        self.last_on = {}
        self.pending_bar = {}

    def op(self, eng, fn, reads=(), writes=(), dma_slot=None, extra_deps=(), small=True):
        idx = len(self.ops)
        deps = set(extra_deps)
        if eng in self.pending_bar:
            deps.update(self.pending_bar.pop(eng))
        for k in reads:
            w = self.last_w.get(k)
            if w is not None:
                deps.add(w)
        for k in writes:
            w = self.last_w.get(k)
            if w is not None:
                deps.add(w)
            deps.update(self.readers.get(k, {}).values())
        for k in reads:
            rk = self.readers.setdefault(k, {})
            rk[eng if dma_slot is None else (eng, dma_slot)] = idx
        for k in writes:
            self.last_w[k] = idx
            self.readers[k] = {}
        deps.discard(idx)
        o = dict(eng=eng, fn=fn, deps=deps, idx=idx, dma=dma_slot, sig=False, small=small)
        if dma_slot is not None:
            self.slot_cnt[dma_slot] = self.slot_cnt.get(dma_slot, 0) + 16
            o["dma_val"] = self.slot_cnt[dma_slot]
        self.ops.append(o)
        self.last_on[eng] = idx
        return idx

    def barrier(self, dummies):
        marks = []
        for e in ("pe", "act", "dve", "pool"):
            if e in self.last_on:
                marks.append(self.last_on[e])
        dmas = []
        seen = set()
        for o in reversed(self.ops):
            if o["dma"] is not None and o["dma"] not in seen:
                seen.add(o["dma"])
                dmas.append(o["idx"])
        deps = marks + dmas
        d_act, d_dve, d_pool = dummies
        self.op("act", lambda e: e.activation(out=d_act, in_=d_act, func=AF.Copy), reads=["dums"], extra_deps=deps)
        self.op("dve", lambda e: e.memset(d_dve, 0.0), extra_deps=deps)
        self.op("pool", lambda e: e.memset(d_pool, 0.0), extra_deps=deps)
        self.pending_bar = {"pe": list(deps), "sp": list(deps)}

    def emit(self, final_wait_eng="sp"):
        nc = self.nc
        ops = self.ops
        for o in ops:
            for d in o["deps"]:
                do = ops[d]
                if do["dma"] is None:
                    if do["eng"] != o["eng"] or (SAME_SYNC[o["eng"]] and (SAME_ALL or do["small"])) or o["dma"] is not None:
                        do["sig"] = True
        cnt = {e: 0 for e in self.ENG}
        for o in ops:
            if o["dma"] is None and o["sig"]:
                cnt[o["eng"]] += 1
                o["sigval"] = cnt[o["eng"]]
        slots = sorted(self.slot_cnt)
        waited = {e: {} for e in self.ENG}
        nwait = 0
        for o in ops:
            need = {}
            for d in o["deps"]:
                do = ops[d]
                if do["dma"] is not None:
                    key = ("d", do["dma"])
                    val = do["dma_val"]
                else:
                    if do["eng"] == o["eng"] and o["dma"] is None and not (SAME_SYNC[o["eng"]] and (SAME_ALL or do["small"])):
                        continue
                    key = ("e", do["eng"])
                    val = do["sigval"]
                need[key] = max(need.get(key, 0), val)
            w = []
            for key, val in need.items():
                if waited[o["eng"]].get(key, 0) >= val:
                    continue
                waited[o["eng"]][key] = val
                w.append((key, val))
            o["waits"] = w
            nwait += len(w)
        self.stats = dict(nops=len(ops), nwait=nwait, nsig=dict(cnt), nslots=len(slots))
        with contextlib.ExitStack() as st:
            esem = {e: st.enter_context(nc.semaphore("s_" + e)) for e in self.ENG}
            ssem = {s: st.enter_context(nc.semaphore("d_" + str(s))) for s in slots}
            block = st.enter_context(nc.Block())

            def run_engine(ename):
                def body(eng):
                    for o in ops:
                        if o["eng"] != ename:
                            continue
                        waits = [(ssem[key[1]] if key[0] == "d" else esem[key[1]], val) for key, val in o["waits"]]
                        fuse = None
                        if waits and o["dma"] is None:
                            fuse = waits.pop()
                        for sem, val in waits:
                            eng.wait_ge(sem, val)
                        ins = o["fn"](eng)
                        if fuse is not None:
                            ins._wait_ge(fuse[0], fuse[1])
                        if o["dma"] is not None:
                            ins.then_inc(ssem[o["dma"]], 16)
                        elif o["sig"]:
                            ins.then_inc(esem[ename], 1)
                    if ename == final_wait_eng:
                        for s in slots:
                            eng.wait_ge(ssem[s], self.slot_cnt[s])
                return body

            block.tensor(run_engine("pe"))
            block.scalar(run_engine("act"))
            block.vector(run_engine("dve"))
            block.gpsimd(run_engine("pool"))
            block.sync(run_engine("sp"))


def split_rows(lo, hi, m):
    out = []
    while lo < hi:
        n = min(m, hi - lo)
        out.append((lo, lo + n))
        lo += n
    return out


def build_program():
    nc = bass.Bass("TRN2", target_bir_lowering=False)
    xT_t = nc.dram_tensor("xT", [KC, 128, XW], F32, kind="ExternalInput")
    memT_t = nc.dram_tensor("memT", [KC, 128, 256], F32, kind="ExternalInput")
    w_in_t = nc.dram_tensor("w_in", [D, 5120], F32, kind="ExternalInput")
    w_out_t = nc.dram_tensor("w_out", [D, D], F32, kind="ExternalInput")
    w_fi_t = nc.dram_tensor("w_ffn_in", [D, 2 * D_FF], F32, kind="ExternalInput")
    w_fo_t = nc.dram_tensor("w_ffn_out", [D_FF, D], F32, kind="ExternalInput")
    w_mkv_t = nc.dram_tensor("w_mem_kv", [D, 1024], F32, kind="ExternalInput")
    cvec_t = nc.dram_tensor("cvec", [128, NCV], F32, kind="ExternalInput")
    cmask_t = nc.dram_tensor("colmask", [128, 64], F32, kind="ExternalInput")
    g2_t = nc.dram_tensor("g2", [16 * 15 * 64, 128], F32, kind="ExternalInput")
    yT_t = nc.dram_tensor("yT", [KC, 128, TOK], F32, kind="ExternalOutput")
    if DEBUG:
        dbg_mix_t = nc.dram_tensor("dbg_mix", [NST, KC, 128, MID], F32, kind="ExternalOutput")
        dbg_x1_t = nc.dram_tensor("dbg_x1", [NST, KC, 128, MIX], F32, kind="ExternalOutput")

    ARENA_BYTES = 211968
    arena = nc.alloc_sbuf_tensor("arena", [128, ARENA_BYTES // 4], F32)

    def view(off, shape, dt):
        n = int(np.prod(shape))
        sz = 4 if dt == F32 else 2
        assert off % 4 == 0 and (n * sz) % 4 == 0
        assert off + n * sz <= ARENA_BYTES, (off, n * sz)
        ap = arena[:, off // 4: off // 4 + (n * sz) // 4]
        if dt != F32:
            ap = ap.bitcast(dt)
        if len(shape) == 2:
            ap = ap.rearrange("p (a b) -> p a b", a=shape[0])
        elif len(shape) == 3:
            ap = ap.rearrange("p (a b c) -> p a b c", a=shape[0], b=shape[1])
        return ap

    class Alloc:
        def __init__(self, base, limit):
            self.o = base
            self.limit = limit

        def get(self, shape, dt):
            n = int(np.prod(shape)) * (4 if dt == F32 else 2)
            n = (n + 31) // 32 * 32
            v = view(self.o, shape, dt)
            self.o += n
            assert self.o <= self.limit, (self.o, self.limit)
            return v

    ca = Alloc(0, 8192)
    cvec = ca.get([NCV], F32)
    colmask = ca.get([64], F32)
    ones32 = ca.get([128], F32)
    ones16 = ca.get([128], BF16)
    onesbd = ca.get([128], BF16)
    onesz = [ca.get([128], BF16) for _ in range(2)]
    mkT = ca.get([4, 256], BF16)
    mvtok = ca.get([2, 512], BF16)
    gq8 = ca.get([1], F32)
    gmq = ca.get([1], F32)
    dums = ca.get([3, 1], F32)
    W0 = 8192
    wslot = [view(W0 + i * 16384, [16, 512], BF16) for i in range(2)]
    wslot_fo = [view(W0 + i * 16384, [8, 512], BF16) for i in range(2)]
    R1 = W0 + 2 * 16384
    nT = view(R1, [KC, WIDE], BF16)
    x1T = view(R1, [KC, MIX], F32)
    R2 = R1 + 65664
    yTb = view(R2, [KC, MID], BF16)
    n2T = view(R2, [KC, MIX], BF16)
    R3 = R2 + 36864
    R3_END = ARENA_BYTES
    a3 = Alloc(R3, R3_END)
    xin = [a3.get([KC, 416], F32) for _ in range(2)]
    b3 = Alloc(R3, R3_END)
    qT4 = b3.get([4, MID], BF16)
    kT4 = b3.get([4, WIDE], BF16)
    Vtok = b3.get([NPAIR, 768], BF16)
    BT = [b3.get([21, 64], F32) for _ in range(2)]
    BT += [view(R1 + 53248 + q * 5376, [21, 64], F32) for q in range(2)]
    qz = [[view(R2 + 8 * MID * 2 + (2 * db + e) * MID * 2, [MID], BF16) for e in range(2)] for db in range(2)]
    PT = [b3.get([576], BF16) for _ in range(3)]
    PT.append(view(R1 + 53248 + 2 * 5376, [576], BF16))
    Sb = [view(R2 + 12 * MID * 2 + q * 2304, [576], F32) for q in range(4)]
    SCR = R3_END - 11264
    assert b3.o <= SCR, b3.o
    sa_ = Alloc(SCR, R3_END)
    PT.append(sa_.get([576], BF16))
    sqx3 = sa_.get([448], F32)
    sa_.get([160], F32)
    rstd = [sa_.get([448], F32) for _ in range(2)]
    so_ = sa_.o
    sq32x = [sa_.get([448], F32) for _ in range(2)] + [sqx3]
    sb_ = Alloc(so_, R3_END)
    sq16b = [sb_.get([448], BF16) for _ in range(2)]
    recD = [sb_.get([256], F32) for _ in range(2)]
    c3 = Alloc(R3, SCR)
    ch32 = c3.get([4, MID], F32)
    cc32 = c3.get([4, MID], F32)
    qmT = [c3.get([384], BF16) for _ in range(2)]
    PTm = [c3.get([2, 384], BF16) for _ in range(2)]
    recM = [c3.get([384], F32) for _ in range(2)]
    sq16 = [c3.get([448], BF16) for _ in range(2)]
    assert c3.o <= SCR, c3.o
    d3 = Alloc(R3, R3_END)
    hT = [d3.get([8, STT], BF16) for _ in range(2)]
    sa32 = d3.get([4, STT], F32)
    ctmp = [d3.get([344], F32) for _ in range(4)]
    sq32 = [d3.get([344], F32) for _ in range(4)]
    rstdD = d3.get([MIX], F32)
    sdD = d3.get([344], F32)

    PS = [nc.alloc_psum_tensor(f"ps{i}", [128, 512], F32) for i in range(8)]

    s = Sched(nc)
    dummies = (dums[0:1, 0, :], dums[0:1, 1, :], dums[0:1, 2, :])

    def barrier():
        s.barrier(dummies)

    wq = []
    wstate = {"issued": 0}

    def wsrc(t, ncols_total, col0, nk, ncols, row0=0):
        return bass.AP(t, row0 * ncols_total + col0, [[ncols_total, 128], [128 * ncols_total, nk], [1, ncols]])

    def w_issue_upto(n):
        while wstate["issued"] < min(n, len(wq)):
            i = wstate["issued"]
            src, kind = wq[i]
            sl = i % 2
            dst = wslot[sl][:] if kind == "k16" else wslot_fo[sl][:, 0:kind, :]
            s.op("pool", lambda e, dst=dst, src=src: e.dma_start(out=dst, in_=src),
                 writes=[f"w{sl}"], dma_slot=f"w{sl}")
            wstate["issued"] += 1

    wuse = {"i": 0}

    def w_next():
        i = wuse["i"]
        wuse["i"] += 1
        w_issue_upto(i + 2)
        sl = i % 2
        return sl

    wq.append((wsrc(w_mkv_t, 1024, 0, 16, 512), "k16"))
    wq.append((wsrc(w_mkv_t, 1024, 512, 16, 512), "k16"))
    FF_GROUPS = [(0, 1), (2, 3), (4, 5), (6, 7), (8, 9), (10,)]
    for st_ in range(NST):
        for b in (0, 2, 4, 1, 3, 5, 6, 8, 7, 9):
            wq.append((wsrc(w_in_t, 5120, b * 512, 16, 512), "k16"))
        for b in range(4):
            wq.append((wsrc(w_out_t, D, b * 512, 16, 512), "k16"))
        for grp in FF_GROUPS:
            for b in grp:
                wq.append((wsrc(w_fi_t, 2 * D_FF, b * 512, 16, 512), "k16"))
                wq.append((wsrc(w_fi_t, 2 * D_FF, D_FF + b * 512, 16, 512), "k16"))
            for mb in range(4):
                wq.append((wsrc(w_fo_t, D, mb * 512, 4 * len(grp), 512, row0=grp[0] * 512), 4 * len(grp)))

    psrot = {"i": 0}

    def ps_next(banks):
        b = banks[psrot["i"] % len(banks)]
        psrot["i"] += 1
        return b

    def MM(out, lhsT, rhs, start, stop, reads, writes):
        s.op("pe", lambda e: e.matmul(out, lhsT, rhs, start=start, stop=stop, skip_group_check=True),
             reads=reads, writes=writes)

    def ACT(out, in_, func, reads, writes, bias=None, scale=None):
        kw = {}
        if bias is not None:
            kw["bias"] = bias
        if scale is not None:
            kw["scale"] = scale
        s.op("act", lambda e: e.activation(out=out, in_=in_, func=func, **kw), reads=reads, writes=writes, small=out.free_size() < 128)

    def DVE_TT(out, in0, in1, op, reads, writes, eng="dve"):
        s.op(eng, lambda e: e.tensor_tensor(out=out, in0=in0, in1=in1, op=op), reads=reads, writes=writes, small=out.free_size() < 128)

    def DVE_STT(out, in0, scalar, in1, op0, op1, reads, writes, eng="dve"):
        s.op(eng, lambda e: e.scalar_tensor_tensor(out=out, in0=in0, scalar=scalar, in1=in1, op0=op0, op1=op1),
             reads=reads, writes=writes, small=out.free_size() < 128)

    def DVE_TS(out, in0, scalar1, op0, reads, writes, eng="dve"):
        s.op(eng, lambda e: e.tensor_scalar(out=out, in0=in0, scalar1=scalar1, scalar2=None, op0=op0),
             reads=reads, writes=writes, small=out.free_size() < 128)

    def DVE_RECIP(out, in_, reads, writes):
        ACT(out, in_, AF.Ln, reads, writes)
        ACT(out, out, AF.Exp, writes, writes, scale=-1.0)

    def RSTD(out, in_psum, inv_d, reads, writes):
        ACT(out, in_psum, AF.Ln, reads, writes, bias=EPS, scale=inv_d)
        ACT(out, out, AF.Exp, writes, writes, scale=-0.5)

    def MEMSET(eng, ap, val, writes):
        s.op(eng, lambda e: e.memset(ap, val), writes=writes, small=ap.free_size() < 128)

    def DMA(eng, out, in_, reads, writes, slot):
        s.op(eng, lambda e: e.dma_start(out=out, in_=in_), reads=reads, writes=writes, dma_slot=slot)

    def cv(col, n=1):
        return cvec[:, col:col + n]

    DMA("sp", cvec, cvec_t.ap(), [], ["cvec"], "c0")
    DMA("sp", colmask, cmask_t.ap(), [], ["colmask"], "c1")
    MEMSET("dve", dums[:, :, :], 0.0, ["dums"])
    MEMSET("dve", ones32, 1.0, ["ones32"])
    MEMSET("dve", ones16, 1.0, ["ones16"])
    MEMSET("dve", onesbd, 0.0, ["onesbd"])
    MEMSET("dve", onesbd[0:64, 0:64], 1.0, ["onesbd"])
    MEMSET("dve", onesbd[64:128, 64:128], 1.0, ["onesbd"])
    for e_ in range(2):
        MEMSET("dve", onesz[e_], 0.0, [f"onesz{e_}"])
        MEMSET("dve", onesz[e_][:, 64 * e_:64 * e_ + 64], 1.0, [f"onesz{e_}"])
    DVE_TS(gq8, cv(C_GQ), 0.125, ALU.mult, ["cvec"], ["gq8"])
    DVE_TS(gmq, cv(C_GMQ), 1.0 / math.sqrt(128.0), ALU.mult, ["cvec"], ["gmq"])
    w_issue_upto(2)

    def rms_norm_tiles(n_list, gcol, dst_fn, tagdst, xin_loader):
        for ti, (off, w) in enumerate(n_list):
            buf = ti % 2
            xin_loader(buf, off, w)
            pb = ps_next([0, 1])
            for c in range(KC):
                sq = sq32x[c % 3]
                ACT(sq[:, 0:w], xin[buf][:, c, 0:w], AF.Square, [f"xin{buf}"], [f"sqx{c % 3}"])
                MM(PS[pb][:, 0:w], ones32, sq[:, 0:w], c == 0, c == KC - 1, ["ones32", f"sqx{c % 3}"], [f"ps{pb}"])
            RSTD(rstd[buf][:, 0:w], PS[pb][:, 0:w], 1.0 / D, [f"ps{pb}"], [f"rstd{buf}"])
            for c in range(KC):
                DVE_STT(dst_fn(c, off, w), xin[buf][:, c, 0:w], cv(gcol + c), rstd[buf][:, 0:w], ALU.mult, ALU.mult,
                        [f"xin{buf}", f"rstd{buf}", "cvec"], [tagdst])

    def load_mem(buf, off, w):
        DMA("sp", xin[buf][:, :, 0:w], bass.AP(memT_t, off, [[256, 128], [128 * 256, KC], [1, w]]),
            [], [f"xin{buf}"], f"xin{buf}")

    memn = nT
    rms_norm_tiles([(0, 256)], C_GMEM, lambda c, off, w: memn[:, c, off:off + w], "nT", load_mem)
    sl = w_next()
    for hm in range(4):
        pz = ps_next([2, 3])
        for c in range(KC):
            MM(PS[pz][:, 0:256], wslot[sl][:, c, hm * 128:(hm + 1) * 128], memn[:, c, 0:256], c == 0, c == KC - 1,
               [f"w{sl}", "nT"], [f"ps{pz}"])
        ACT(sq16b[0][:, 0:256], PS[pz][:, 0:256], AF.Square, [f"ps{pz}"], ["sq16b0"])
        MM(PS[4][:, 0:256], ones16, sq16b[0][:, 0:256], True, True, ["ones16", "sq16b0"], ["ps4"])
        RSTD(rstd[0][:, 0:256], PS[4][:, 0:256], 1.0 / 128, ["ps4"], ["rstd0"])
        DVE_STT(mkT[:, hm, :], PS[pz][:, 0:256], cv(C_GMK), rstd[0][:, 0:256], ALU.mult, ALU.mult,
                [f"ps{pz}", "rstd0", "cvec"], ["mkT"])
    sl = w_next()
    for mt in range(2):
        pz = ps_next([2, 3])
        for c in range(KC):
            MM(PS[pz][:, :], memn[:, c, mt * 128:(mt + 1) * 128], wslot[sl][:, c, :], c == 0, c == KC - 1,
               [f"w{sl}", "nT"], [f"ps{pz}"])
        ACT(mvtok[:, mt, :], PS[pz][:, :], AF.Copy, [f"ps{pz}"], ["mvtok"])

    for st in range(NST):
        T0 = st * STT
        R0 = st * 16
        barrier()
        def load_x(buf, off, w, T0=T0):
            DMA("sp", xin[buf][:, :, 0:w], bass.AP(xT_t, T0 + off, [[XW, 128], [128 * XW, KC], [1, w]]),
                [], [f"xin{buf}"], f"xin{buf}")

        rms_norm_tiles([(0, 416), (416, 416), (832, 416), (1248, 416)], C_GMIX,
                       lambda c, off, w: nT[:, c, off:off + w], "nT", load_x)
        barrier()

        WT = [(0, 448), (448, 448), (896, 384), (1280, 384)]
        MT = [(0, 384), (384, 384), (768, 384)]

        def fm_block(sl, ntiles, col_off, consume, banks=(0, 1, 2, 3)):
            for m in range(4):
                for (off, w) in ntiles:
                    pz = ps_next(list(banks))
                    for c in range(KC):
                        MM(PS[pz][:, 0:w], wslot[sl][:, c, m * 128:(m + 1) * 128], nT[:, c, col_off + off: col_off + off + w],
                           c == 0, c == KC - 1, [f"w{sl}", "nT"], [f"ps{pz}"])
                    consume(m, off, w, pz)

        qkc = {"i": 0}

        def qk_consume(dst, gain_ap, lhs_ones, inv_d, tagdst):
            pend = []

            def stage_b(m, off, w, pz, i):
                pst = 4 + i
                MM(PS[pst][:, 0:w], lhs_ones, sq16b[i][:, 0:w], True, True, ["ones16", "onesbd", f"sq16b{i}"], [f"ps{pst}"])
                RSTD(rstd[i][:, 0:w], PS[pst][:, 0:w], inv_d, [f"ps{pst}"], [f"rstd{i}"])
                DVE_STT(dst(m, off, w), PS[pz][:, 0:w], gain_ap, rstd[i][:, 0:w], ALU.mult, ALU.mult,
                        [f"ps{pz}", f"rstd{i}", "cvec", "gq8", "gmq"], [tagdst])

            def f(m, off, w, pz):
                i = qkc["i"] % 2
                qkc["i"] += 1
                ACT(sq16b[i][:, 0:w], PS[pz][:, 0:w], AF.Square, [f"ps{pz}"], [f"sq16b{i}"])
                if pend:
                    stage_b(*pend.pop(0))
                pend.append((m, off, w, pz, i))

            def flush():
                while pend:
                    stage_b(*pend.pop(0))
            f.flush = flush
            return f

        def build_jobs():
            jobs = []
            for i in range(NPAIR):
                lo = max(0, 7 - 2 * i)
                hi = min(9, 25 - 2 * i)
                jobs.append(dict(pair=i, segs=[(a, b, 2 * i - 7 + a, a) for (a, b) in split_rows(lo, hi, 8)],
                                 bias=("kmask", i)))
                if st == 0 and i in (4, 5, 6):
                    base = 9 + 4 * (i - 4)
                    jobs.append(dict(pair=i, segs=[(0, 4, 1, base)], bias=("flag", 0)))
                if st == 1 and i in (6, 7):
                    base = 9 + 4 * (i - 6)
                    jobs.append(dict(pair=i, segs=[(0, 4, 13, base)], bias=("flag", 1)))
            return jobs

        jobs = build_jobs()
        first_touch, last_touch = {}, {}
        for ji, jb in enumerate(jobs):
            for (a, b, mrow0, slot0) in jb["segs"]:
                for r in range(mrow0, mrow0 + (b - a)):
                    g = r // 4
                    first_touch.setdefault(g, ji)
                    last_touch[g] = ji

        def bt_load(buf, h):
            t = BT[buf]
            key = f"BT{buf}"
            subs = [f"BT{buf}_{q}" for q in range(8)]
            MEMSET("pool", t[:, :, :], NEG, [key] + subs)

            def g2src(j0, nj):
                return bass.AP(g2_t, ((h * 15 + j0) * 64) * 128 + 63, [[127, 64], [64 * 128, nj], [1, 64]])
            lst = [(0, 0, 4, 8), (1, 1, 4, 8)]
            if st == 0:
                lst += [(0, 9 + 8, 0, 4), (0, 9 + 4, 2, 2), (1, 9 + 4, 1, 3), (1, 9 + 0, 3, 1)]
            else:
                lst += [(1, 9 + 1, 12, 3), (0, 9 + 4 + 2, 12, 2), (1, 9 + 4 + 3, 12, 1)]
            for q, (half, s0, j0, n) in enumerate(lst):
                DMA("sp", t[64 * half:64 * half + 64, s0:s0 + n, :], g2src(j0, n), [], [subs[q]], key)
            cm = colmask.unsqueeze(1).to_broadcast([128, 21, 64])
            DVE_TT(t[:, :, :], t[:, :, :], cm, ALU.add, [key, "colmask"] + subs, [key], eng="pool")

        def na_hp(hg, hp):
            hls = [hp * 2, hp * 2 + 1]
            k = hg * 4 + hp
            if k + 1 < 8:
                for eh in range(2):
                    bt_load(2 * ((k + 1) % 2) + eh, (k + 1) * 2 + eh)

            def qz_build(kk, hpp):
                for e in range(2):
                    PHe = slice(64 * e, 64 * e + 64)
                    s.op("pool", lambda en, e=e, PHe=PHe: en.tensor_copy(out=qz[kk % 2][e][PHe, :], in_=qT4[PHe, hpp, :]),
                         reads=["qT4"], writes=[f"qz{kk % 2}_{e}"], small=False)
            if hp == 0:
                qz_build(k, hp)
            if hp + 1 < 4:
                qz_build(k + 1, hp + 1)
            chunk = k
            units = [(ji, eh) for ji in range(len(jobs)) for eh in range(2)]

            def s_stage(ui):
                ji, eh = units[ui]
                jb = jobs[ji]
                i = jb["pair"]
                PH = slice(64 * eh, 64 * eh + 64)
                slot = ui % 4
                pslot = ui % 5
                pbase = 2 * (ui % 2)
                bt = BT[2 * (k % 2) + eh]
                btk = f"BT{2 * (k % 2) + eh}"
                col = 0
                for kk, (a, b, mrow0, bslot0) in enumerate(jb["segs"]):
                    n = (b - a) * 64
                    pb = pbase + kk
                    MM(PS[pb][:, 0:n], kT4[:, hp, i * 128:(i + 1) * 128], qz[k % 2][eh][:, mrow0 * 64: mrow0 * 64 + n],
                       True, True, ["kT4", f"qz{k % 2}_{eh}"], [f"ps{pb}"])
                    DVE_TT(Sb[slot][:, col:col + n], PS[pb][:, 0:n],
                           bt[:, bslot0:bslot0 + (b - a), :].rearrange("p a b -> p (a b)"), ALU.add,
                           [f"ps{pb}", btk], [f"Sb{slot}"])
                    col += n
                if jb["bias"][0] == "kmask":
                    bias_ap = cv(C_KMASK + st * NPAIR + i)
                else:
                    bias_ap = cv(C_FLAG + jb["bias"][1])
                ACT(PT[pslot][:, 0:col], Sb[slot][:, 0:col], AF.Exp, [f"Sb{slot}", "cvec"], [f"PT{pslot}"], bias=bias_ap)

            fresh = set()

            def p_stage(ui):
                ji, eh = units[ui]
                jb = jobs[ji]
                i = jb["pair"]
                hl = hls[eh]
                PH = slice(64 * eh, 64 * eh + 64)
                slot = ui % 5
                if eh == 0:
                    for g, fj in first_touch.items():
                        if fj == ji:
                            fresh.add(g)
                col = 0
                for (a, b, mrow0, bslot0) in jb["segs"]:
                    r = mrow0
                    while r < mrow0 + (b - a):
                        g = r // 4
                        r_end = min(mrow0 + (b - a), 4 * g + 4)
                        n = (r_end - r) * 64
                        pcol = col + (r - mrow0) * 64
                        acc = PS[4 + g % 4]
                        oc = (r - 4 * g) * 64
                        vc = hp * 192 + 64 * eh
                        st_flag = g in fresh
                        fresh.discard(g)
                        MM(acc[:, oc:oc + n], Vtok[:, i, vc:vc + 128], PT[slot][:, pcol:pcol + n],
                           st_flag, False, ["Vtok", f"PT{slot}"], [f"ps{4 + g % 4}"])
                        MM(acc[:, 256 + oc:256 + oc + n], onesz[eh], PT[slot][:, pcol:pcol + n],
                           False, False, [f"onesz{eh}", f"PT{slot}"], [f"ps{4 + g % 4}"])
                        r = r_end
                    col += (b - a) * 64
                if eh == 1:
                    for g, lj in last_touch.items():
                        if lj == ji:
                            nr = min(4, 18 - 4 * g)
                            n = nr * 64
                            acc = PS[4 + g % 4]
                            rb = g % 2
                            DVE_RECIP(recD[rb][:, 0:n], acc[:, 256:256 + n], [f"ps{4 + g % 4}"], [f"recD{rb}"])
                            DVE_TT(yTb[:, chunk, 256 * g:256 * g + n], acc[:, 0:n], recD[rb][:, 0:n], ALU.mult,
                                   [f"ps{4 + g % 4}", f"recD{rb}"], ["yTb"])

            LOOK = 4
            for ui in range(len(units) + LOOK):
                if ui < len(units):
                    s_stage(ui)
                if ui - LOOK >= 0:
                    p_stage(ui - LOOK)

        for eh in range(2):
            bt_load(eh, eh)
        for db_ in range(2):
            for e_ in range(2):
                MEMSET("pool", qz[db_][e_][64 * (1 - e_):64 * (1 - e_) + 64, :], 0.0, [f"qz{db_}_{e_}"])
        for hp_ in range(4):
            MEMSET("pool", Vtok[:, :, hp_ * 192 + 64:hp_ * 192 + 128], 0.0, ["Vtok"])
        for hg in range(2):
            sl = w_next()
            qcons = qk_consume(lambda m, off, w: qT4[:, m, off:off + w], gq8, onesbd, 1.0 / 64, "qT4")
            fm_block(sl, MT, 256, qcons)
            qcons.flush()
            sl = w_next()
            kcons = qk_consume(lambda m, off, w: kT4[:, m, off:off + w], cv(C_GK), onesbd, 1.0 / 64, "kT4")
            fm_block(sl, WT, 0, kcons)
            kcons.flush()
            sl = w_next()
            for i in range(NPAIR):
                pz = ps_next([0, 1, 2, 3])
                for c in range(KC):
                    MM(PS[pz][:, :], nT[:, c, i * 128:(i + 1) * 128], wslot[sl][:, c, :], c == 0, c == KC - 1,
                       [f"w{sl}", "nT"], [f"ps{pz}"])
                vsrc = PS[pz][:, :].rearrange("p (h e d) -> p h e d", h=4, e=2)
                vdst = Vtok[:, i, :].rearrange("p (h x) -> p h x", h=4)
                ACT(vdst[:, :, 0:64], vsrc[:, :, 0, :], AF.Copy, [f"ps{pz}"], ["Vtok"])
                ACT(vdst[:, :, 128:192], vsrc[:, :, 1, :], AF.Copy, [f"ps{pz}"], ["Vtok"])
            for hp in range(4):
                na_hp(hg, hp)
        barrier()
        sl = w_next()
        fm_block(sl, MT, 256, lambda m, off, w, pz: ACT(ch32[:, m, off:off + w], PS[pz][:, 0:w], AF.Copy, [f"ps{pz}"], ["ch32"]))
        sl = w_next()
        fm_block(sl, MT, 256, lambda m, off, w, pz: DVE_TT(ch32[:, m, off:off + w], PS[pz][:, 0:w], ch32[:, m, off:off + w],
                                                            ALU.mult, [f"ps{pz}", "ch32"], ["ch32"]))
        for m in range(4):
            wc = C_CONV + 3 * m
            ACT(cc32[:, m, 1:MID - 1], ch32[:, m, 1:MID - 1], AF.Copy, ["ch32", "cvec"], ["cc32"], scale=cv(wc + 1))
            DVE_STT(cc32[:, m, 1:MID - 1], ch32[:, m, 0:MID - 2], cv(wc + 0), cc32[:, m, 1:MID - 1], ALU.mult, ALU.add,
                    ["ch32", "cc32", "cvec"], ["cc32"])
            DVE_STT(cc32[:, m, 1:MID - 1], ch32[:, m, 2:MID], cv(wc + 2), cc32[:, m, 1:MID - 1], ALU.mult, ALU.add,
                    ["ch32", "cc32", "cvec"], ["cc32"])
        sl = w_next()

        def b_consume(m, off, w, pz):
            lo = max(off, 1)
            hi = min(off + w, MID - 1)
            DVE_TT(yTb[:, 8 + m, lo:hi], PS[pz][:, lo - off:hi - off], cc32[:, m, lo:hi], ALU.mult,
                   [f"ps{pz}", "cc32"], ["yTb"])
        fm_block(sl, MT, 256, b_consume)
        sl = w_next()
        mq = []

        def qm_s1(u):
            m, off, w, pz = mq[u]
            i = u % 2
            ACT(sq16[i][:, 0:w], PS[pz][:, 0:w], AF.Square, [f"ps{pz}"], [f"sq16{i}"])
            MM(PS[3][:, 0:w], ones16, sq16[i][:, 0:w], True, True, ["ones16", f"sq16{i}"], ["ps3"])
            RSTD(rstd[i][:, 0:w], PS[3][:, 0:w], 1.0 / 128, ["ps3"], [f"rstd{i}"])
            DVE_STT(qmT[i][:, 0:w], PS[pz][:, 0:w], gmq, rstd[i][:, 0:w], ALU.mult, ALU.mult,
                    [f"ps{pz}", f"rstd{i}", "gmq"], [f"qmT{i}"])

        def qm_s2(u):
            m, off, w, pz = mq[u]
            i = u % 2
            for mc in range(2):
                pS = 4 + mc
                MM(PS[pS][:, 0:w], mkT[:, m, mc * 128:(mc + 1) * 128], qmT[i][:, 0:w], True, True,
                   ["mkT", f"qmT{i}"], [f"ps{pS}"])
                ACT(PTm[i][:, mc, 0:w], PS[pS][:, 0:w], AF.Exp, [f"ps{pS}"], [f"PTm{i}_{mc}"])

        def qm_s3(u):
            m, off, w, pz = mq[u]
            i = u % 2
            pO, pD = 6, 7
            for mc in range(2):
                MM(PS[pO][:, 0:w], mvtok[:, mc, m * 128:(m + 1) * 128], PTm[i][:, mc, 0:w], mc == 0, mc == 1,
                   ["mvtok", f"PTm{i}_{mc}"], [f"ps{pO}"])
            for mc in range(2):
                MM(PS[pD][:, 0:w], ones16, PTm[i][:, mc, 0:w], mc == 0, mc == 1, ["ones16", f"PTm{i}_{mc}"], [f"ps{pD}"])
            DVE_RECIP(recM[i][:, 0:w], PS[pD][:, 0:w], [f"ps{pD}"], [f"recM{i}"])
            DVE_TT(yTb[:, 12 + m, off:off + w], PS[pO][:, 0:w], recM[i][:, 0:w], ALU.mult,
                   [f"ps{pO}", f"recM{i}"], ["yTb"])

        def qm_consume(m, off, w, pz):
            u = len(mq)
            mq.append((m, off, w, pz))
            qm_s1(u)
            if u >= 1:
                qm_s2(u - 1)
            if u >= 2:
                qm_s3(u - 2)
        fm_block(sl, MT, 256, qm_consume, banks=(0, 1, 2))
        nq = len(mq)
        qm_s2(nq - 1)
        qm_s3(nq - 2)
        qm_s3(nq - 1)
        barrier()
        if DEBUG:
            for c in range(KC):
                t32 = xin[0]
                ACT(t32[:, 0, 0:384], yTb[:, c, 0:384], AF.Copy, ["yTb"], ["xin0"])
                ACT(t32[:, 1, 0:384], yTb[:, c, 384:768], AF.Copy, ["yTb"], ["xin0"])
                ACT(t32[:, 2, 0:384], yTb[:, c, 768:1152], AF.Copy, ["yTb"], ["xin0"])
                DMA("sp", dbg_mix_t.ap()[st, c].rearrange("p (a b) -> p a b", a=3), t32[:, 0:3, 0:384], ["xin0"], [], "dbg")
            barrier()

        for c in range(KC):
            DMA("sp", x1T[:, c, :], bass.AP(xT_t, c * 128 * XW + T0 + 319, [[XW, 128], [1, MIX]]), [], [f"x1T{c}"], f"x1ld{c}")
        XT = [(0, 342), (342, 342), (684, 342)]
        for b in range(4):
            sl = w_next()
            for m in range(4):
                mc = b * 4 + m
                for (off, w) in XT:
                    pz = ps_next([0, 1, 2, 3])
                    for c in range(KC):
                        MM(PS[pz][:, 0:w], wslot[sl][:, c, m * 128:(m + 1) * 128], yTb[:, c, 63 + off:63 + off + w],
                           c == 0, c == KC - 1, [f"w{sl}", "yTb"], [f"ps{pz}"])
                    DVE_TT(x1T[:, mc, off:off + w], PS[pz][:, 0:w], x1T[:, mc, off:off + w], ALU.add,
                           [f"ps{pz}", f"x1T{mc}"], [f"x1T{mc}"])
        barrier()
        if DEBUG:
            for c in range(KC):
                DMA("sp", dbg_x1_t.ap()[st, c], x1T[:, c, :], [f"x1T{c}"], [], "dbg")
            barrier()

        for ti, (off, w) in enumerate(XT):
            pb = ps_next([0, 1, 2])
            for c in range(KC):
                sq = sq32[c % 4]
                ACT(sq[:, 0:w], x1T[:, c, off:off + w], AF.Square, [f"x1T{c}"], [f"sq32{c % 4}"])
                MM(PS[pb][:, 0:w], ones32, sq[:, 0:w], c == 0, c == KC - 1, ["ones32", f"sq32{c % 4}"], [f"ps{pb}"])
            RSTD(rstdD[:, off:off + w], PS[pb][:, 0:w], 1.0 / D, [f"ps{pb}"], ["rstdD"])
        if st == 0:
            DVE_TT(rstdD[:, 0:1], rstdD[:, 0:1], cv(C_FLAG + 2), ALU.mult, ["rstdD", "cvec"], ["rstdD"])
        else:
            DVE_TT(rstdD[:, MIX - 1:MIX], rstdD[:, MIX - 1:MIX], cv(C_FLAG + 3), ALU.mult, ["rstdD", "cvec"], ["rstdD"])
        NP = [(0, 344), (344, 686), (686, MIX)]

        def n2k(c, lo, hi):
            return [f"n2T_{c}_{p}" for p, (a_, b_) in enumerate(NP) if a_ < hi and lo < b_]
        for p, (a_, b_) in enumerate(NP):
            for c in range(KC):
                DVE_STT(n2T[:, c, a_:b_], x1T[:, c, a_:b_], cv(C_GFFN + c), rstdD[:, a_:b_], ALU.mult, ALU.mult,
                        [f"x1T{c}", "rstdD", "cvec"], [f"n2T_{c}_{p}"])
        OT = [(0, 342), (342, 684), (684, 1024)]
        tcount = {"i": 0}
        for gi, grp in enumerate(FF_GROUPS):
            hb = gi % 2
            nblk = len(grp)
            for bi, b in enumerate(grp):
                sl = w_next()
                for jj in range(4):
                    j = b * 4 + jj
                    wc = C_FCONV + 3 * j
                    banks = [0, 1, 2] if (tcount["i"] % 2 == 0) else [3, 4, 5]
                    tcount["i"] += 1
                    for ti, (o0, o1) in enumerate(OT):
                        w = o1 - o0 + 2
                        pz = banks[ti]
                        for c in range(KC):
                            MM(PS[pz][:, 0:w], wslot[sl][:, c, jj * 128:(jj + 1) * 128], n2T[:, c, o0:o0 + w],
                               c == 0, c == KC - 1, [f"w{sl}"] + n2k(c, o0, o0 + w), [f"ps{pz}"])
                        n = o1 - o0
                        t = ctmp[ti % 2]
                        tk = f"ctmp{ti % 2}"
                        ACT(t[:, 0:n], PS[pz][:, 1:1 + n], AF.Copy, [f"ps{pz}", "cvec"], [tk], scale=cv(wc + 1))
                        DVE_STT(t[:, 0:n], PS[pz][:, 0:n], cv(wc + 0), t[:, 0:n], ALU.mult, ALU.add, [f"ps{pz}", tk, "cvec"], [tk])
                        DVE_STT(t[:, 0:n], PS[pz][:, 2:2 + n], cv(wc + 2), t[:, 0:n], ALU.mult, ALU.add, [f"ps{pz}", tk, "cvec"], [tk])
                        ACT(sa32[:, jj, o0:o1], t[:, 0:n], AF.Silu, [tk], ["sa32"])
                sl = w_next()
                for jj in range(4):
                    j = D_FF // 128 + b * 4 + jj
                    wc = C_FCONV + 3 * j
                    banks = [0, 1, 2] if (tcount["i"] % 2 == 0) else [3, 4, 5]
                    tcount["i"] += 1
                    for ti, (o0, o1) in enumerate(OT):
                        w = o1 - o0 + 2
                        pz = banks[ti]
                        for c in range(KC):
                            MM(PS[pz][:, 0:w], wslot[sl][:, c, jj * 128:(jj + 1) * 128], n2T[:, c, o0:o0 + w],
                               c == 0, c == KC - 1, [f"w{sl}"] + n2k(c, o0, o0 + w), [f"ps{pz}"])
                        n = o1 - o0
                        t = ctmp[2 + ti % 2]
                        tk = f"ctmp{2 + ti % 2}"
                        ACT(t[:, 0:n], PS[pz][:, 1:1 + n], AF.Copy, [f"ps{pz}", "cvec"], [tk], scale=cv(wc + 1))
                        DVE_STT(t[:, 0:n], PS[pz][:, 0:n], cv(wc + 0), t[:, 0:n], ALU.mult, ALU.add, [f"ps{pz}", tk, "cvec"], [tk])
                        DVE_STT(t[:, 0:n], PS[pz][:, 2:2 + n], cv(wc + 2), t[:, 0:n], ALU.mult, ALU.add, [f"ps{pz}", tk, "cvec"], [tk])
                        DVE_TT(hT[hb][:, bi * 4 + jj, o0:o1], t[:, 0:n], sa32[:, jj, o0:o1], ALU.mult,
                               [tk, "sa32"], [f"hT{hb}_{bi * 4 + jj}"], eng="pool")
            nkf = 4 * nblk
            last_grp = (gi == len(FF_GROUPS) - 1)
            for mb in range(4):
                sl = w_next()
                for m in range(4):
                    mc = mb * 4 + m
                    for (o0, o1) in ((0, 512), (512, 1024)):
                        n = o1 - o0
                        pz = ps_next([6, 7])
                        for kf in range(nkf):
                            MM(PS[pz][:, 0:n], wslot_fo[sl][:, kf, m * 128:(m + 1) * 128], hT[hb][:, kf, o0:o1],
                               kf == 0, kf == nkf - 1, [f"w{sl}", f"hT{hb}_{kf}"], [f"ps{pz}"])
                        DVE_TT(x1T[:, mc, 1 + o0:1 + o1], PS[pz][:, 0:n], x1T[:, mc, 1 + o0:1 + o1], ALU.add,
                               [f"ps{pz}", f"x1T{mc}"], [f"x1T{mc}"])
                    if last_grp:
                        DMA("sp", bass.AP(yT_t, mc * 128 * TOK + T0, [[TOK, 128], [1, STT]]), x1T[:, mc, 1:1 + STT],
                            [f"x1T{mc}"], [], f"yout{mc % 4}")

    assert wuse["i"] == len(wq), (wuse["i"], len(wq))
    s.emit()
    return nc, s.stats


_SEQS = [("p", 0, 0), ("p", 0, 32), ("p", 0, 64), ("p", 0, 96), ("s", 0, 0), ("s", 0, 32), ("s", 1, 0), ("s", 1, 32)]


def _prep_core(core, x_prompt, x_sample, mem_prompt, mem_sample, cv_common, nrows):
    kind, b, r0 = _SEQS[core]
    x = x_prompt[b] if kind == "p" else x_sample[b]
    mem = mem_prompt[b] if kind == "p" else mem_sample[b]
    rows = nrows[kind]
    xe = np.zeros((XW, D), np.float32)
    g0 = (r0 - 5) * 64
    lo = max(g0, 0)
    hi = min(g0 + XW, rows * 64)
    xe[lo - g0:hi - g0] = x[lo:hi]
    xT = np.ascontiguousarray(xe.T).reshape(KC, 128, XW)
    memT = np.ascontiguousarray(mem.T).reshape(KC, 128, 256)
    cvv = cv_common.copy()
    for st in range(NST):
        for i in range(NPAIR):
            for half in range(2):
                gr = r0 + 16 * st - 5 + 2 * i + half
                ok = 0 <= gr < rows
                cvv[64 * half:64 * half + 64, C_KMASK + st * NPAIR + i] = 0.0 if ok else NEG
    top = (r0 == 0)
    bot = (r0 + 32 == rows)
    cvv[:, C_FLAG + 0] = 0.0 if top else NEG
    cvv[:, C_FLAG + 1] = 0.0 if bot else NEG
    cvv[:, C_FLAG + 2] = 0.0 if top else 1.0
    cvv[:, C_FLAG + 3] = 0.0 if bot else 1.0
    return xT, memT, cvv


_CACHE = {}


def kernel(x_prompt, x_sample, mem_prompt, mem_sample, g_mix, w_in, na_q_gain, na_k_gain,
           na_rel_bias, conv_w, mem_norm_g, w_mem_kv, mem_q_gain, mem_k_gain, w_out,
           g_ffn, w_ffn_in, ffn_conv_w, w_ffn_out):
    f = lambda a: np.ascontiguousarray(np.asarray(a, dtype=np.float32))
    x_prompt, x_sample, mem_prompt, mem_sample = f(x_prompt), f(x_sample), f(mem_prompt), f(mem_sample)
    cvc = np.zeros((128, NCV), np.float32)
    cvc[:, C_GMIX:C_GMIX + 16] = f(g_mix)[0].reshape(16, 128).T
    cvc[:, C_GFFN:C_GFFN + 16] = f(g_ffn)[0].reshape(16, 128).T
    cvc[:, C_GMEM:C_GMEM + 16] = f(mem_norm_g)[0].reshape(16, 128).T
    cvc[:, C_GQ] = np.tile(f(na_q_gain)[0], 2)
    cvc[:, C_GK] = np.tile(f(na_k_gain)[0], 2)
    cvc[:, C_GMQ] = f(mem_q_gain)[0]
    cvc[:, C_GMK] = f(mem_k_gain)[0]
    cw = f(conv_w)[0]
    cvc[:, C_CONV:C_CONV + 12] = cw.reshape(3, 4, 128).transpose(2, 1, 0).reshape(128, 12)
    fw = f(ffn_conv_w)[0]
    cvc[:, C_FCONV:C_FCONV + 264] = fw.reshape(3, 88, 128).transpose(2, 1, 0).reshape(128, 264)
    qc = np.arange(64)
    q_cs = np.clip(qc - 8, 0, 48)
    kc = np.arange(64)[:, None]
    valid = (kc >= q_cs[None, :]) & (kc < q_cs[None, :] + 16)
    cm = np.where(valid, 0.0, NEG).astype(np.float32)
    colmask = np.concatenate([cm, cm], axis=0)
    rb = f(na_rel_bias)[0]
    G = np.zeros((16, 15, 128), np.float32)
    G[:, :, 48:79] = rb[:, ::-1, ::-1]
    g2 = np.ascontiguousarray(np.broadcast_to(G[:, :, None, :], (16, 15, 64, 128))).reshape(16 * 15 * 64, 128)

    nrows = {"p": x_prompt.shape[1] // 64, "s": x_sample.shape[1] // 64}
    if "nc" not in _CACHE:
        _CACHE["nc"] = build_program()
    nc, stats = _CACHE["nc"]
    shared = {"w_in": f(w_in)[0], "w_out": f(w_out)[0], "w_ffn_in": f(w_ffn_in)[0], "w_ffn_out": f(w_ffn_out)[0],
              "w_mem_kv": f(w_mem_kv)[0], "colmask": colmask, "g2": g2}
    in_maps = []
    for core in range(NCORE):
        xT, memT, cvv = _prep_core(core, x_prompt, x_sample, mem_prompt, mem_sample, cvc, nrows)
        d = dict(shared)
        d.update({"xT": xT, "memT": memT, "cvec": cvv})
        in_maps.append(d)
    res = run_bass_kernel_spmd(nc, in_maps, core_ids=list(range(NCORE)))
    _CACHE["res"] = res
    y_prompt = np.zeros_like(x_prompt)
    y_sample = np.zeros_like(x_sample)
    for core in range(NCORE):
        kind, b, r0 = _SEQS[core]
        yT = np.asarray(res.results[core]["yT"]).reshape(D, TOK)
        dst = y_prompt if kind == "p" else y_sample
        dst[b, r0 * 64:r0 * 64 + TOK, :] = yT.T
    return (y_prompt, y_sample)
```

```python
import contextlib
import math
import os

import numpy as np
import concourse.bass as bass
import concourse.mybir as mybir
from concourse.bass_utils import run_bass_kernel_spmd

F32 = mybir.dt.float32
BF16 = mybir.dt.bfloat16
AF = mybir.ActivationFunctionType
ALU = mybir.AluOpType

NEG = -1.0e30
EPS = 1e-6
D = 2048
KC = 16
NCORE = 8
TOK = 2048
NST = 2
STT = 1024
XW = 2688
WIDE = 1664
MID = 1152
MIX = 1026
NPAIR = 13
D_FF = 5632
NCV = 358
C_GMIX, C_GFFN, C_GMEM, C_GQ, C_GK, C_GMQ, C_GMK, C_CONV, C_FCONV, C_KMASK, C_FLAG = 0, 16, 32, 48, 49, 50, 51, 52, 64, 328, 354

DEBUG = bool(int(os.environ.get("MK_DEBUG", "0")))
SAME_SYNC = {"pe": False, "act": True, "dve": True, "pool": True, "sp": False}
SAME_ALL = bool(int(os.environ.get("MK_SAME_ALL", "1")))


class Sched:
    ENG = ("pe", "act", "dve", "pool", "sp")

    def __init__(self, nc):
        self.nc = nc
        self.ops = []
        self.last_w = {}
        self.readers = {}
        self.slot_cnt = {}
        self.last_on = {}
        self.pending_bar = {}

    def op(self, eng, fn, reads=(), writes=(), dma_slot=None, extra_deps=(), small=True):
        idx = len(self.ops)
        deps = set(extra_deps)
        if eng in self.pending_bar:
            deps.update(self.pending_bar.pop(eng))
        for k in reads:
            w = self.last_w.get(k)
            if w is not None:
                deps.add(w)
        for k in writes:
            w = self.last_w.get(k)
            if w is not None:
                deps.add(w)
            deps.update(self.readers.get(k, {}).values())
        for k in reads:
            rk = self.readers.setdefault(k, {})
            rk[eng if dma_slot is None else (eng, dma_slot)] = idx
        for k in writes:
            self.last_w[k] = idx
            self.readers[k] = {}
        deps.discard(idx)
        o = dict(eng=eng, fn=fn, deps=deps, idx=idx, dma=dma_slot, sig=False, small=small)
        if dma_slot is not None:
            self.slot_cnt[dma_slot] = self.slot_cnt.get(dma_slot, 0) + 16
            o["dma_val"] = self.slot_cnt[dma_slot]
        self.ops.append(o)
        self.last_on[eng] = idx
        return idx

    def barrier(self, dummies):
        marks = []
        for e in ("pe", "act", "dve", "pool"):
            if e in self.last_on:
                marks.append(self.last_on[e])
        dmas = []
        seen = set()
        for o in reversed(self.ops):
            if o["dma"] is not None and o["dma"] not in seen:
                seen.add(o["dma"])
                dmas.append(o["idx"])
        deps = marks + dmas
        d_act, d_dve, d_pool = dummies
        self.op("act", lambda e: e.activation(out=d_act, in_=d_act, func=AF.Copy), reads=["dums"], extra_deps=deps)
        self.op("dve", lambda e: e.memset(d_dve, 0.0), extra_deps=deps)
        self.op("pool", lambda e: e.memset(d_pool, 0.0), extra_deps=deps)
        self.pending_bar = {"pe": list(deps), "sp": list(deps)}

    def emit(self, final_wait_eng="sp"):
        nc = self.nc
        ops = self.ops
        for o in ops:
            for d in o["deps"]:
                do = ops[d]
                if do["dma"] is None:
                    if do["eng"] != o["eng"] or (SAME_SYNC[o["eng"]] and (SAME_ALL or do["small"])) or o["dma"] is not None:
                        do["sig"] = True
        cnt = {e: 0 for e in self.ENG}
        for o in ops:
            if o["dma"] is None and o["sig"]:
                cnt[o["eng"]] += 1
                o["sigval"] = cnt[o["eng"]]
        slots = sorted(self.slot_cnt)
        engK = {e: {} for e in self.ENG}
        opK = {}
        nwait = 0
        for o in ops:
            need = {}
            for d in o["deps"]:
                do = ops[d]
                if do["dma"] is not None:
                    key = ("d", do["dma"])
                    val = do["dma_val"]
                else:
                    if do["eng"] == o["eng"] and o["dma"] is None and not (SAME_SYNC[o["eng"]] and (SAME_ALL or do["small"])):
                        continue
                    key = ("e", do["eng"])
                    val = do["sigval"]
                if key not in need or need[key][0] < val:
                    need[key] = (val, d)
            K = engK[o["eng"]]
            w = []
            for key, (val, d) in sorted(need.items(), key=lambda kv: -kv[1][1]):
                if K.get(key, 0) >= val:
                    continue
                w.append((key, val))
                for k2, v2 in opK[d].items():
                    if K.get(k2, 0) < v2:
                        K[k2] = v2
            o["waits"] = w
            nwait += len(w)
            mine = dict(K)
            if o["dma"] is not None:
                mine[("d", o["dma"])] = max(mine.get(("d", o["dma"]), 0), o["dma_val"])
            elif o["sig"]:
                mine[("e", o["eng"])] = max(mine.get(("e", o["eng"]), 0), o["sigval"])
            opK[o["idx"]] = mine
        self.stats = dict(nops=len(ops), nwait=nwait, nsig=dict(cnt), nslots=len(slots))
        with contextlib.ExitStack() as st:
            esem = {e: st.enter_context(nc.semaphore("s_" + e)) for e in self.ENG}
            ssem = {s: st.enter_context(nc.semaphore("d_" + str(s))) for s in slots}
            block = st.enter_context(nc.Block())

            def run_engine(ename):
                def body(eng):
                    for o in ops:
                        if o["eng"] != ename:
                            continue
                        waits = [(ssem[key[1]] if key[0] == "d" else esem[key[1]], val) for key, val in o["waits"]]
                        fuse = None
                        if waits and o["dma"] is None:
                            fuse = waits.pop()
                        for sem, val in waits:
                            eng.wait_ge(sem, val)
                        ins = o["fn"](eng)
                        if fuse is not None:
                            ins._wait_ge(fuse[0], fuse[1])
                        if o["dma"] is not None:
                            ins.then_inc(ssem[o["dma"]], 16)
                        elif o["sig"]:
                            ins.then_inc(esem[ename], 1)
                    if ename == final_wait_eng:
                        for s in slots:
                            eng.wait_ge(ssem[s], self.slot_cnt[s])
                return body

            block.tensor(run_engine("pe"))
            block.scalar(run_engine("act"))
            block.vector(run_engine("dve"))
            block.gpsimd(run_engine("pool"))
            block.sync(run_engine("sp"))


def split_rows(lo, hi, m):
    out = []
    while lo < hi:
        n = min(m, hi - lo)
        out.append((lo, lo + n))
        lo += n
    return out


def build_program():
    nc = bass.Bass("TRN2", target_bir_lowering=False)
    xT_t = nc.dram_tensor("xT", [KC, 128, XW], F32, kind="ExternalInput")
    memT_t = nc.dram_tensor("memT", [KC, 128, 256], F32, kind="ExternalInput")
    w_in_t = nc.dram_tensor("w_in", [D, 5120], F32, kind="ExternalInput")
    w_out_t = nc.dram_tensor("w_out", [D, D], F32, kind="ExternalInput")
    w_fi_t = nc.dram_tensor("w_ffn_in", [D, 2 * D_FF], F32, kind="ExternalInput")
    w_fo_t = nc.dram_tensor("w_ffn_out", [D_FF, D], F32, kind="ExternalInput")
    w_mkv_t = nc.dram_tensor("w_mem_kv", [D, 1024], F32, kind="ExternalInput")
    cvec_t = nc.dram_tensor("cvec", [128, NCV], F32, kind="ExternalInput")
    cmask_t = nc.dram_tensor("colmask", [128, 64], F32, kind="ExternalInput")
    g2_t = nc.dram_tensor("g2", [16 * 15 * 64, 128], F32, kind="ExternalInput")
    yT_t = nc.dram_tensor("yT", [KC, 128, TOK], F32, kind="ExternalOutput")
    if DEBUG:
        dbg_mix_t = nc.dram_tensor("dbg_mix", [NST, KC, 128, MID], F32, kind="ExternalOutput")
        dbg_x1_t = nc.dram_tensor("dbg_x1", [NST, KC, 128, MIX], F32, kind="ExternalOutput")

    ARENA_BYTES = 211968
    arena = nc.alloc_sbuf_tensor("arena", [128, ARENA_BYTES // 4], F32)

    def view(off, shape, dt):
        n = int(np.prod(shape))
        sz = 4 if dt == F32 else 2
        assert off % 4 == 0 and (n * sz) % 4 == 0
        assert off + n * sz <= ARENA_BYTES, (off, n * sz)
        ap = arena[:, off // 4: off // 4 + (n * sz) // 4]
        if dt != F32:
            ap = ap.bitcast(dt)
        if len(shape) == 2:
            ap = ap.rearrange("p (a b) -> p a b", a=shape[0])
        elif len(shape) == 3:
            ap = ap.rearrange("p (a b c) -> p a b c", a=shape[0], b=shape[1])
        return ap

    class Alloc:
        def __init__(self, base, limit):
            self.o = base
            self.limit = limit

        def get(self, shape, dt):
            n = int(np.prod(shape)) * (4 if dt == F32 else 2)
            n = (n + 31) // 32 * 32
            v = view(self.o, shape, dt)
            self.o += n
            assert self.o <= self.limit, (self.o, self.limit)
            return v

    ca = Alloc(0, 8192)
    cvec = ca.get([NCV], F32)
    colmask = ca.get([64], F32)
    ones32 = ca.get([128], F32)
    ones16 = ca.get([128], BF16)
    onesbd = ca.get([128], BF16)
    onesz = [ca.get([128], BF16) for _ in range(2)]
    mkT = ca.get([4, 256], BF16)
    mvtok = ca.get([2, 512], BF16)
    gq8 = ca.get([1], F32)
    gmq = ca.get([1], F32)
    dums = ca.get([3, 1], F32)
    W0 = 8192
    wslot = [view(W0 + i * 16384, [16, 512], BF16) for i in range(2)]
    wslot_fo = [view(W0 + i * 16384, [8, 512], BF16) for i in range(2)]
    R1 = W0 + 2 * 16384
    nT = view(R1, [KC, WIDE], BF16)
    x1T = view(R1, [KC, MIX], F32)
    R2 = R1 + 65664
    yTb = view(R2, [KC, MID], BF16)
    n2T = view(R2, [KC, MIX], BF16)
    R3 = R2 + 36864
    R3_END = ARENA_BYTES
    a3 = Alloc(R3, R3_END)
    xin = [a3.get([KC, 416], F32) for _ in range(2)]
    b3 = Alloc(R3, R3_END)
    qT4 = b3.get([4, MID], BF16)
    kT4 = b3.get([4, WIDE], BF16)
    Vtok = b3.get([NPAIR, 768], BF16)
    BT = [b3.get([21, 64], F32) for _ in range(2)]
    BT += [view(R1 + 53248 + q * 5376, [21, 64], F32) for q in range(2)]
    qz = [[view(R2 + 8 * MID * 2 + (2 * db + e) * MID * 2, [MID], BF16) for e in range(2)] for db in range(2)]
    PT = [b3.get([576], BF16) for _ in range(3)]
    PT.append(view(R1 + 53248 + 2 * 5376, [576], BF16))
    Sb = [view(R2 + 12 * MID * 2 + q * 2304, [576], F32) for q in range(4)]
    SCR = R3_END - 11264
    assert b3.o <= SCR, b3.o
    sa_ = Alloc(SCR, R3_END)
    PT.append(sa_.get([576], BF16))
    sqx3 = sa_.get([448], F32)
    sa_.get([160], F32)
    rstd = [sa_.get([448], F32) for _ in range(2)]
    so_ = sa_.o
    sq32x = [sa_.get([448], F32) for _ in range(2)] + [sqx3]
    sb_ = Alloc(so_, R3_END)
    sq16b = [sb_.get([448], BF16) for _ in range(2)]
    recD = [sb_.get([256], F32) for _ in range(2)]
    c3 = Alloc(R3, SCR)
    ch32 = c3.get([4, MID], F32)
    cc32 = c3.get([4, MID], F32)
    qmT = [c3.get([384], BF16) for _ in range(2)]
    PTm = [c3.get([2, 384], BF16) for _ in range(2)]
    recM = [c3.get([384], F32) for _ in range(2)]
    sq16 = [c3.get([448], BF16) for _ in range(2)]
    assert c3.o <= SCR, c3.o
    d3 = Alloc(R3, R3_END)
    hT = [d3.get([8, STT], BF16) for _ in range(2)]
    sa32 = d3.get([4, STT], F32)
    ctmp = [d3.get([344], F32) for _ in range(4)]
    sq32 = [d3.get([344], F32) for _ in range(4)]
    rstdD = d3.get([MIX], F32)
    sdD = d3.get([344], F32)

    PS = [nc.alloc_psum_tensor(f"ps{i}", [128, 512], F32) for i in range(8)]

    s = Sched(nc)
    dummies = (dums[0:1, 0, :], dums[0:1, 1, :], dums[0:1, 2, :])

    def barrier():
        s.barrier(dummies)

    wq = []
    wstate = {"issued": 0}

    def wsrc(t, ncols_total, col0, nk, ncols, row0=0):
        return bass.AP(t, row0 * ncols_total + col0, [[ncols_total, 128], [128 * ncols_total, nk], [1, ncols]])

    def w_issue_upto(n):
        while wstate["issued"] < min(n, len(wq)):
            i = wstate["issued"]
            src, kind = wq[i]
            sl = i % 2
            dst = wslot[sl][:] if kind == "k16" else wslot_fo[sl][:, 0:kind, :]
            s.op("pool", lambda e, dst=dst, src=src: e.dma_start(out=dst, in_=src),
                 writes=[f"w{sl}"], dma_slot=f"w{sl}")
            wstate["issued"] += 1

    wuse = {"i": 0}

    def w_next():
        i = wuse["i"]
        wuse["i"] += 1
        w_issue_upto(i + 2)
        sl = i % 2
        return sl

    wq.append((wsrc(w_mkv_t, 1024, 0, 16, 512), "k16"))
    wq.append((wsrc(w_mkv_t, 1024, 512, 16, 512), "k16"))
    FF_GROUPS = [(0, 1), (2, 3), (4, 5), (6, 7), (8, 9), (10,)]
    for st_ in range(NST):
        for b in (0, 2, 4, 1, 3, 5, 6, 8, 7, 9):
            wq.append((wsrc(w_in_t, 5120, b * 512, 16, 512), "k16"))
        for b in range(4):
            wq.append((wsrc(w_out_t, D, b * 512, 16, 512), "k16"))
        for grp in FF_GROUPS:
            for b in grp:
                wq.append((wsrc(w_fi_t, 2 * D_FF, b * 512, 16, 512), "k16"))
                wq.append((wsrc(w_fi_t, 2 * D_FF, D_FF + b * 512, 16, 512), "k16"))
            for mb in range(4):
                wq.append((wsrc(w_fo_t, D, mb * 512, 4 * len(grp), 512, row0=grp[0] * 512), 4 * len(grp)))

    psrot = {"i": 0}

    def ps_next(banks):
        b = banks[psrot["i"] % len(banks)]
        psrot["i"] += 1
        return b

    def MM(out, lhsT, rhs, start, stop, reads, writes):
        s.op("pe", lambda e: e.matmul(out, lhsT, rhs, start=start, stop=stop, skip_group_check=True),
             reads=reads, writes=writes)

    def ACT(out, in_, func, reads, writes, bias=None, scale=None):
        kw = {}
        if bias is not None:
            kw["bias"] = bias
        if scale is not None:
            kw["scale"] = scale
        s.op("act", lambda e: e.activation(out=out, in_=in_, func=func, **kw), reads=reads, writes=writes, small=out.free_size() < 128)

    def DVE_TT(out, in0, in1, op, reads, writes, eng="dve"):
        s.op(eng, lambda e: e.tensor_tensor(out=out, in0=in0, in1=in1, op=op), reads=reads, writes=writes, small=out.free_size() < 128)

    def DVE_STT(out, in0, scalar, in1, op0, op1, reads, writes, eng="dve"):
        s.op(eng, lambda e: e.scalar_tensor_tensor(out=out, in0=in0, scalar=scalar, in1=in1, op0=op0, op1=op1),
             reads=reads, writes=writes, small=out.free_size() < 128)

    def DVE_TS(out, in0, scalar1, op0, reads, writes, eng="dve"):
        s.op(eng, lambda e: e.tensor_scalar(out=out, in0=in0, scalar1=scalar1, scalar2=None, op0=op0),
             reads=reads, writes=writes, small=out.free_size() < 128)

    def DVE_RECIP(out, in_, reads, writes):
        ACT(out, in_, AF.Ln, reads, writes)
        ACT(out, out, AF.Exp, writes, writes, scale=-1.0)

    def RSTD(out, in_psum, inv_d, reads, writes):
        ACT(out, in_psum, AF.Ln, reads, writes, bias=EPS, scale=inv_d)
        ACT(out, out, AF.Exp, writes, writes, scale=-0.5)

    def MEMSET(eng, ap, val, writes):
        s.op(eng, lambda e: e.memset(ap, val), writes=writes, small=ap.free_size() < 128)

    def DMA(eng, out, in_, reads, writes, slot):
        s.op(eng, lambda e: e.dma_start(out=out, in_=in_), reads=reads, writes=writes, dma_slot=slot)

    def cv(col, n=1):
        return cvec[:, col:col + n]

    DMA("sp", cvec, cvec_t.ap(), [], ["cvec"], "c0")
    DMA("sp", colmask, cmask_t.ap(), [], ["colmask"], "c1")
    MEMSET("dve", dums[:, :, :], 0.0, ["dums"])
    MEMSET("dve", ones32, 1.0, ["ones32"])
    MEMSET("dve", ones16, 1.0, ["ones16"])
    MEMSET("dve", onesbd, 0.0, ["onesbd"])
    MEMSET("dve", onesbd[0:64, 0:64], 1.0, ["onesbd"])
    MEMSET("dve", onesbd[64:128, 64:128], 1.0, ["onesbd"])
    for e_ in range(2):
        MEMSET("dve", onesz[e_], 0.0, [f"onesz{e_}"])
        MEMSET("dve", onesz[e_][:, 64 * e_:64 * e_ + 64], 1.0, [f"onesz{e_}"])
    DVE_TS(gq8, cv(C_GQ), 0.125, ALU.mult, ["cvec"], ["gq8"])
    DVE_TS(gmq, cv(C_GMQ), 1.0 / math.sqrt(128.0), ALU.mult, ["cvec"], ["gmq"])
    w_issue_upto(2)

    def rms_norm_tiles(n_list, gcol, dst_fn, tagdst, xin_loader):
        for ti, (off, w) in enumerate(n_list):
            buf = ti % 2
            xin_loader(buf, off, w)
            pb = ps_next([0, 1])
            for c in range(KC):
                sq = sq32x[c % 3]
                ACT(sq[:, 0:w], xin[buf][:, c, 0:w], AF.Square, [f"xin{buf}"], [f"sqx{c % 3}"])
                MM(PS[pb][:, 0:w], ones32, sq[:, 0:w], c == 0, c == KC - 1, ["ones32", f"sqx{c % 3}"], [f"ps{pb}"])
            RSTD(rstd[buf][:, 0:w], PS[pb][:, 0:w], 1.0 / D, [f"ps{pb}"], [f"rstd{buf}"])
            for c in range(KC):
                DVE_STT(dst_fn(c, off, w), xin[buf][:, c, 0:w], cv(gcol + c), rstd[buf][:, 0:w], ALU.mult, ALU.mult,
                        [f"xin{buf}", f"rstd{buf}", "cvec"], [tagdst])

    def load_mem(buf, off, w):
        DMA("sp", xin[buf][:, :, 0:w], bass.AP(memT_t, off, [[256, 128], [128 * 256, KC], [1, w]]),
            [], [f"xin{buf}"], f"xin{buf}")

    memn = nT
    rms_norm_tiles([(0, 256)], C_GMEM, lambda c, off, w: memn[:, c, off:off + w], "nT", load_mem)
    sl = w_next()
    for hm in range(4):
        pz = ps_next([2, 3])
        for c in range(KC):
            MM(PS[pz][:, 0:256], wslot[sl][:, c, hm * 128:(hm + 1) * 128], memn[:, c, 0:256], c == 0, c == KC - 1,
               [f"w{sl}", "nT"], [f"ps{pz}"])
        ACT(sq16b[0][:, 0:256], PS[pz][:, 0:256], AF.Square, [f"ps{pz}"], ["sq16b0"])
        MM(PS[4][:, 0:256], ones16, sq16b[0][:, 0:256], True, True, ["ones16", "sq16b0"], ["ps4"])
        RSTD(rstd[0][:, 0:256], PS[4][:, 0:256], 1.0 / 128, ["ps4"], ["rstd0"])
        DVE_STT(mkT[:, hm, :], PS[pz][:, 0:256], cv(C_GMK), rstd[0][:, 0:256], ALU.mult, ALU.mult,
                [f"ps{pz}", "rstd0", "cvec"], ["mkT"])
    sl = w_next()
    for mt in range(2):
        pz = ps_next([2, 3])
        for c in range(KC):
            MM(PS[pz][:, :], memn[:, c, mt * 128:(mt + 1) * 128], wslot[sl][:, c, :], c == 0, c == KC - 1,
               [f"w{sl}", "nT"], [f"ps{pz}"])
        ACT(mvtok[:, mt, :], PS[pz][:, :], AF.Copy, [f"ps{pz}"], ["mvtok"])

    for st in range(NST):
        T0 = st * STT
        R0 = st * 16
        barrier()
        def load_x(buf, off, w, T0=T0):
            DMA("sp", xin[buf][:, :, 0:w], bass.AP(xT_t, T0 + off, [[XW, 128], [128 * XW, KC], [1, w]]),
                [], [f"xin{buf}"], f"xin{buf}")

        rms_norm_tiles([(0, 416), (416, 416), (832, 416), (1248, 416)], C_GMIX,
                       lambda c, off, w: nT[:, c, off:off + w], "nT", load_x)
        barrier()

        WT = [(0, 448), (448, 448), (896, 384), (1280, 384)]
        MT = [(0, 384), (384, 384), (768, 384)]

        def fm_block(sl, ntiles, col_off, consume, banks=(0, 1, 2, 3)):
            for m in range(4):
                for (off, w) in ntiles:
                    pz = ps_next(list(banks))
                    for c in range(KC):
                        MM(PS[pz][:, 0:w], wslot[sl][:, c, m * 128:(m + 1) * 128], nT[:, c, col_off + off: col_off + off + w],
                           c == 0, c == KC - 1, [f"w{sl}", "nT"], [f"ps{pz}"])
                    consume(m, off, w, pz)

        qkc = {"i": 0}

        def qk_consume(dst, gain_ap, lhs_ones, inv_d, tagdst):
            pend = []

            def stage_b(m, off, w, pz, i):
                pst = 4 + i
                MM(PS[pst][:, 0:w], lhs_ones, sq16b[i][:, 0:w], True, True, ["ones16", "onesbd", f"sq16b{i}"], [f"ps{pst}"])
                RSTD(rstd[i][:, 0:w], PS[pst][:, 0:w], inv_d, [f"ps{pst}"], [f"rstd{i}"])
                DVE_STT(dst(m, off, w), PS[pz][:, 0:w], gain_ap, rstd[i][:, 0:w], ALU.mult, ALU.mult,
                        [f"ps{pz}", f"rstd{i}", "cvec", "gq8", "gmq"], [tagdst])

            def f(m, off, w, pz):
                i = qkc["i"] % 2
                qkc["i"] += 1
                ACT(sq16b[i][:, 0:w], PS[pz][:, 0:w], AF.Square, [f"ps{pz}"], [f"sq16b{i}"])
                if pend:
                    stage_b(*pend.pop(0))
                pend.append((m, off, w, pz, i))

            def flush():
                while pend:
                    stage_b(*pend.pop(0))
            f.flush = flush
            return f

        def build_jobs():
            jobs = []
            for i in range(NPAIR):
                lo = max(0, 7 - 2 * i)
                hi = min(9, 25 - 2 * i)
                jobs.append(dict(pair=i, segs=[(a, b, 2 * i - 7 + a, a) for (a, b) in split_rows(lo, hi, 8)],
                                 bias=("kmask", i)))
                if st == 0 and i in (4, 5, 6):
                    base = 9 + 4 * (i - 4)
                    jobs.append(dict(pair=i, segs=[(0, 4, 1, base)], bias=("flag", 0)))
                if st == 1 and i in (6, 7):
                    base = 9 + 4 * (i - 6)
                    jobs.append(dict(pair=i, segs=[(0, 4, 13, base)], bias=("flag", 1)))
            return jobs

        jobs = build_jobs()
        first_touch, last_touch = {}, {}
        for ji, jb in enumerate(jobs):
            for (a, b, mrow0, slot0) in jb["segs"]:
                for r in range(mrow0, mrow0 + (b - a)):
                    g = r // 4
                    first_touch.setdefault(g, ji)
                    last_touch[g] = ji

        def bt_load(buf, h):
            t = BT[buf]
            key = f"BT{buf}"
            subs = [f"BT{buf}_{q}" for q in range(8)]
            MEMSET("pool", t[:, :, :], NEG, [key] + subs)

            def g2src(j0, nj):
                return bass.AP(g2_t, ((h * 15 + j0) * 64) * 128 + 63, [[127, 64], [64 * 128, nj], [1, 64]])
            lst = [(0, 0, 4, 8), (1, 1, 4, 8)]
            if st == 0:
                lst += [(0, 9 + 8, 0, 4), (0, 9 + 4, 2, 2), (1, 9 + 4, 1, 3), (1, 9 + 0, 3, 1)]
            else:
                lst += [(1, 9 + 1, 12, 3), (0, 9 + 4 + 2, 12, 2), (1, 9 + 4 + 3, 12, 1)]
            for q, (half, s0, j0, n) in enumerate(lst):
                DMA("sp", t[64 * half:64 * half + 64, s0:s0 + n, :], g2src(j0, n), [], [subs[q]], key)
            cm = colmask.unsqueeze(1).to_broadcast([128, 21, 64])
            DVE_TT(t[:, :, :], t[:, :, :], cm, ALU.add, [key, "colmask"] + subs, [key], eng="pool")

        def na_hp(hg, hp):
            hls = [hp * 2, hp * 2 + 1]
            k = hg * 4 + hp
            if k + 1 < 8:
                for eh in range(2):
                    bt_load(2 * ((k + 1) % 2) + eh, (k + 1) * 2 + eh)

            def qz_build(kk, hpp):
                for e in range(2):
                    PHe = slice(64 * e, 64 * e + 64)
                    s.op("pool", lambda en, e=e, PHe=PHe: en.tensor_copy(out=qz[kk % 2][e][PHe, :], in_=qT4[PHe, hpp, :]),
                         reads=["qT4"], writes=[f"qz{kk % 2}_{e}"], small=False)
            if hp == 0:
                qz_build(k, hp)
            if hp + 1 < 4:
                qz_build(k + 1, hp + 1)
            chunk = k
            units = [(ji, eh) for ji in range(len(jobs)) for eh in range(2)]

            def s_stage(ui):
                ji, eh = units[ui]
                jb = jobs[ji]
                i = jb["pair"]
                PH = slice(64 * eh, 64 * eh + 64)
                slot = ui % 4
                pslot = ui % 5
                pbase = 2 * (ui % 2)
                bt = BT[2 * (k % 2) + eh]
                btk = f"BT{2 * (k % 2) + eh}"
                col = 0
                for kk, (a, b, mrow0, bslot0) in enumerate(jb["segs"]):
                    n = (b - a) * 64
                    pb = pbase + kk
                    MM(PS[pb][:, 0:n], kT4[:, hp, i * 128:(i + 1) * 128], qz[k % 2][eh][:, mrow0 * 64: mrow0 * 64 + n],
                       True, True, ["kT4", f"qz{k % 2}_{eh}"], [f"ps{pb}"])
                    DVE_TT(Sb[slot][:, col:col + n], PS[pb][:, 0:n],
                           bt[:, bslot0:bslot0 + (b - a), :].rearrange("p a b -> p (a b)"), ALU.add,
                           [f"ps{pb}", btk], [f"Sb{slot}"])
                    col += n
                if jb["bias"][0] == "kmask":
                    bias_ap = cv(C_KMASK + st * NPAIR + i)
                else:
                    bias_ap = cv(C_FLAG + jb["bias"][1])
                ACT(PT[pslot][:, 0:col], Sb[slot][:, 0:col], AF.Exp, [f"Sb{slot}", "cvec"], [f"PT{pslot}"], bias=bias_ap)

            fresh = set()

            def p_stage(ui):
                ji, eh = units[ui]
                jb = jobs[ji]
                i = jb["pair"]
                hl = hls[eh]
                PH = slice(64 * eh, 64 * eh + 64)
                slot = ui % 5
                if eh == 0:
                    for g, fj in first_touch.items():
                        if fj == ji:
                            fresh.add(g)
                col = 0
                for (a, b, mrow0, bslot0) in jb["segs"]:
                    r = mrow0
                    while r < mrow0 + (b - a):
                        g = r // 4
                        r_end = min(mrow0 + (b - a), 4 * g + 4)
                        n = (r_end - r) * 64
                        pcol = col + (r - mrow0) * 64
                        acc = PS[4 + g % 4]
                        oc = (r - 4 * g) * 64
                        vc = hp * 192 + 64 * eh
                        st_flag = g in fresh
                        fresh.discard(g)
                        MM(acc[:, oc:oc + n], Vtok[:, i, vc:vc + 128], PT[slot][:, pcol:pcol + n],
                           st_flag, False, ["Vtok", f"PT{slot}"], [f"ps{4 + g % 4}"])
                        MM(acc[:, 256 + oc:256 + oc + n], onesz[eh], PT[slot][:, pcol:pcol + n],
                           False, False, [f"onesz{eh}", f"PT{slot}"], [f"ps{4 + g % 4}"])
                        r = r_end
                    col += (b - a) * 64
                if eh == 1:
                    for g, lj in last_touch.items():
                        if lj == ji:
                            nr = min(4, 18 - 4 * g)
                            n = nr * 64
                            acc = PS[4 + g % 4]
                            rb = g % 2
                            DVE_RECIP(recD[rb][:, 0:n], acc[:, 256:256 + n], [f"ps{4 + g % 4}"], [f"recD{rb}"])
                            DVE_TT(yTb[:, chunk, 256 * g:256 * g + n], acc[:, 0:n], recD[rb][:, 0:n], ALU.mult,
                                   [f"ps{4 + g % 4}", f"recD{rb}"], ["yTb"])

            LOOK = 4
            for ui in range(len(units) + LOOK):
                if ui < len(units):
                    s_stage(ui)
                if ui - LOOK >= 0:
                    p_stage(ui - LOOK)

        for eh in range(2):
            bt_load(eh, eh)
        for db_ in range(2):
            for e_ in range(2):
                MEMSET("pool", qz[db_][e_][64 * (1 - e_):64 * (1 - e_) + 64, :], 0.0, [f"qz{db_}_{e_}"])
        for hp_ in range(4):
            MEMSET("pool", Vtok[:, :, hp_ * 192 + 64:hp_ * 192 + 128], 0.0, ["Vtok"])
        for hg in range(2):
            sl = w_next()
            qcons = qk_consume(lambda m, off, w: qT4[:, m, off:off + w], gq8, onesbd, 1.0 / 64, "qT4")
            fm_block(sl, MT, 256, qcons)
            qcons.flush()
            sl = w_next()
            kcons = qk_consume(lambda m, off, w: kT4[:, m, off:off + w], cv(C_GK), onesbd, 1.0 / 64, "kT4")
            fm_block(sl, WT, 0, kcons)
            kcons.flush()
            sl = w_next()
            for i in range(NPAIR):
                pz = ps_next([0, 1, 2, 3])
                for c in range(KC):
                    MM(PS[pz][:, :], nT[:, c, i * 128:(i + 1) * 128], wslot[sl][:, c, :], c == 0, c == KC - 1,
                       [f"w{sl}", "nT"], [f"ps{pz}"])
                vsrc = PS[pz][:, :].rearrange("p (h e d) -> p h e d", h=4, e=2)
                vdst = Vtok[:, i, :].rearrange("p (h x) -> p h x", h=4)
                ACT(vdst[:, :, 0:64], vsrc[:, :, 0, :], AF.Copy, [f"ps{pz}"], ["Vtok"])
                ACT(vdst[:, :, 128:192], vsrc[:, :, 1, :], AF.Copy, [f"ps{pz}"], ["Vtok"])
            for hp in range(4):
                na_hp(hg, hp)
        barrier()
        sl = w_next()
        fm_block(sl, MT, 256, lambda m, off, w, pz: ACT(ch32[:, m, off:off + w], PS[pz][:, 0:w], AF.Copy, [f"ps{pz}"], ["ch32"]))
        sl = w_next()
        fm_block(sl, MT, 256, lambda m, off, w, pz: DVE_TT(ch32[:, m, off:off + w], PS[pz][:, 0:w], ch32[:, m, off:off + w],
                                                            ALU.mult, [f"ps{pz}", "ch32"], ["ch32"]))
        for m in range(4):
            wc = C_CONV + 3 * m
            ACT(cc32[:, m, 1:MID - 1], ch32[:, m, 1:MID - 1], AF.Copy, ["ch32", "cvec"], ["cc32"], scale=cv(wc + 1))
            DVE_STT(cc32[:, m, 1:MID - 1], ch32[:, m, 0:MID - 2], cv(wc + 0), cc32[:, m, 1:MID - 1], ALU.mult, ALU.add,
                    ["ch32", "cc32", "cvec"], ["cc32"])
            DVE_STT(cc32[:, m, 1:MID - 1], ch32[:, m, 2:MID], cv(wc + 2), cc32[:, m, 1:MID - 1], ALU.mult, ALU.add,
                    ["ch32", "cc32", "cvec"], ["cc32"])
        sl = w_next()

        def b_consume(m, off, w, pz):
            lo = max(off, 1)
            hi = min(off + w, MID - 1)
            DVE_TT(yTb[:, 8 + m, lo:hi], PS[pz][:, lo - off:hi - off], cc32[:, m, lo:hi], ALU.mult,
                   [f"ps{pz}", "cc32"], ["yTb"])
        fm_block(sl, MT, 256, b_consume)
        sl = w_next()
        mq = []

        def qm_s1(u):
            m, off, w, pz = mq[u]
            i = u % 2
            ACT(sq16[i][:, 0:w], PS[pz][:, 0:w], AF.Square, [f"ps{pz}"], [f"sq16{i}"])
            MM(PS[3][:, 0:w], ones16, sq16[i][:, 0:w], True, True, ["ones16", f"sq16{i}"], ["ps3"])
            RSTD(rstd[i][:, 0:w], PS[3][:, 0:w], 1.0 / 128, ["ps3"], [f"rstd{i}"])
            DVE_STT(qmT[i][:, 0:w], PS[pz][:, 0:w], gmq, rstd[i][:, 0:w], ALU.mult, ALU.mult,
                    [f"ps{pz}", f"rstd{i}", "gmq"], [f"qmT{i}"])

        def qm_s2(u):
            m, off, w, pz = mq[u]
            i = u % 2
            for mc in range(2):
                pS = 4 + mc
                MM(PS[pS][:, 0:w], mkT[:, m, mc * 128:(mc + 1) * 128], qmT[i][:, 0:w], True, True,
                   ["mkT", f"qmT{i}"], [f"ps{pS}"])
                ACT(PTm[i][:, mc, 0:w], PS[pS][:, 0:w], AF.Exp, [f"ps{pS}"], [f"PTm{i}_{mc}"])

        def qm_s3(u):
            m, off, w, pz = mq[u]
            i = u % 2
            pO, pD = 6, 7
            for mc in range(2):
                MM(PS[pO][:, 0:w], mvtok[:, mc, m * 128:(m + 1) * 128], PTm[i][:, mc, 0:w], mc == 0, mc == 1,
                   ["mvtok", f"PTm{i}_{mc}"], [f"ps{pO}"])
            for mc in range(2):
                MM(PS[pD][:, 0:w], ones16, PTm[i][:, mc, 0:w], mc == 0, mc == 1, ["ones16", f"PTm{i}_{mc}"], [f"ps{pD}"])
            DVE_RECIP(recM[i][:, 0:w], PS[pD][:, 0:w], [f"ps{pD}"], [f"recM{i}"])
            DVE_TT(yTb[:, 12 + m, off:off + w], PS[pO][:, 0:w], recM[i][:, 0:w], ALU.mult,
                   [f"ps{pO}", f"recM{i}"], ["yTb"])

        def qm_consume(m, off, w, pz):
            u = len(mq)
            mq.append((m, off, w, pz))
            qm_s1(u)
            if u >= 1:
                qm_s2(u - 1)
            if u >= 2:
                qm_s3(u - 2)
        fm_block(sl, MT, 256, qm_consume, banks=(0, 1, 2))
        nq = len(mq)
        qm_s2(nq - 1)
        qm_s3(nq - 2)
        qm_s3(nq - 1)
        barrier()
        if DEBUG:
            for c in range(KC):
                t32 = xin[0]
                ACT(t32[:, 0, 0:384], yTb[:, c, 0:384], AF.Copy, ["yTb"], ["xin0"])
                ACT(t32[:, 1, 0:384], yTb[:, c, 384:768], AF.Copy, ["yTb"], ["xin0"])
                ACT(t32[:, 2, 0:384], yTb[:, c, 768:1152], AF.Copy, ["yTb"], ["xin0"])
                DMA("sp", dbg_mix_t.ap()[st, c].rearrange("p (a b) -> p a b", a=3), t32[:, 0:3, 0:384], ["xin0"], [], "dbg")
            barrier()

        for c in range(KC):
            DMA("sp", x1T[:, c, :], bass.AP(xT_t, c * 128 * XW + T0 + 319, [[XW, 128], [1, MIX]]), [], [f"x1T{c}"], f"x1ld{c}")
        XT = [(0, 342), (342, 342), (684, 342)]
        for b in range(4):
            sl = w_next()
            for m in range(4):
                mc = b * 4 + m
                for (off, w) in XT:
                    pz = ps_next([0, 1, 2, 3])
                    for c in range(KC):
                        MM(PS[pz][:, 0:w], wslot[sl][:, c, m * 128:(m + 1) * 128], yTb[:, c, 63 + off:63 + off + w],
                           c == 0, c == KC - 1, [f"w{sl}", "yTb"], [f"ps{pz}"])
                    DVE_TT(x1T[:, mc, off:off + w], PS[pz][:, 0:w], x1T[:, mc, off:off + w], ALU.add,
                           [f"ps{pz}", f"x1T{mc}"], [f"x1T{mc}"])
        barrier()
        if DEBUG:
            for c in range(KC):
                DMA("sp", dbg_x1_t.ap()[st, c], x1T[:, c, :], [f"x1T{c}"], [], "dbg")
            barrier()

        for ti, (off, w) in enumerate(XT):
            pb = ps_next([0, 1, 2])
            for c in range(KC):
                sq = sq32[c % 4]
                ACT(sq[:, 0:w], x1T[:, c, off:off + w], AF.Square, [f"x1T{c}"], [f"sq32{c % 4}"])
                MM(PS[pb][:, 0:w], ones32, sq[:, 0:w], c == 0, c == KC - 1, ["ones32", f"sq32{c % 4}"], [f"ps{pb}"])
            RSTD(rstdD[:, off:off + w], PS[pb][:, 0:w], 1.0 / D, [f"ps{pb}"], ["rstdD"])
        if st == 0:
            DVE_TT(rstdD[:, 0:1], rstdD[:, 0:1], cv(C_FLAG + 2), ALU.mult, ["rstdD", "cvec"], ["rstdD"])
        else:
            DVE_TT(rstdD[:, MIX - 1:MIX], rstdD[:, MIX - 1:MIX], cv(C_FLAG + 3), ALU.mult, ["rstdD", "cvec"], ["rstdD"])
        NP = [(0, 344), (344, 686), (686, MIX)]

        def n2k(c, lo, hi):
            return [f"n2T_{c}_{p}" for p, (a_, b_) in enumerate(NP) if a_ < hi and lo < b_]
        for p, (a_, b_) in enumerate(NP):
            for c in range(KC):
                DVE_STT(n2T[:, c, a_:b_], x1T[:, c, a_:b_], cv(C_GFFN + c), rstdD[:, a_:b_], ALU.mult, ALU.mult,
                        [f"x1T{c}", "rstdD", "cvec"], [f"n2T_{c}_{p}"])
        OT = [(0, 342), (342, 684), (684, 1024)]
        tcount = {"i": 0}
        for gi, grp in enumerate(FF_GROUPS):
            hb = gi % 2
            nblk = len(grp)
            for bi, b in enumerate(grp):
                sl = w_next()
                for jj in range(4):
                    j = b * 4 + jj
                    wc = C_FCONV + 3 * j
                    banks = [0, 1, 2] if (tcount["i"] % 2 == 0) else [3, 4, 5]
                    tcount["i"] += 1
                    for ti, (o0, o1) in enumerate(OT):
                        w = o1 - o0 + 2
                        pz = banks[ti]
                        for c in range(KC):
                            MM(PS[pz][:, 0:w], wslot[sl][:, c, jj * 128:(jj + 1) * 128], n2T[:, c, o0:o0 + w],
                               c == 0, c == KC - 1, [f"w{sl}"] + n2k(c, o0, o0 + w), [f"ps{pz}"])
                        n = o1 - o0
                        t = ctmp[ti % 2]
                        tk = f"ctmp{ti % 2}"
                        ACT(t[:, 0:n], PS[pz][:, 1:1 + n], AF.Copy, [f"ps{pz}", "cvec"], [tk], scale=cv(wc + 1))
                        DVE_STT(t[:, 0:n], PS[pz][:, 0:n], cv(wc + 0), t[:, 0:n], ALU.mult, ALU.add, [f"ps{pz}", tk, "cvec"], [tk])
                        DVE_STT(t[:, 0:n], PS[pz][:, 2:2 + n], cv(wc + 2), t[:, 0:n], ALU.mult, ALU.add, [f"ps{pz}", tk, "cvec"], [tk])
                        ACT(sa32[:, jj, o0:o1], t[:, 0:n], AF.Silu, [tk], ["sa32"])
                sl = w_next()
                for jj in range(4):
                    j = D_FF // 128 + b * 4 + jj
                    wc = C_FCONV + 3 * j
                    banks = [0, 1, 2] if (tcount["i"] % 2 == 0) else [3, 4, 5]
                    tcount["i"] += 1
                    for ti, (o0, o1) in enumerate(OT):
                        w = o1 - o0 + 2
                        pz = banks[ti]
                        for c in range(KC):
                            MM(PS[pz][:, 0:w], wslot[sl][:, c, jj * 128:(jj + 1) * 128], n2T[:, c, o0:o0 + w],
                               c == 0, c == KC - 1, [f"w{sl}"] + n2k(c, o0, o0 + w), [f"ps{pz}"])
                        n = o1 - o0
                        t = ctmp[2 + ti % 2]
                        tk = f"ctmp{2 + ti % 2}"
                        ACT(t[:, 0:n], PS[pz][:, 1:1 + n], AF.Copy, [f"ps{pz}", "cvec"], [tk], scale=cv(wc + 1))
                        DVE_STT(t[:, 0:n], PS[pz][:, 0:n], cv(wc + 0), t[:, 0:n], ALU.mult, ALU.add, [f"ps{pz}", tk, "cvec"], [tk])
                        DVE_STT(t[:, 0:n], PS[pz][:, 2:2 + n], cv(wc + 2), t[:, 0:n], ALU.mult, ALU.add, [f"ps{pz}", tk, "cvec"], [tk])
                        DVE_TT(hT[hb][:, bi * 4 + jj, o0:o1], t[:, 0:n], sa32[:, jj, o0:o1], ALU.mult,
                               [tk, "sa32"], [f"hT{hb}_{bi * 4 + jj}"], eng="pool")
            nkf = 4 * nblk
            last_grp = (gi == len(FF_GROUPS) - 1)
            for mb in range(4):
                sl = w_next()
                for m in range(4):
                    mc = mb * 4 + m
                    for (o0, o1) in ((0, 512), (512, 1024)):
                        n = o1 - o0
                        pz = ps_next([6, 7])
                        for kf in range(nkf):
                            MM(PS[pz][:, 0:n], wslot_fo[sl][:, kf, m * 128:(m + 1) * 128], hT[hb][:, kf, o0:o1],
                               kf == 0, kf == nkf - 1, [f"w{sl}", f"hT{hb}_{kf}"], [f"ps{pz}"])
                        DVE_TT(x1T[:, mc, 1 + o0:1 + o1], PS[pz][:, 0:n], x1T[:, mc, 1 + o0:1 + o1], ALU.add,
                               [f"ps{pz}", f"x1T{mc}"], [f"x1T{mc}"])
                    if last_grp:
                        DMA("sp", bass.AP(yT_t, mc * 128 * TOK + T0, [[TOK, 128], [1, STT]]), x1T[:, mc, 1:1 + STT],
                            [f"x1T{mc}"], [], f"yout{mc % 4}")

    assert wuse["i"] == len(wq), (wuse["i"], len(wq))
    s.emit()
    return nc, s.stats


_SEQS = [("p", 0, 0), ("p", 0, 32), ("p", 0, 64), ("p", 0, 96), ("s", 0, 0), ("s", 0, 32), ("s", 1, 0), ("s", 1, 32)]


def _prep_core(core, x_prompt, x_sample, mem_prompt, mem_sample, cv_common, nrows):
    kind, b, r0 = _SEQS[core]
    x = x_prompt[b] if kind == "p" else x_sample[b]
    mem = mem_prompt[b] if kind == "p" else mem_sample[b]
    rows = nrows[kind]
    xe = np.zeros((XW, D), np.float32)
    g0 = (r0 - 5) * 64
    lo = max(g0, 0)
    hi = min(g0 + XW, rows * 64)
    xe[lo - g0:hi - g0] = x[lo:hi]
    xT = np.ascontiguousarray(xe.T).reshape(KC, 128, XW)
    memT = np.ascontiguousarray(mem.T).reshape(KC, 128, 256)
    cvv = cv_common.copy()
    for st in range(NST):
        for i in range(NPAIR):
            for half in range(2):
                gr = r0 + 16 * st - 5 + 2 * i + half
                ok = 0 <= gr < rows
                cvv[64 * half:64 * half + 64, C_KMASK + st * NPAIR + i] = 0.0 if ok else NEG
    top = (r0 == 0)
    bot = (r0 + 32 == rows)
    cvv[:, C_FLAG + 0] = 0.0 if top else NEG
    cvv[:, C_FLAG + 1] = 0.0 if bot else NEG
    cvv[:, C_FLAG + 2] = 0.0 if top else 1.0
    cvv[:, C_FLAG + 3] = 0.0 if bot else 1.0
    return xT, memT, cvv


_CACHE = {}


def kernel(x_prompt, x_sample, mem_prompt, mem_sample, g_mix, w_in, na_q_gain, na_k_gain,
           na_rel_bias, conv_w, mem_norm_g, w_mem_kv, mem_q_gain, mem_k_gain, w_out,
           g_ffn, w_ffn_in, ffn_conv_w, w_ffn_out):
    f = lambda a: np.ascontiguousarray(np.asarray(a, dtype=np.float32))
    x_prompt, x_sample, mem_prompt, mem_sample = f(x_prompt), f(x_sample), f(mem_prompt), f(mem_sample)
    cvc = np.zeros((128, NCV), np.float32)
    cvc[:, C_GMIX:C_GMIX + 16] = f(g_mix)[0].reshape(16, 128).T
    cvc[:, C_GFFN:C_GFFN + 16] = f(g_ffn)[0].reshape(16, 128).T
    cvc[:, C_GMEM:C_GMEM + 16] = f(mem_norm_g)[0].reshape(16, 128).T
    cvc[:, C_GQ] = np.tile(f(na_q_gain)[0], 2)
    cvc[:, C_GK] = np.tile(f(na_k_gain)[0], 2)
    cvc[:, C_GMQ] = f(mem_q_gain)[0]
    cvc[:, C_GMK] = f(mem_k_gain)[0]
    cw = f(conv_w)[0]
    cvc[:, C_CONV:C_CONV + 12] = cw.reshape(3, 4, 128).transpose(2, 1, 0).reshape(128, 12)
    fw = f(ffn_conv_w)[0]
    cvc[:, C_FCONV:C_FCONV + 264] = fw.reshape(3, 88, 128).transpose(2, 1, 0).reshape(128, 264)
    qc = np.arange(64)
    q_cs = np.clip(qc - 8, 0, 48)
    kc = np.arange(64)[:, None]
    valid = (kc >= q_cs[None, :]) & (kc < q_cs[None, :] + 16)
    cm = np.where(valid, 0.0, NEG).astype(np.float32)
    colmask = np.concatenate([cm, cm], axis=0)
    rb = f(na_rel_bias)[0]
    G = np.zeros((16, 15, 128), np.float32)
    G[:, :, 48:79] = rb[:, ::-1, ::-1]
    g2 = np.ascontiguousarray(np.broadcast_to(G[:, :, None, :], (16, 15, 64, 128))).reshape(16 * 15 * 64, 128)

    nrows = {"p": x_prompt.shape[1] // 64, "s": x_sample.shape[1] // 64}
    if "nc" not in _CACHE:
        _CACHE["nc"] = build_program()
    nc, stats = _CACHE["nc"]
    shared = {"w_in": f(w_in)[0], "w_out": f(w_out)[0], "w_ffn_in": f(w_ffn_in)[0], "w_ffn_out": f(w_ffn_out)[0],
              "w_mem_kv": f(w_mem_kv)[0], "colmask": colmask, "g2": g2}
    in_maps = []
    for core in range(NCORE):
        xT, memT, cvv = _prep_core(core, x_prompt, x_sample, mem_prompt, mem_sample, cvc, nrows)
        d = dict(shared)
        d.update({"xT": xT, "memT": memT, "cvec": cvv})
        in_maps.append(d)
    res = run_bass_kernel_spmd(nc, in_maps, core_ids=list(range(NCORE)))
    _CACHE["res"] = res
    y_prompt = np.zeros_like(x_prompt)
    y_sample = np.zeros_like(x_sample)
    for core in range(NCORE):
        kind, b, r0 = _SEQS[core]
        yT = np.asarray(res.results[core]["yT"]).reshape(D, TOK)
        dst = y_prompt if kind == "p" else y_sample
        dst[b, r0 * 64:r0 * 64 + TOK, :] = yT.T
    return (y_prompt, y_sample)
```
